# Optimizing a Trainium2 kernel written in Bass

```python
import math
import jax, jax.numpy as jnp
from jax import lax
import numpy as np

D_MODEL = 1024
BATCH = 4
SEQ = 4096
DEPTH = 4

CHUNK = 64
N_META = 16
META_PAD = CHUNK - N_META

A_HEADS = 6
A_DK = 64
A_DV = 64
A_WIDTH = A_HEADS * A_DV

B_HEADS = 6
B_HEADDIM = 64
B_WIDTH = B_HEADS * B_HEADDIM
B_GROUPS = 2
B_DSTATE = 128
B_CONV = 4
B_CONV_DIM = B_WIDTH + 2 * B_GROUPS * B_DSTATE

C_GROUPS = 16
C_GROUP_CH = 16
C_WIDTH = C_GROUPS * C_GROUP_CH
C_STATE = 64

D_MIX = A_WIDTH + B_WIDTH + C_WIDTH
D_FF = 4 * D_MODEL
SPLITS = (A_WIDTH, A_WIDTH, A_WIDTH, A_WIDTH, B_WIDTH, B_CONV_DIM, B_HEADS, C_WIDTH)
D_IN = sum(SPLITS)
ALPHA = (2 * DEPTH) ** 0.25
BETA = (8 * DEPTH) ** -0.25
LN_EPS = 1e-5
RMS_EPS = 1e-6
S5_MAX_RE = -1e-4

kernel_name = 'hybrid_hgrn2_ssd_s5_deepnorm'


def layer_norm(x, g, b):
    xf = x.astype(jnp.float32)
    mu = jnp.mean(xf, axis=-1, keepdims=True)
    var = jnp.mean(jnp.square(xf - mu), axis=-1, keepdims=True)
    y = (xf - mu) * lax.rsqrt(var + LN_EPS) * g.astype(jnp.float32) + b.astype(jnp.float32)
    return y.astype(x.dtype)


def rms_norm(x):
    xf = x.astype(jnp.float32)
    return xf * lax.rsqrt(jnp.mean(jnp.square(xf), axis=-1, keepdims=True) + RMS_EPS)


def pad_front(t):
    pad = [(0, 0)] * t.ndim
    pad[1] = (META_PAD, 0)
    return jnp.pad(t, pad)


def hgrn2_mixer(q_raw, f_raw, i_raw, g_raw, lb, norm_w):
    f32 = jnp.float32
    bsz, seq_len, _ = q_raw.shape
    zf = f_raw.astype(f32)
    q = jax.nn.silu(q_raw.astype(f32))
    log_f = jnp.logaddexp(jax.nn.log_sigmoid(zf), jnp.log(lb) + jax.nn.log_sigmoid(-zf))
    k = (1.0 - lb) * jax.nn.sigmoid(-zf)
    v = i_raw.astype(f32)

    def to_chunks(t, d):
        t = pad_front(t).reshape(bsz, -1, CHUNK, A_HEADS, d)
        return t.transpose(1, 0, 3, 2, 4)

    causal = jnp.tril(jnp.ones((CHUNK, CHUNK), dtype=bool))

    def chunk_step(state, inp):
        qc, kc, vc, gc = inp
        G = jnp.cumsum(gc, axis=2)
        o_inter = jnp.einsum('bhtk,bhkv->bhtv', qc * jnp.exp(G), state)
        diff = G[:, :, :, None, :] - G[:, :, None, :, :]
        decay = jnp.exp(jnp.where(causal[:, :, None], diff, -jnp.inf))
        scores = jnp.einsum('bhtsk,bhsk->bhts', qc[:, :, :, None, :] * decay, kc)
        o_intra = jnp.einsum('bhts,bhsv->bhtv', scores, vc)
        G_last = G[:, :, -1:, :]
        new_state = (jnp.exp(G_last[:, :, 0, :])[..., None] * state
                     + jnp.einsum('bhsk,bhsv->bhkv', kc * jnp.exp(G_last - G), vc))
        return new_state, o_inter + o_intra

    state0 = jnp.zeros((bsz, A_HEADS, A_DK, A_DV), f32)
    _, o = lax.scan(chunk_step, state0,
                    (to_chunks(q, A_DK), to_chunks(k, A_DK), to_chunks(v, A_DV), to_chunks(log_f, A_DK)))
    o = o.transpose(1, 0, 3, 2, 4).reshape(bsz, -1, A_HEADS, A_DV)[:, META_PAD:]
    gate = jax.nn.silu(g_raw.astype(f32)).reshape(bsz, seq_len, A_HEADS, A_DV)
    o = rms_norm(o) * norm_w.astype(f32) * gate
    return o.reshape(bsz, seq_len, A_WIDTH)


def causal_depthwise_conv(x, w, b):
    out = lax.conv_general_dilated(x, w[:, None, :], window_strides=(1,),
                                   padding=[(B_CONV - 1, 0)],
                                   dimension_numbers=('NWC', 'WIO', 'NWC'),
                                   feature_group_count=x.shape[-1])
    return out + b


def segsum(a):
    T = a.shape[-1]
    cs = jnp.cumsum(a, axis=-1)
    diff = cs[..., :, None] - cs[..., None, :]
    mask = jnp.tril(jnp.ones((T, T), dtype=bool))
    return jnp.where(mask, diff, -jnp.inf)


def ssd_chunked(xdt, dA, b_h, c_h):
    a_cum = jnp.cumsum(dA, axis=-1)
    l_mat = jnp.exp(segsum(dA))
    scores = jnp.einsum('bclhn,bcshn->bhcls', c_h, b_h) * l_mat
    y_diag = jnp.einsum('bhcls,bcshp->bclhp', scores, xdt)
    decay_states = jnp.exp(a_cum[..., -1:] - a_cum)
    states = jnp.einsum('bclhn,bhcl,bclhp->bchpn', b_h, decay_states, xdt)
    states = jnp.concatenate([jnp.zeros_like(states[:, :1]), states], axis=1)
    chunk_decay = jnp.exp(segsum(jnp.pad(a_cum[..., -1], ((0, 0), (0, 0), (1, 0)))))
    new_states = jnp.einsum('bhzc,bchpn->bzhpn', chunk_decay, states)
    prev_states = new_states[:, :-1]
    y_off = jnp.einsum('bclhn,bchpn,bhcl->bclhp', c_h, prev_states, jnp.exp(a_cum))
    return y_diag + y_off


def mamba2_mixer(z, xbc, dt_raw, conv_w, conv_b, dt_bias, a_log, d_skip, norm_w):
    f32 = jnp.float32
    bsz, seq_len, _ = z.shape
    xbc = jax.nn.silu(causal_depthwise_conv(xbc.astype(f32), conv_w.astype(f32), conv_b.astype(f32)))
    xs, b_in, c_in = jnp.split(xbc, [B_WIDTH, B_WIDTH + B_GROUPS * B_DSTATE], axis=-1)
    dt = jax.nn.softplus(dt_raw.astype(f32) + dt_bias.astype(f32))
    a = -jnp.exp(a_log.astype(f32))
    nc = (seq_len + META_PAD) // CHUNK
    rep = B_HEADS // B_GROUPS
    xs = pad_front(xs).reshape(bsz, nc, CHUNK, B_HEADS, B_HEADDIM)
    b_h = jnp.repeat(pad_front(b_in).reshape(bsz, nc, CHUNK, B_GROUPS, B_DSTATE), rep, axis=3)
    c_h = jnp.repeat(pad_front(c_in).reshape(bsz, nc, CHUNK, B_GROUPS, B_DSTATE), rep, axis=3)
    dt = pad_front(dt).reshape(bsz, nc, CHUNK, B_HEADS)
    dA = (dt * a).transpose(0, 3, 1, 2)
    y = ssd_chunked(xs * dt[..., None], dA, b_h, c_h) + d_skip.astype(f32)[:, None] * xs
    y = y.reshape(bsz, -1, B_WIDTH)[:, META_PAD:]
    y = y * jax.nn.silu(z.astype(f32))
    y = rms_norm(y.reshape(bsz, seq_len, B_GROUPS, B_WIDTH // B_GROUPS)).reshape(bsz, seq_len, B_WIDTH)
    return y * norm_w.astype(f32)


def s5_mixer(u, a_re, a_im, log_dt, b_re, b_im, c_re, c_im, d_skip, glu_w, glu_b):
    f32 = jnp.float32
    bsz, seq_len, _ = u.shape
    uf = u.astype(f32)
    lam_re = jnp.minimum(a_re.astype(f32), S5_MAX_RE)
    lam_im = a_im.astype(f32)
    dt = jnp.exp(log_dt.astype(f32))[:, None]
    mag = jnp.exp(lam_re * dt)
    lb_re = mag * jnp.cos(lam_im * dt)
    lb_im = mag * jnp.sin(lam_im * dt)
    den = jnp.square(lam_re) + jnp.square(lam_im)
    nr = lb_re - 1.0
    s_re = (nr * lam_re + lb_im * lam_im) / den
    s_im = (lb_im * lam_re - nr * lam_im) / den
    br = b_re.astype(f32)
    bi = b_im.astype(f32)
    bb_re = s_re[..., None] * br - s_im[..., None] * bi
    bb_im = s_re[..., None] * bi + s_im[..., None] * br
    ug = uf.reshape(bsz, seq_len, C_GROUPS, C_GROUP_CH)
    bu_re = jnp.einsum('blgc,gnc->blgn', ug, bb_re)
    bu_im = jnp.einsum('blgc,gnc->blgn', ug, bb_im)
    a_el_re = jnp.broadcast_to(lb_re, bu_re.shape)
    a_el_im = jnp.broadcast_to(lb_im, bu_im.shape)

    def combine(e1, e2):
        a1r, a1i, b1r, b1i = e1
        a2r, a2i, b2r, b2i = e2
        return (a2r * a1r - a2i * a1i, a2r * a1i + a2i * a1r,
                a2r * b1r - a2i * b1i + b2r, a2r * b1i + a2i * b1r + b2i)

    _, _, x_re, x_im = lax.associative_scan(combine, (a_el_re, a_el_im, bu_re, bu_im), axis=1)
    y = (jnp.einsum('blgn,gcn->blgc', x_re, c_re.astype(f32))
         - jnp.einsum('blgn,gcn->blgc', x_im, c_im.astype(f32)))
    y = y.reshape(bsz, seq_len, C_WIDTH) + d_skip.astype(f32) * uf
    y = jax.nn.gelu(y)
    return y * jax.nn.sigmoid(y @ glu_w.astype(f32) + glu_b.astype(f32))


def setup_inputs(seed: int = 0) -> dict:
    key = jax.random.key(seed)
    ks = jax.random.split(key, 32)
    f32 = jnp.float32

    def nrm(k, shape, scale):
        return scale * jax.random.normal(k, shape, f32)

    dt0 = jnp.exp(jax.random.uniform(ks[6], (DEPTH, B_HEADS), f32, math.log(1e-3), math.log(1e-1)))
    s5_a_im = (math.pi * jnp.arange(C_STATE, dtype=f32))[None, None, :] + nrm(ks[11], (DEPTH, C_GROUPS, C_STATE), 0.01)
    return {
        'x': nrm(ks[0], (BATCH, SEQ, D_MODEL), 1.0),
        'meta_tokens': nrm(ks[1], (N_META, D_MODEL), 1.0),
        'w_in': nrm(ks[2], (DEPTH, D_MODEL, D_IN), D_MODEL ** -0.5),
        'hgrn_lb_logits': nrm(ks[3], (DEPTH, A_WIDTH), 0.1),
        'hgrn_norm_w': 1.0 + nrm(ks[4], (DEPTH, A_DV), 0.02),
        'm2_conv_w': nrm(ks[5], (DEPTH, B_CONV, B_CONV_DIM), B_CONV ** -0.5),
        'm2_conv_b': nrm(ks[7], (DEPTH, B_CONV_DIM), 0.02),
        'm2_dt_bias': dt0 + jnp.log(-jnp.expm1(-dt0)),
        'm2_a_log': jnp.log(jax.random.uniform(ks[8], (DEPTH, B_HEADS), f32, 1.0, 16.0)),
        'm2_d': 1.0 + nrm(ks[9], (DEPTH, B_HEADS), 0.1),
        'm2_norm_w': 1.0 + nrm(ks[10], (DEPTH, B_WIDTH), 0.02),
        's5_a_re': -0.5 + nrm(ks[12], (DEPTH, C_GROUPS, C_STATE), 0.01),
        's5_a_im': s5_a_im,
        's5_log_dt': jax.random.uniform(ks[13], (DEPTH, C_GROUPS), f32, math.log(1e-3), math.log(1e-1)),
        's5_b_re': nrm(ks[14], (DEPTH, C_GROUPS, C_STATE, C_GROUP_CH), (2 * C_GROUP_CH) ** -0.5),
        's5_b_im': nrm(ks[15], (DEPTH, C_GROUPS, C_STATE, C_GROUP_CH), (2 * C_GROUP_CH) ** -0.5),
        's5_c_re': nrm(ks[16], (DEPTH, C_GROUPS, C_GROUP_CH, C_STATE), C_STATE ** -0.5),
        's5_c_im': nrm(ks[17], (DEPTH, C_GROUPS, C_GROUP_CH, C_STATE), C_STATE ** -0.5),
        's5_d': nrm(ks[18], (DEPTH, C_WIDTH), 1.0),
        's5_glu_w': nrm(ks[19], (DEPTH, C_WIDTH, C_WIDTH), C_WIDTH ** -0.5),
        's5_glu_b': nrm(ks[20], (DEPTH, C_WIDTH), 0.02),
        'w_out': nrm(ks[21], (DEPTH, D_MIX, D_MODEL), BETA * D_MIX ** -0.5),
        'ln1_g': 1.0 + nrm(ks[22], (DEPTH, D_MODEL), 0.02),
        'ln1_b': nrm(ks[23], (DEPTH, D_MODEL), 0.02),
        'w_mlp_in': nrm(ks[24], (DEPTH, D_MODEL, D_FF), D_MODEL ** -0.5),
        'w_mlp_out': nrm(ks[25], (DEPTH, D_FF, D_MODEL), BETA * D_FF ** -0.5),
        'ln2_g': 1.0 + nrm(ks[26], (DEPTH, D_MODEL), 0.02),
        'ln2_b': nrm(ks[27], (DEPTH, D_MODEL), 0.02),
    }


def reference(x, meta_tokens, w_in, hgrn_lb_logits, hgrn_norm_w, m2_conv_w, m2_conv_b,
              m2_dt_bias, m2_a_log, m2_d, m2_norm_w, s5_a_re, s5_a_im, s5_log_dt,
              s5_b_re, s5_b_im, s5_c_re, s5_c_im, s5_d, s5_glu_w, s5_glu_b, w_out,
              ln1_g, ln1_b, w_mlp_in, w_mlp_out, ln2_g, ln2_b):
    bsz = x.shape[0]
    meta = jnp.broadcast_to(meta_tokens.astype(x.dtype)[None], (bsz, N_META, D_MODEL))
    h = jnp.concatenate([meta, x], axis=1)
    lb_cum = jnp.cumsum(jax.nn.softmax(hgrn_lb_logits.astype(jnp.float32), axis=0), axis=0)
    lower_bounds = lb_cum - lb_cum[0]
    split_idx = np.cumsum(SPLITS)[:-1].tolist()
    for l in range(DEPTH):
        proj = h @ w_in[l]
        q_a, f_a, i_a, g_a, z_b, xbc_b, dt_b, u_c = jnp.split(proj, split_idx, axis=-1)
        y_a = hgrn2_mixer(q_a, f_a, i_a, g_a, lower_bounds[l], hgrn_norm_w[l])
        y_b = mamba2_mixer(z_b, xbc_b, dt_b, m2_conv_w[l], m2_conv_b[l], m2_dt_bias[l],
                           m2_a_log[l], m2_d[l], m2_norm_w[l])
        y_c = s5_mixer(u_c, s5_a_re[l], s5_a_im[l], s5_log_dt[l], s5_b_re[l], s5_b_im[l],
                       s5_c_re[l], s5_c_im[l], s5_d[l], s5_glu_w[l], s5_glu_b[l])
        mixed = jnp.concatenate([y_a, y_b, y_c], axis=-1).astype(h.dtype) @ w_out[l]
        h = layer_norm(ALPHA * h + mixed, ln1_g[l], ln1_b[l])
        ff = jnp.square(jax.nn.relu(h @ w_mlp_in[l])) @ w_mlp_out[l]
        h = layer_norm(ALPHA * h + ff, ln2_g[l], ln2_b[l])
    return h[:, N_META:]
```

```python
import math
from contextlib import ExitStack
import numpy as np
import concourse.bass as bass
import concourse.mybir as mybir
from concourse.bass_utils import run_bass_kernel_spmd

F32 = mybir.dt.float32
BF16 = mybir.dt.bfloat16
AF = mybir.ActivationFunctionType
ALU = mybir.AluOpType
AX = mybir.AxisListType

TB = 384
NT = 3
NPIECE = 26
ALPHA = 8.0 ** 0.25
LN_EPS = 1e-5
RMS_EPS = 1e-6
PI = math.pi
NPV = 122
NA = 16
NB = 40
NWS = 3
ND = 24


class Prog:
    def __init__(self):
        self.ops = []
        self.lastw = {}
        self.rd = {}

    def add(self, eng, fn, r=(), w=(), dma=False):
        i = len(self.ops)
        raw, other = set(), set()
        for b in r:
            j = self.lastw.get(b)
            if j is not None:
                raw.add(j)
        for b in w:
            j = self.lastw.get(b)
            if j is not None:
                other.add(j)
            for j in self.rd.get(b, {}).values():
                if isinstance(j, list):
                    other.update(j)
                else:
                    other.add(j)
        deps = set()
        for j in raw | other:
            oj = self.ops[j]
            if oj['dma']:
                deps.add(j)
            elif oj['eng'] == eng and not dma:
                if eng == 'pe':
                    continue
                deps.add(j)
            else:
                deps.add(j)
        for b in r:
            d = self.rd.setdefault(b, {})
            if dma:
                d.setdefault('dma', []).append(i)
            else:
                d[eng] = i
        for b in w:
            self.lastw[b] = i
            self.rd[b] = {}
        self.ops.append(dict(eng=eng, fn=fn, deps=deps, dma=dma, sig=False))
        return i


class _Stop(Exception):
    pass


STAGE = 99
FFN_DEFER = 2
HG_LAG = 2
GELU_ACT = True
VAR = 0
DEBUG = False
TCUT = 0


def build(NBLK, DEPTH):
    LT = NBLK * TB
    nc = bass.Bass("TRN2", target_bir_lowering=False)
    P = Prog()
    es = ExitStack()

    def dram(name, shape, dt, kind):
        return nc.dram_tensor(name, shape, dt, kind=kind).ap()

    h0 = dram("h0", [128, 8, LT], F32, "ExternalInput")
    wsrc = dram("wsrc", [DEPTH * NPIECE * 4, 128, 1024], F32, "ExternalInput")
    pvd = dram("pv", [DEPTH, 128, NPV], F32, "ExternalInput")
    pmd = dram("pm", [DEPTH * 5, 128, 1024], F32, "ExternalInput")
    lbd = dram("lbl", [128, 12], F32, "ExternalInput")
    cfd = dram("cf", [128, 1024], F32, "ExternalInput")
    cbd = dram("cb", [128, 384], F32, "ExternalInput")
    outd = dram("out", [128, 8, LT], F32, "ExternalOutput")
    wbf = dram("wbf", [DEPTH * NPIECE, 128, 4096], BF16, "Internal")
    dbgd = dram("dbg", [NBLK * 16, 128, TB], F32, "ExternalOutput") if DEBUG else None

    def sb(name, shape, dt):
        return es.enter_context(nc.sbuf_tensor(name, shape, dt))

    hbf = sb("hbf", [128, 8, LT], BF16)
    wslot = [sb(f"wslot{i}", [128, 4096], BF16) for i in range(NWS)]
    Fre = sb("Fre", [128, 8, TB], BF16)
    nFim = sb("nFim", [128, 8, TB], BF16)
    cvf = [sb(f"cvf{i}", [128, 1024], F32) for i in range(2)]
    cvb = [sb(f"cvb{i}", [128, 1024], BF16) for i in range(2)]
    pv = sb("pvs", [128, NPV], F32)
    btbf = sb("btbf", [128, 2, 1024], BF16)
    ctbf = sb("ctbf", [128, 3, 1024], BF16)
    gwbf = sb("gwbf", [128, 512], BF16)
    cf = sb("cfs", [128, 1024], F32)
    cb = sb("cbs", [128, 384], BF16)
    onesb = sb("onesb", [128, 128], BF16)
    onesf = sb("onesf", [128, 128], F32)
    lbt = sb("lbt", [128, 3, 4], F32)
    lbw = sb("lbw", [128, 3, 4], F32)
    lba = sb("lba", [128, 3, DEPTH, 3], F32)
    s5s = sb("s5s", [128, 16, 8], F32)
    s5c = sb("s5c", [128, 2, 8], F32)
    hS = [sb(f"hS{i}", [128, 3, 64], F32) for i in range(3)]
    hSb = [sb(f"hSb{i}", [128, 3, 64], BF16) for i in range(3)]
    egl = sb("egl", [128, 3, 8], F32)
    Sm = sb("Sm", [128, 6, 64], F32)
    Smb = sb("Smb", [128, 6, 64], BF16)
    XB = sb("XB", [128, 7, 388], BF16)
    dgw = sb("dgw", [128, 28, 128], BF16)
    dgd = sb("dgd", [128, 3, 128], BF16)
    sm = sb("smalls", [128, 16, 16], F32)
    abc = sb("abc", [128, 8], F32)
    dtt = sb("dtt", [128, 3, 16], F32)
    acs = sb("acs", [128, 3, 48], F32)
    hgb = sb("hgb", [128, 2], F32)
    pib = sb("pib", [128, 2], F32)
    mcol = sb("mcol", [128, 2], F32)
    nident = sb("nident", [128, 128], BF16)
    negmf = sb("negmf", [128, 128], F32)
    Apool = [sb(f"A{i}", [128, TB], F32) for i in range(NA)]
    Bpool = [sb(f"B{i}", [128, TB], BF16) for i in range(NB)]
    print("sbuf bytes remaining", nc.sbuf_bytes_remaining)
    psum = [es.enter_context(nc.psum_tensor(f"ps{i}", [128, 512], F32)) for i in range(8)]

    tri = cf[:, 0:128]
    rstm = cf[:, 128:512]
    ramp = cf[:, 512:896]
    identf = cf[:, 896:1024]
    ident = cb[:, 0:128]
    negm = cb[:, 128:256]
    mask2 = cb[:, 256:384]

    class Pool:
        def __init__(self, tiles, nm):
            self.free = [(t, (nm, i)) for i, t in enumerate(tiles)]
            self.nm = nm

        def get(self):
            if not self.free:
                raise RuntimeError("pool empty " + self.nm)
            it = self.free.pop(0)
            self.minfree = min(getattr(self, 'minfree', 999), len(self.free))
            return it

        def put(self, *its):
            for it in its:
                self.free.append(it)

    A = Pool(Apool, 'A')
    B = Pool(Bpool, 'B')
    PS = Pool(psum, 'ps')
    smi = [0]

    def small():
        i = smi[0] % 16
        smi[0] += 1
        return sm[:, i, :], ('sm', i)

    def pe_mm(out, lhsT, rhs, start, stop, r, w):
        P.add('pe', lambda e: e.matmul(out, lhsT, rhs, start=start, stop=stop), r=r, w=w)

    def pe_tr(out, in_, r, w):
        P.add('pe', lambda e: e.transpose(out, in_, ident), r=list(r) + ['cb'], w=w)

    def act(out, in_, func, r, w, bias=0.0, scale=1.0):
        P.add('act', lambda e: e.activation(out, in_, func, bias=bias, scale=scale), r=r, w=w)

    def tt(eng, out, a, b, op, r, w):
        P.add(eng, lambda e: e.tensor_tensor(out, a, b, op), r=r, w=w)

    def ts(eng, out, a, s1, s2, op0, op1, r, w):
        if op1 is None:
            P.add(eng, lambda e: e.tensor_single_scalar(out, a, s1, op0), r=r, w=w)
        else:
            P.add(eng, lambda e: e.tensor_scalar(out, a, s1, s2, op0, op1), r=r, w=w)

    def stt(eng, out, a, s, b, op0, op1, r, w):
        P.add(eng, lambda e: e.scalar_tensor_tensor(out, a, s, b, op0, op1), r=r, w=w)

    def cp(eng, out, in_, r, w):
        if eng == 'act':
            act(out, in_, AF.Copy, r, w)
        else:
            P.add(eng, lambda e: e.tensor_copy(out, in_), r=r, w=w)

    def dma(out, in_, r, w):
        P.add('sp', lambda e: e.dma_start(out=out, in_=in_), r=r, w=w, dma=True)

    dbg_names = []

    def dump(name, ap, key, n=TB):
        if not DEBUG:
            return
        dt_ = dram("dbg_" + name, [128, n], F32, "ExternalOutput")
        dbg_names.append("dbg_" + name)
        a, ka = A.get()
        cp('dve', a[:, 0:n], ap, r=[key], w=[ka])
        dma(dt_[:, :], a[:, 0:n], r=[ka], w=[('dbgx', name)])
        A.put((a, ka))

    def memset(eng, ap, val, w):
        P.add(eng, lambda e: e.memset(ap, val), r=(), w=w)

    cvi = [0]

    pending = []

    def flush_convert():
        while pending:
            pending.pop(0)()

    def convert_unit(l, q):
        i = cvi[0] % 2
        eng = ['act', 'dve'][cvi[0] % 2]
        cvi[0] += 1
        pi, qq = q // 4, q % 4
        flush_convert() if len(pending) >= 1 and False else None
        dma(cvf[i][:, :], wsrc[(l * NPIECE * 4 + q)], r=[], w=[('cvf', i)])
        cp(eng, cvb[i][:, :], cvf[i][:, :], r=[('cvf', i)], w=[('cvb', i)])
        while pending:
            pending.pop(0)()
        pending.append(lambda: dma(wbf[l * NPIECE + pi][:, qq * 1024:(qq + 1) * 1024], cvb[i][:, :],
                                   r=[('cvb', i)], w=[('wbf', l, pi, qq)]))

    sched = []
    for l in range(DEPTH):
        for b in range(NBLK):
            sched += [(l, pi) for pi in range(10)]
        for b in range(NBLK):
            sched += [(l, pi) for pi in range(10, 26)]
    st = dict(next_load=0, next_use=0)

    def issue_loads(upto):
        while st['next_load'] <= min(upto, len(sched) - 1):
            k = st['next_load']
            l, pi = sched[k]
            s = k % NWS
            dma(wslot[s][:, :], wbf[l * NPIECE + pi], r=[('wbf', l, pi, qq) for qq in range(4)], w=[('w', s)])
            st['next_load'] += 1

    def get_piece(l, pi):
        k = st['next_use']
        assert sched[k] == (l, pi), (sched[k], l, pi)
        issue_loads(k + NWS - 1)
        st['next_use'] += 1
        return wslot[k % NWS], ('w', k % NWS)

    dma(cf[:, :], cfd[:, :], r=[], w=['cf'])
    dma(cvf[0][:, 0:384], cbd[:, :], r=[], w=[('cvf', 0)])
    cp('dve', cb[:, :], cvf[0][:, 0:384], r=[('cvf', 0)], w=['cb'])
    cp('dve', negmf[:, :], cvf[0][:, 128:256], r=[('cvf', 0)], w=['negmf'])
    ts('dve', nident[:, :], ident, -1.0, None, ALU.mult, None, r=['cb'], w=['nident'])
    memset('pool', onesb[:, :], 1.0, w=['onesb'])
    memset('pool', onesf[:, :], 1.0, w=['onesf'])
    memset('pool', pib[:, 0:1], PI, w=['pib'])
    memset('pool', mcol[0:64, 0:1], 1.0, w=['mcol'])
    memset('pool', mcol[64:128, 0:1], 0.0, w=['mcol'])
    memset('pool', mcol[0:64, 1:2], 0.0, w=['mcol'])
    memset('pool', mcol[64:128, 1:2], 1.0, w=['mcol'])
    memset('pool', pib[:, 1:2], -PI, w=['pib'])
    dma(lbt[:, :, :], lbd.rearrange("p (a b) -> p a b", a=3), r=[], w=['lbt'])
    act(lbw[:, :, :], lbt[:, :, :], AF.Exp, r=['lbt'], w=['lbw'])
    ssum, ksum = small()
    P.add('dve', lambda e: e.tensor_reduce(ssum[:, 0:3], lbw[:, :, :], AX.X, ALU.add), r=['lbw'], w=[ksum])
    rs, krs = small()
    P.add('dve', lambda e: e.reciprocal(rs[:, 0:3], ssum[:, 0:3]), r=[ksum], w=[krs])
    tt('dve', lbt[:, :, :], lbw[:, :, :], rs[:, 0:3].unsqueeze(2).broadcast_to([128, 3, 4]), ALU.mult,
       r=['lbw', krs], w=['lbt'])
    memset('dve', lbw[:, :, 0:1], 0.0, w=['lbw'])
    for l in range(1, 4):
        tt('dve', lbw[:, :, l:l + 1], lbw[:, :, l - 1:l], lbt[:, :, l:l + 1], ALU.add, r=['lbw', 'lbt'], w=['lbw'])
    for l in range(DEPTH):
        ts('dve', lba[:, :, l, 0:1], lbw[:, :, l:l + 1], 0.5, 0.5, ALU.mult, ALU.add, r=['lbw'], w=['lba'])
        ts('dve', lba[:, :, l, 1:2], lbw[:, :, l:l + 1], -0.5, 0.5, ALU.mult, ALU.add, r=['lbw'], w=['lba'])
        ts('dve', lba[:, :, l, 2:3], lbw[:, :, l:l + 1], 0.5, -0.5, ALU.mult, ALU.add, r=['lbw'], w=['lba'])

    for b in range(NBLK):
        for k in range(8):
            a, ka = A.get()
            dma(a[:, :], h0[:, k, b * TB:(b + 1) * TB], r=[], w=[ka])
            cp(['act', 'dve', 'pool'][k % 3], hbf[:, k, b * TB:(b + 1) * TB], a[:, :], r=[ka], w=[('h', b, k)])
            A.put((a, ka))

    for q in range(40):
        convert_unit(0, q)
    flush_convert()

    OFF = dict(convw=0, convb=28, ln1g=35, ln1b=43, ln2g=51, ln2b=59, glub=67, s5d=69, are=71, aim=79, ldt=87,
               dtb=95, alog=101, md=107, hnw=113, mnw=116, mdc=119)

    def pvc(name, i=0, n=1):
        o = OFF[name] + i
        return pv[:, o:o + n]

    def layer_setup(l):
        dma(pv[:, :], pvd[l], r=[], w=['pv'])
        for m in (0, 1, 4):
            i = cvi[0] % 2
            cvi[0] += 1
            dma(cvf[i][:, :], pmd[l * 5 + m], r=[], w=[('cvf', i)])
            if m < 2:
                cp('dve', btbf[:, m, :], cvf[i][:, :], r=[('cvf', i)], w=['btbf'])
            else:
                cp('dve', gwbf[:, :], cvf[i][:, 0:512], r=[('cvf', i)], w=['gwbf'])
        S = lambda i: s5s[:, i, :]
        ts('dve', S(0), pvc('are', 0, 8), -1e-4, None, ALU.min, None, r=['pv'], w=['s5s'])
        act(S(1), pvc('ldt', 0, 8), AF.Exp, r=['pv'], w=['s5s'])
        tt('dve', S(2), S(0), S(1), ALU.mult, r=['s5s'], w=['s5s'])
        act(S(3), S(2), AF.Exp, r=['s5s'], w=['s5s'])
        tt('dve', S(4), pvc('aim', 0, 8), S(1), ALU.mult, r=['pv', 's5s'], w=['s5s'])
        C1 = 6.28125
        C2 = 2 * PI - 6.28125
        I32 = mybir.dt.int32

        def reduce_to_pi(src, ksrc):
            qi, kqi = A.get()
            kf, kkf = A.get()
            ph, kph = A.get()
            ts('dve', qi[:, :].bitcast(I32), src[:, :], 1.0 / (2 * PI), None, ALU.mult, None, r=[ksrc], w=[kqi])
            cp('dve', kf[:, :], qi[:, :].bitcast(I32), r=[kqi], w=[kkf])
            stt('dve', ph[:, :], kf[:, :], -C1, src[:, :], ALU.mult, ALU.add, r=[kkf, ksrc], w=[kph])
            stt('dve', ph[:, :], kf[:, :], -C2, ph[:, :], ALU.mult, ALU.add, r=[kkf, kph], w=[kph])
            ts('dve', qi[:, :], ph[:, :], PI, 2 * PI, ALU.is_gt, ALU.mult, r=[kph], w=[kqi])
            tt('dve', ph[:, :], ph[:, :], qi[:, :], ALU.subtract, r=[kph, kqi], w=[kph])
            ts('dve', qi[:, :], ph[:, :], -PI, 2 * PI, ALU.is_lt, ALU.mult, r=[kph], w=[kqi])
            tt('dve', ph[:, :], ph[:, :], qi[:, :], ALU.add, r=[kph, kqi], w=[kph])
            A.put((qi, kqi), (kf, kkf))
            return ph, kph

        for j in range(8):
            a1, k1 = A.get()
            a2, k2 = A.get()
            ts('dve', a1[:, :], ramp, s5s[:, 4, j:j + 1], None, ALU.mult, None, r=['cf', 's5s'], w=[k1])
            ts('dve', a2[:, :], a1[:, :], 0.5 * PI, None, ALU.add, None, r=[k1], w=[k2])
            ph, kph = reduce_to_pi(a1, k1)
            act(nFim[:, j, :], ph[:, :], AF.Sin, r=[kph], w=[('nF', j)], scale=-1.0)
            A.put((ph, kph))
            ph, kph = reduce_to_pi(a2, k2)
            act(Fre[:, j, :], ph[:, :], AF.Sin, r=[kph], w=[('Fr', j)])
            A.put((ph, kph))
            A.put((a1, k1), (a2, k2))
        allF = [('Fr', j) for j in range(8)] + [('nF', j) for j in range(8)]
        tt('dve', S(5), S(3), Fre[:, :, 0], ALU.mult, r=['s5s'] + allF, w=['s5s'])
        tt('dve', S(6), S(3), nFim[:, :, 0], ALU.mult, r=['s5s'] + allF, w=['s5s'])
        tt('dve', S(7), S(0), S(0), ALU.mult, r=['s5s'], w=['s5s'])
        tt('dve', S(8), pvc('aim', 0, 8), pvc('aim', 0, 8), ALU.mult, r=['pv'], w=['s5s'])
        tt('dve', S(7), S(7), S(8), ALU.add, r=['s5s'], w=['s5s'])
        P.add('dve', lambda e: e.reciprocal(S(8), S(7)), r=['s5s'], w=['s5s'])
        ts('dve', S(9), S(5), -1.0, None, ALU.add, None, r=['s5s'], w=['s5s'])
        tt('dve', S(10), S(9), S(0), ALU.mult, r=['s5s'], w=['s5s'])
        tt('dve', S(11), S(6), pvc('aim', 0, 8), ALU.mult, r=['s5s', 'pv'], w=['s5s'])
        tt('dve', S(10), S(10), S(11), ALU.subtract, r=['s5s'], w=['s5s'])
        tt('dve', S(10), S(10), S(8), ALU.mult, r=['s5s'], w=['s5s'])
        tt('dve', S(11), S(6), S(0), ALU.mult, r=['s5s'], w=['s5s'])
        tt('dve', S(12), S(9), pvc('aim', 0, 8), ALU.mult, r=['s5s', 'pv'], w=['s5s'])
        tt('dve', S(11), S(11), S(12), ALU.add, r=['s5s'], w=['s5s'])
        tt('dve', S(11), S(11), S(8), ALU.mult, r=['s5s'], w=['s5s'])
        ts('dve', S(11), S(11), -1.0, None, ALU.mult, None, r=['s5s'], w=['s5s'])
        ire = cvi[0] % 2
        iim = 1 - ire
        cvi[0] += 2
        dma(cvf[ire][:, :], pmd[l * 5 + 2], r=[], w=[('cvf', ire)])
        dma(cvf[iim][:, :], pmd[l * 5 + 3], r=[], w=[('cvf', iim)])
        for j0, j1 in ((0, 3), (3, 6), (6, 8)):
            n = j1 - j0

            def g3(ap):
                return ap.rearrange("p (j c) -> p j c", j=n)
            sre = s5s[:, 10, j0:j1].unsqueeze(2).broadcast_to([128, n, 128])
            sim = s5s[:, 11, j0:j1].unsqueeze(2).broadcast_to([128, n, 128])
            cre = g3(cvf[ire][:, j0 * 128:j1 * 128])
            cim = g3(cvf[iim][:, j0 * 128:j1 * 128])
            obr = g3(ctbf[:, 0, j0 * 128:j1 * 128])
            obi = g3(ctbf[:, 1, j0 * 128:j1 * 128])
            a1, k1 = A.get()
            a2, k2 = A.get()
            t1 = g3(a1[:, 0:n * 128])
            t2 = g3(a2[:, 0:n * 128])
            tt('dve', t1, cre, sre, ALU.mult, r=[('cvf', ire), 's5s'], w=[k1])
            tt('dve', t2, cim, sim, ALU.mult, r=[('cvf', iim), 's5s'], w=[k2])
            tt('dve', obr, t1, t2, ALU.subtract, r=[k1, k2], w=['ctbf'])
            tt('dve', t1, cre, sim, ALU.mult, r=[('cvf', ire), 's5s'], w=[k1])
            tt('dve', t2, cim, sre, ALU.mult, r=[('cvf', iim), 's5s'], w=[k2])
            tt('dve', obi, t1, t2, ALU.add, r=[k1, k2], w=['ctbf'])
            ts('dve', g3(ctbf[:, 2, j0 * 128:j1 * 128]), obi, -1.0, None, ALU.mult, None, r=['ctbf'], w=['ctbf'])
            A.put((a1, k1), (a2, k2))
        for cj in range(28):
            ts('pool' if cj % 2 else 'dve', dgw[:, cj, :], ident, pvc('convw', cj), None, ALU.mult, None,
               r=['cb', 'pv'], w=['dgw'])
        for c in range(3):
            ts('dve', dgd[:, c, :], ident, pvc('mdc', c), None, ALU.mult, None, r=['cb', 'pv'], w=['dgd'])
        act(abc[:, 0:6], pvc('alog', 0, 6), AF.Exp, r=['pv'], w=['abc'])
        ts('dve', abc[:, 0:6], abc[:, 0:6], -1.0, None, ALU.mult, None, r=['abc'], w=['abc'])
        ts('dve', hgb[:, 0:2], pvc('glub', 0, 2), 0.5, None, ALU.mult, None, r=['pv'], w=['hgb'])
        for i in range(3):
            memset('pool', hS[i][:, :, :], 0.0, w=[('hS', i)])
            memset('pool', hSb[i][:, :, :], 0.0, w=[('hSb', i)])
        memset('pool', Sm[:, :, :], 0.0, w=['Sm'])
        memset('pool', Smb[:, :, :], 0.0, w=['Smb'])
        memset('pool', s5c[:, :, :], 0.0, w=['s5c'])
        memset('pool', XB[:, :, 0:3], 0.0, w=[('XB', c) for c in range(7)])

    FMMAP = {}
    for pc in range(3):
        FMMAP[('q', pc)] = (0, pc * 1024)
    FMMAP[('f', 0)] = (0, 3072)
    FMMAP[('f', 1)] = (1, 0)
    FMMAP[('f', 2)] = (1, 1024)
    for c in range(7):
        FMMAP[('x', c)] = (4, c * 1024) if c < 4 else (5, (c - 4) * 1024)
    for m in range(2):
        FMMAP[('u', m)] = (7, m * 1024)

    def hk(b, k):
        return ('h', b, k)

    def fm_proj(b, slot, kslot, off):
        ps, kp = PS.get()
        for k in range(8):
            pe_mm(ps[:, 0:TB], slot[:, off + k * 128: off + (k + 1) * 128], hbf[:, k, b * TB:(b + 1) * TB],
                  k == 0, k == 7, r=[kslot, hk(b, k)], w=[kp])
        return ps, kp

    def tm_proj(b, t, slot, kslot, n):
        ps, kp = PS.get()
        c0 = b * TB + t * 128
        for k in range(8):
            pe_mm(ps[:, 0:n], hbf[:, k, c0:c0 + 128], slot[:, k * n:(k + 1) * n], k == 0, k == 7,
                  r=[kslot, hk(b, k)], w=[kp])
        return ps, kp

    def v3(ap, h):
        return ap.rearrange("p (h v) -> p h v", h=h)

    def bc(ap, n, m):
        return ap.unsqueeze(2).broadcast_to([128, n, m])

    def rstd_small(ss, kss, n, inv, eps):
        ts('dve', ss[:, 0:n], ss[:, 0:n], inv, eps, ALU.mult, ALU.add, r=[kss], w=[kss])
        act(ss[:, 0:n], ss[:, 0:n], AF.Ln, r=[kss], w=[kss])
        act(ss[:, 0:n], ss[:, 0:n], AF.Exp, r=[kss], w=[kss], scale=-0.5)

    def layernorm(rr, gname, bname, dst_fn):
        p1, k1 = PS.get()
        p2, k2 = PS.get()
        for n in range(8):
            rb, krb = B.get()
            rq, krq = B.get()
            cp('act', rb[:, :], rr[n][0][:, :], r=[rr[n][1]], w=[krb])
            act(rq[:, :], rr[n][0][:, :], AF.Square, r=[rr[n][1]], w=[krq])
            pe_mm(p1[:, 0:TB], onesb[:, :], rb[:, :], n == 0, n == 7, r=['onesb', krb], w=[k1])
            pe_mm(p2[:, 0:TB], onesb[:, :], rq[:, :], n == 0, n == 7, r=['onesb', krq], w=[k2])
            B.put((rb, krb), (rq, krq))
        mean, kme = A.get()
        msq, kms = A.get()
        rstd, krs_ = A.get()
        ts('dve', mean[:, :], p1[:, 0:TB], 1.0 / 1024, None, ALU.mult, None, r=[k1], w=[kme])
        tt('pool', msq[:, :], mean[:, :], mean[:, :], ALU.mult, r=[kme], w=[kms])
        stt('dve', rstd[:, :], p2[:, 0:TB], 1.0 / 1024, msq[:, :], ALU.mult, ALU.subtract, r=[k2, kms], w=[krs_])
        ts('dve', rstd[:, :], rstd[:, :], LN_EPS, None, ALU.add, None, r=[krs_], w=[krs_])
        act(rstd[:, :], rstd[:, :], AF.Ln, r=[krs_], w=[krs_])
        act(rstd[:, :], rstd[:, :], AF.Exp, r=[krs_], w=[krs_], scale=-0.5)
        PS.put((p1, k1), (p2, k2))
        for n in range(8):
            t1, kt1 = A.get()
            tt('pool', t1[:, :], rr[n][0][:, :], mean[:, :], ALU.subtract, r=[rr[n][1], kme], w=[kt1])
            tt('dve', t1[:, :], t1[:, :], rstd[:, :], ALU.mult, r=[kt1, krs_], w=[kt1])
            dst, kd, post = dst_fn(n)
            act(dst, t1[:, :], AF.Identity, r=[kt1, 'pv'], w=kd, bias=pvc(bname, n), scale=pvc(gname, n))
            A.put((t1, kt1))
            if post is not None:
                post()
        A.put((mean, kme), (msq, kms), (rstd, krs_))

    def mixer_block(l, b):
        c0 = b * TB
        yT = [B.get() for _ in range(8)]
        wmap = {}

        def wp_(pi):
            if pi not in wmap:
                wmap[pi] = get_piece(l, pi)
            return wmap[pi]
        qs, tth = [], []
        for pc in range(3):
            pi, off = FMMAP[('q', pc)]
            ps, kp = fm_proj(b, wp_(pi)[0], wp_(pi)[1], off)
            a, ka = A.get()
            act(a[:, :], ps[:, 0:TB], AF.Silu, r=[kp], w=[ka])
            PS.put((ps, kp))
            qs.append((a, ka))
        for pc in range(3):
            pi, off = FMMAP[('f', pc)]
            ps, kp = fm_proj(b, wp_(pi)[0], wp_(pi)[1], off)
            a, ka = A.get()
            act(a[:, :], ps[:, 0:TB], AF.Tanh, r=[kp], w=[ka], scale=0.5)
            PS.put((ps, kp))
            tth.append((a, ka))
        if STAGE <= 2:
            raise _Stop()
        qT, kT = [], []
        for pc in range(3):
            t_, kt = tth[pc]
            lf, klf = A.get()
            act(lf[:, :], t_[:, :], AF.Ln, r=[kt, 'lba'], w=[klf], bias=lba[:, pc, l, 0:1], scale=lba[:, pc, l, 1:2])
            kk, kkk = A.get()
            ts('dve', kk[:, :], t_[:, :], lba[:, pc, l, 2:3], lba[:, pc, l, 1:2], ALU.mult, ALU.add,
               r=[kt, 'lba'], w=[kkk])
            A.put(tth[pc])
            G, kG = A.get()
            P.add('dve', lambda e, G=G, lf=lf: e.tensor_tensor_scan(G[:, :], rstm, lf[:, :], 0.0, ALU.mult, ALU.add),
                  r=['cf', klf], w=[kG])
            A.put((lf, klf))
            eG, keG = A.get()
            enG, kenG = A.get()
            act(eG[:, :], G[:, :], AF.Exp, r=[kG], w=[keG])
            act(enG[:, :], G[:, :], AF.Exp, r=[kG], w=[kenG], scale=-1.0)
            if b == 0 and pc == 0 and l == 0:
                dump('G', G[:, :], kG)
                dump('qs', qs[pc][0][:, :], qs[pc][1])
                dump('kk', kk[:, :], kkk)
                dump('eG', eG[:, :], keG)
                dump('enG', enG[:, :], kenG)
            A.put((G, kG))
            cp('pool', egl[:, pc, 0:6], v3(eG[:, :], 6)[:, :, 63], r=[keG], w=['egl'])
            q_, kq = B.get()
            q2_, kq2 = B.get()
            k_, kk_ = B.get()
            stt('dve', q_[:, :], qs[pc][0][:, :], mcol[:, 0:1], eG[:, :], ALU.mult, ALU.mult, r=[qs[pc][1], keG, 'mcol'], w=[kq])
            stt('dve', q2_[:, :], qs[pc][0][:, :], mcol[:, 1:2], eG[:, :], ALU.mult, ALU.mult, r=[qs[pc][1], keG, 'mcol'], w=[kq2])
            tt('pool', k_[:, :], kk[:, :], enG[:, :], ALU.mult, r=[kkk, kenG], w=[kk_])
            A.put(qs[pc], (kk, kkk), (eG, keG), (enG, kenG))
            qT.append(((q_, kq), (q2_, kq2)))
            kT.append((k_, kk_))
        if STAGE <= 1:
            raise _Stop()
        wi = get_piece(l, 2)
        vb = []
        for t in range(NT):
            ps, kp = tm_proj(b, t, wi[0], wi[1], 384)
            v, kv = B.get()
            cp('act', v[:, :], ps[:, 0:384], r=[kp], w=[kv])
            PS.put((ps, kp))
            vb.append((v, kv))
        wg = get_piece(l, 3)
        sg = []
        for t in range(NT):
            ps, kp = tm_proj(b, t, wg[0], wg[1], 384)
            a, ka = A.get()
            act(a[:, :], ps[:, 0:384], AF.Silu, r=[kp], w=[ka])
            PS.put((ps, kp))
            sg.append((a, ka))
        if STAGE <= 3:
            raise _Stop()
        def hg_tile(t):
            if True:
                tc0 = t * 128
                pst, kpt = PS.get()
                pstb = pst.bitcast(BF16)
                for pc in range(3):
                    pe_tr(pstb[:, pc * 128:(pc + 1) * 128], kT[pc][0][:, tc0:tc0 + 128], r=[kT[pc][1]], w=[kpt])
                ktm, kktm = B.get()
                cp('dve', ktm[:, :], pstb[:, 0:384], r=[kpt], w=[kktm])
                PS.put((pst, kpt))
                yield
                pss = [PS.get(), PS.get()]
                for h in range(6):
                    pc, po = h // 2, 64 * (h % 2)
                    ps, kp = pss[h // 3]
                    qm, kqm = qT[pc][h % 2]
                    pe_mm(ps[:, (h % 3) * 128:(h % 3 + 1) * 128], kT[pc][0][:, tc0:tc0 + 128],
                          qm[:, tc0:tc0 + 128], True, True, r=[kT[pc][1], kqm], w=[kp])
                yield
                AT = [B.get(), B.get()]
                for g in range(2):
                    tt('dve', v3(AT[g][0][:, :], 3), v3(pss[g][0][:, 0:384], 3),
                       mask2.unsqueeze(1).broadcast_to([128, 3, 128]), ALU.mult, r=[pss[g][1], 'cb'], w=[AT[g][1]])
                    PS.put(pss[g])
                yield
                gci0 = (b * NT + t) * 2
                for c in range(2):
                    cur, nw = (gci0 + c) % 3, (gci0 + c + 1) % 3
                    ps, kp = PS.get()
                    r0 = 64 * c
                    for h in range(6):
                        pc, po = h // 2, 64 * (h % 2)
                        pe_mm(ps[po:po + 64, pc * 64:(pc + 1) * 64], ktm[r0:r0 + 64, h * 64:(h + 1) * 64],
                              vb[t][0][r0:r0 + 64, h * 64:(h + 1) * 64], True, True, r=[kktm, vb[t][1]], w=[kp])
                    tmp, ktmp = A.get()
                    tt('dve', v3(tmp[:, 0:192], 3), v3(ps[:, 0:192], 3), hS[cur][:, :, :], ALU.add,
                       r=[kp, ('hS', cur)], w=[ktmp])
                    PS.put((ps, kp))
                    tt('dve', hS[nw][:, :, :], v3(tmp[:, 0:192], 3),
                       egl[:, :, 2 * t + c:2 * t + c + 1].broadcast_to([128, 3, 64]), ALU.mult,
                       r=[ktmp, 'egl'], w=[('hS', nw)])
                    tt('dve', hSb[nw][:, :, :], v3(tmp[:, 0:192], 3),
                       egl[:, :, 2 * t + c:2 * t + c + 1].broadcast_to([128, 3, 64]), ALU.mult,
                       r=[ktmp, 'egl'], w=[('hSb', nw)])
                    A.put((tmp, ktmp))
                yield
                pso, kpo = PS.get()
                for h in range(6):
                    pc, po = h // 2, 64 * (h % 2)
                    s0, s1 = gci0 % 3, (gci0 + 1) % 3
                    pe_mm(pso[:, h * 64:(h + 1) * 64], AT[h // 3][0][:, (h % 3) * 128:(h % 3 + 1) * 128],
                          vb[t][0][:, h * 64:(h + 1) * 64], True, False, r=[AT[h // 3][1], vb[t][1]], w=[kpo])
                    qm, kqm = qT[pc][h % 2]
                    pe_mm(pso[0:64, h * 64:(h + 1) * 64], qm[:, tc0:tc0 + 64],
                          hSb[s0][:, pc, :], False, True, r=[kqm, ('hSb', s0)], w=[kpo])
                    pe_mm(pso[64:128, h * 64:(h + 1) * 64], qm[:, tc0 + 64:tc0 + 128],
                          hSb[s1][:, pc, :], False, True, r=[kqm, ('hSb', s1)], w=[kpo])
                B.put(*AT)
                B.put((ktm, kktm))
                yield
                sq, ksq = A.get()
                act(sq[:, :], pso[:, 0:384], AF.Square, r=[kpo], w=[ksq])
                ss, kss = small()
                P.add('dve', lambda e, ss=ss, sq=sq: e.tensor_reduce(ss[:, 0:6], v3(sq[:, :], 6), AX.X, ALU.add),
                      r=[ksq], w=[kss])
                rstd_small(ss, kss, 6, 1.0 / 64, RMS_EPS)
                tt('dve', v3(sq[:, :], 6), v3(pso[:, 0:384], 6), bc(ss[:, 0:6], 6, 64), ALU.mult, r=[kpo, kss], w=[ksq])
                PS.put((pso, kpo))
                yield
                yb_, kyb = B.get()
                tt('pool', yb_[:, :], sq[:, :], sg[t][0][:, :], ALU.mult, r=[ksq, sg[t][1]], w=[kyb])
                A.put((sq, ksq))
                pst, kpt = PS.get()
                pstb = pst.bitcast(BF16)
                for pc in range(3):
                    pe_tr(pstb[:, pc * 128:(pc + 1) * 128], yb_[:, pc * 128:(pc + 1) * 128], r=[kyb], w=[kpt])
                for pc in range(3):
                    ts('dve', yT[pc][0][:, tc0:tc0 + 128], pstb[:, pc * 128:(pc + 1) * 128], pvc('hnw', pc), None,
                       ALU.mult, None, r=[kpt, 'pv'], w=[yT[pc][1]])
                PS.put((pst, kpt))
                yield
                B.put((yb_, kyb))

        def hg_gen():
            tiles = [hg_tile(t) for t in range(NT)]
            started, live, step = 0, [], 0
            while started < NT or live:
                if started < NT and step % HG_LAG == 0:
                    live.append(tiles[started])
                    started += 1
                for g in list(live):
                    try:
                        next(g)
                    except StopIteration:
                        live.remove(g)
                step += 1
                yield
            for x in kT + vb:
                B.put(x)
            for x in qT:
                B.put(x[0], x[1])
            for x in sg:
                A.put(x)
            yield

        xc = []
        sz = []
        def prep_gen():
            allXB = [('XB', c) for c in range(7)]
            if b > 0:
                cp('dve', XB[:, :, 0:3], XB[:, :, 384:387], r=allXB, w=allXB)
            for c in range(7):
                pi, off = FMMAP[('x', c)]
                ps, kp = fm_proj(b, wp_(pi)[0], wp_(pi)[1], off)
                cp('act', XB[:, c, 3:387], ps[:, 0:TB], r=[kp], w=[('XB', c)])
                PS.put((ps, kp))
                yield
                ps2, kp2 = PS.get()
                for j in range(4):
                    pe_mm(ps2[:, 0:TB], dgw[:, c * 4 + j, :], XB[:, c, j:j + TB], j == 0, j == 3,
                          r=['dgw', ('XB', c)], w=[kp2])
                x_, kx = B.get()
                act(x_[:, :], ps2[:, 0:TB], AF.Silu, r=[kp2, 'pv'], w=[kx], bias=pvc('convb', c))
                PS.put((ps2, kp2))
                xc.append((x_, kx))
                yield
            w6 = get_piece(l, 6)
            for t in range(NT):
                ps, kp = tm_proj(b, t, w6[0], w6[1], 390)
                a, ka = A.get()
                act(a[:, :], ps[:, 0:384], AF.Silu, r=[kp], w=[ka])
                cp('act', dtt[:, t, 0:6], ps[:, 384:390], r=[kp], w=[('dtt', t)])
                tt('dve', dtt[:, t, 0:6], dtt[:, t, 0:6], pvc('dtb', 0, 6), ALU.add, r=[('dtt', t), 'pv'], w=[('dtt', t)])
                yield
                PS.put((ps, kp))
                sz.append((a, ka))
            for t in range(NT):
                act(dtt[:, t, 0:6], dtt[:, t, 0:6], AF.Exp, r=[('dtt', t)], w=[('dtt', t)])
            for t in range(NT):
                act(dtt[:, t, 0:6], dtt[:, t, 0:6], AF.Ln, r=[('dtt', t)], w=[('dtt', t)], bias=1.0)
                tt('dve', dtt[:, t, 8:14], dtt[:, t, 0:6], abc[:, 0:6], ALU.mult, r=[('dtt', t), 'abc'], w=[('dtt', t)])
            if STAGE == 45:
                raise _Stop()
            yield

        gens0 = [(hg_gen(), 1), (prep_gen(), 2)]
        alive0 = list(gens0)
        while alive0:
            for g in list(alive0):
                for _ in range(g[1]):
                    try:
                        next(g[0])
                    except StopIteration:
                        alive0.remove(g)
                        break

        if STAGE <= 4:
            raise _Stop()
        def ssd_gen():
            for t in range(NT):
                tc0 = t * 128
                kdt = ('dtt', t)
                dtv = dtt[:, t, 0:6]
                dA = dtt[:, t, 8:14]
                pst, kpt = PS.get()
                pstb = pst.bitcast(BF16)
                for j in range(5):
                    pe_tr(pstb[:, j * 128:(j + 1) * 128], xc[j][0][:, tc0:tc0 + 128], r=[xc[j][1]], w=[kpt])
                if STAGE == 461 and t == TCUT:
                    raise _Stop()
                xtm, kxtm = B.get()
                if VAR == 1:
                    _sp = B.get()
                btm, kbtm = B.get()
                cp('act', xtm[:, :], pstb[:, 0:384], r=[kpt], w=[kxtm])
                if STAGE == 462 and t == TCUT:
                    raise _Stop()
                cp('act', btm[:, 0:256], pstb[:, 384:640], r=[kpt], w=[kbtm])
                PS.put((pst, kpt))
                if STAGE == 460 and t == TCUT:
                    raise _Stop()
                yield
                pa, kpa = PS.get()
                pe_mm(pa[:, 0:6], tri, dA, True, True, r=['cf', kdt], w=[kpa])
                pe_mm(pa[:, 8:14], onesf[:, :], dA, True, True, r=['onesf', kdt], w=[kpa])
                ac = acs[:, t, :]
                kac = ('acs', t)
                cp('dve', ac[:, 0:14], pa[:, 0:14], r=[kpa], w=[kac])
                PS.put((pa, kpa))
                if STAGE == 46 and t == TCUT:
                    raise _Stop()
                act(ac[:, 16:30], ac[:, 0:14], AF.Exp, r=[kac], w=[kac])
                ts('dve', ac[:, 32:38], ac[:, 0:6], -1.0, None, ALU.mult, None, r=[kac], w=[kac])
                tt('dve', ac[:, 40:46], ac[:, 8:14], ac[:, 0:6], ALU.subtract, r=[kac], w=[kac])
                act(ac[:, 40:46], ac[:, 40:46], AF.Exp, r=[kac], w=[kac])
                eac = ac[:, 16:22]
                etot = ac[:, 24:30]
                dec = ac[:, 40:46]
                yield
                dAb = [A.get(), A.get()]
                for g in range(2):
                    cp('pool', v3(dAb[g][0][:, :], 3), bc(dA[:, 3 * g:3 * g + 3], 3, 128), r=[kdt], w=[dAb[g][1]])
                pL = [PS.get(), PS.get()]
                for h in range(6):
                    ps, kp = pL[h // 3]
                    o = (h % 3) * 128
                    pe_mm(ps[:, o:o + 128], dAb[h // 3][0][:, o:o + 128], tri, True, False, r=[dAb[h // 3][1], 'cf'], w=[kp])
                    pe_mm(ps[:, o:o + 128], identf, negmf[:, :], False, True, r=['cf', 'negmf'], w=[kp])
                LTs = [A.get(), A.get()]
                for h in range(6):
                    o = (h % 3) * 128
                    act(LTs[h // 3][0][:, o:o + 128], pL[h // 3][0][:, o:o + 128], AF.Exp, r=[pL[h // 3][1], kac],
                        w=[LTs[h // 3][1]], bias=ac[:, 32 + h:33 + h])
                PS.put(*pL)
                A.put(*dAb)
                if STAGE == 47 and t == TCUT:
                    raise _Stop()
                yield
                psg, kpsg = PS.get()
                for g in range(2):
                    pe_mm(psg[:, g * 128:(g + 1) * 128], xc[3 + g][0][:, tc0:tc0 + 128], xc[5 + g][0][:, tc0:tc0 + 128],
                          True, True, r=[xc[3 + g][1], xc[5 + g][1]], w=[kpsg])
                AT = [B.get(), B.get()]
                for g in range(2):
                    tt('dve', v3(AT[g][0][:, :], 3), v3(LTs[g][0][:, :], 3),
                       psg[:, g * 128:(g + 1) * 128].unsqueeze(1).broadcast_to([128, 3, 128]), ALU.mult,
                       r=[LTs[g][1], kpsg], w=[AT[g][1]])
                PS.put((psg, kpsg))
                A.put(*LTs)
                if STAGE == 48 and t == TCUT:
                    raise _Stop()
                yield
                xdt, kxdt = B.get()
                xw, kxw = B.get()
                tt('pool', v3(xdt[:, :], 6), v3(xtm[:, :], 6), bc(dtv, 6, 64), ALU.mult, r=[kxtm, kdt], w=[kxdt])
                tt('pool', v3(xw[:, :], 6), v3(xdt[:, :], 6), bc(dec, 6, 64), ALU.mult, r=[kxdt, kac], w=[kxw])
                psy, kpsy = PS.get()
                psyo, kpsyo = PS.get()
                psS, kpsS = PS.get()
                for h in range(6):
                    g = h // 3
                    hs = slice(h * 64, (h + 1) * 64)
                    pe_mm(psy[:, hs], xc[h // 2][0][:, tc0:tc0 + 128], dgd[:, h // 2, (h % 2) * 64:(h % 2 + 1) * 64],
                          True, False, r=[xc[h // 2][1], 'dgd'], w=[kpsy])
                    pe_mm(psy[:, hs], AT[g][0][:, (h % 3) * 128:(h % 3 + 1) * 128], xdt[:, hs], False, True,
                          r=[AT[g][1], kxdt], w=[kpsy])
                for h in range(6):
                    g = h // 3
                    hs = slice(h * 64, (h + 1) * 64)
                    pe_mm(psyo[:, hs], xc[5 + g][0][:, tc0:tc0 + 128], Smb[:, h, :], True, True,
                          r=[xc[5 + g][1], 'Smb'], w=[kpsyo])
                for h in range(6):
                    g = h // 3
                    hs = slice(h * 64, (h + 1) * 64)
                    pe_mm(psS[:, hs], btm[:, g * 128:(g + 1) * 128], xw[:, hs], True, True, r=[kbtm, kxw], w=[kpsS])
                B.put(*AT)
                if STAGE == 49 and t == TCUT:
                    raise _Stop()
                yield
                t1, kt1 = A.get()
                t2, kt2 = A.get()
                tt('dve', v3(t1[:, :], 6), v3(psyo[:, 0:384], 6), bc(eac, 6, 64), ALU.mult, r=[kpsyo, kac], w=[kt1])
                PS.put((psyo, kpsyo))
                tt('dve', t1[:, :], psy[:, 0:384], t1[:, :], ALU.add, r=[kpsy, kt1], w=[kt1])
                PS.put((psy, kpsy))
                tt('pool', t1[:, :], t1[:, :], sz[t][0][:, :], ALU.mult, r=[kt1, sz[t][1]], w=[kt1])
                act(t2[:, :], t1[:, :], AF.Square, r=[kt1], w=[kt2])
                ss, kss = small()
                P.add('dve', lambda e, ss=ss, t2=t2: e.tensor_reduce(ss[:, 0:2], v3(t2[:, :], 2), AX.X, ALU.add),
                      r=[kt2], w=[kss])
                rstd_small(ss, kss, 2, 1.0 / 192, RMS_EPS)
                yb_, kyb = B.get()
                tt('pool', v3(yb_[:, :], 2), v3(t1[:, :], 2), bc(ss[:, 0:2], 2, 192), ALU.mult, r=[kt1, kss], w=[kyb])
                if STAGE == 50 and t == TCUT:
                    raise _Stop()
                yield
                tt('dve', v3(t2[:, :], 6), Sm[:, :, :], bc(etot, 6, 64), ALU.mult, r=['Sm', kac], w=[kt2])
                tt('dve', Sm[:, :, :], v3(t2[:, :], 6), v3(psS[:, 0:384], 6), ALU.add, r=[kt2, kpsS], w=['Sm'])
                PS.put((psS, kpsS))
                cp('act', Smb[:, :, :], Sm[:, :, :], r=['Sm'], w=['Smb'])
                if STAGE == 51 and t == TCUT:
                    raise _Stop()
                yield
                A.put((t1, kt1), (t2, kt2))
                pst, kpt = PS.get()
                pstb = pst.bitcast(BF16)
                for pc in range(3):
                    pe_tr(pstb[:, pc * 128:(pc + 1) * 128], yb_[:, pc * 128:(pc + 1) * 128], r=[kyb], w=[kpt])
                for pc in range(3):
                    act(yT[3 + pc][0][:, tc0:tc0 + 128], pstb[:, pc * 128:(pc + 1) * 128], AF.Copy,
                        r=[kpt, 'pv'], w=[yT[3 + pc][1]], scale=pvc('mnw', pc))
                PS.put((pst, kpt))
                yield
                B.put((yb_, kyb), (xtm, kxtm), (btm, kbtm), (xdt, kxdt), (xw, kxw))
                if STAGE == 52 + t:
                    raise _Stop()
            for x in xc:
                B.put(x)
            for x in sz:
                A.put(x)


        if STAGE <= 5:
            raise _Stop()
        def s5_gen():
            w7 = get_piece(l, 7)
            uf, ub = [], []
            for m in range(2):
                pi, off = FMMAP[('u', m)]
                ps, kp = fm_proj(b, w7[0], w7[1], off)
                a, ka = A.get()
                u_, ku = B.get()
                cp('act', a[:, :], ps[:, 0:TB], r=[kp], w=[ka])
                cp('act', u_[:, :], ps[:, 0:TB], r=[kp], w=[ku])
                PS.put((ps, kp))
                uf.append((a, ka))
                ub.append((u_, ku))
            if STAGE == 60:
                raise _Stop()
            gl = []
            gelb = []
            for m in range(2):
                py, kpy = PS.get()

                def chunk_gen(jj, m=m, py=py, kpy=kpy):
                    j = 4 * m + jj
                    kF, kN = ('Fr', j), ('nF', j)
                    pA, kpA = PS.get()
                    pB, kpB = PS.get()
                    pe_mm(pA[:, 0:TB], btbf[:, 0, j * 128:(j + 1) * 128], ub[m][0][:, :], True, True,
                          r=['btbf', ub[m][1]], w=[kpA])
                    pe_mm(pB[:, 0:TB], btbf[:, 1, j * 128:(j + 1) * 128], ub[m][0][:, :], True, True,
                          r=['btbf', ub[m][1]], w=[kpB])
                    yield
                    a_bf, kab = B.get()
                    b_bf, kbb = B.get()
                    cp('act', a_bf[:, :], pA[:, 0:TB], r=[kpA], w=[kab])
                    cp('act', b_bf[:, :], pB[:, 0:TB], r=[kpB], w=[kbb])
                    PS.put((pA, kpA), (pB, kpB))
                    qq = [B.get() for _ in range(4)]
                    tt('dve', qq[0][0][:, :], Fre[:, j, :], a_bf[:, :], ALU.mult, r=[kF, kab], w=[qq[0][1]])
                    tt('pool', qq[1][0][:, :], nFim[:, j, :], b_bf[:, :], ALU.mult, r=[kN, kbb], w=[qq[1][1]])
                    tt('dve', qq[2][0][:, :], Fre[:, j, :], b_bf[:, :], ALU.mult, r=[kF, kbb], w=[qq[2][1]])
                    tt('pool', qq[3][0][:, :], nFim[:, j, :], a_bf[:, :], ALU.mult, r=[kN, kab], w=[qq[3][1]])
                    B.put((a_bf, kab), (b_bf, kbb))
                    yield
                    pre, kpre = PS.get()
                    pim, kpim = PS.get()
                    pe_mm(pre[:, 0:TB], ident, qq[0][0][:, :], True, False, r=['cb', qq[0][1]], w=[kpre])
                    pe_mm(pre[:, 0:TB], nident[:, :], qq[1][0][:, :], False, True, r=['nident', qq[1][1]], w=[kpre])
                    pe_mm(pim[:, 0:TB], ident, qq[2][0][:, :], True, False, r=['cb', qq[2][1]], w=[kpim])
                    pe_mm(pim[:, 0:TB], ident, qq[3][0][:, :], False, True, r=['cb', qq[3][1]], w=[kpim])
                    B.put(*qq)
                    if STAGE == 61 and jj == 0 and m == 0:
                        raise _Stop()
                    yield
                    t2, k2 = A.get()
                    t4, k4 = A.get()
                    rj = s5s[:, 3, j:j + 1].broadcast_to([128, TB])
                    P.add('dve', lambda e, o=t2, rj=rj, p=pre, i=s5c[:, 0, j:j + 1]:
                          e.tensor_tensor_scan(o[:, :], rj, p[:, 0:TB], i, ALU.mult, ALU.add),
                          r=['s5s', kpre, ('s5c', j)], w=[k2])
                    P.add('dve', lambda e, o=t4, rj=rj, p=pim, i=s5c[:, 1, j:j + 1]:
                          e.tensor_tensor_scan(o[:, :], rj, p[:, 0:TB], i, ALU.mult, ALU.add),
                          r=['s5s', kpim, ('s5c', j)], w=[k4])
                    PS.put((pre, kpre), (pim, kpim))
                    if STAGE == 62 and jj == 0 and m == 0:
                        raise _Stop()
                    yield
                    bb = [B.get() for _ in range(4)]
                    tt('pool', bb[0][0][:, :], Fre[:, j, :], t2[:, :], ALU.mult, r=[kF, k2], w=[bb[0][1]])
                    tt('dve', bb[1][0][:, :], nFim[:, j, :], t4[:, :], ALU.mult, r=[kN, k4], w=[bb[1][1]])
                    tt('pool', bb[2][0][:, :], nFim[:, j, :], t2[:, :], ALU.mult, r=[kN, k2], w=[bb[2][1]])
                    tt('pool', bb[3][0][:, :], Fre[:, j, :], t4[:, :], ALU.mult, r=[kF, k4], w=[bb[3][1]])
                    cc, kcc = small()
                    e = TB - 1
                    tt('dve', cc[:, 0:1], Fre[:, j, e:e + 1], t2[:, e:e + 1], ALU.mult, r=[kF, k2], w=[kcc])
                    tt('dve', cc[:, 1:2], nFim[:, j, e:e + 1], t2[:, e:e + 1], ALU.mult, r=[kN, k2], w=[kcc])
                    tt('dve', cc[:, 2:3], nFim[:, j, e:e + 1], t4[:, e:e + 1], ALU.mult, r=[kN, k4], w=[kcc])
                    tt('dve', cc[:, 3:4], Fre[:, j, e:e + 1], t4[:, e:e + 1], ALU.mult, r=[kF, k4], w=[kcc])
                    tt('dve', s5c[:, 0, j:j + 1], cc[:, 0:1], cc[:, 2:3], ALU.add, r=[kcc], w=[('s5c', j)])
                    tt('dve', s5c[:, 1, j:j + 1], cc[:, 3:4], cc[:, 1:2], ALU.subtract, r=[kcc], w=[('s5c', j)])
                    if STAGE == 63 and jj == 0 and m == 0:
                        raise _Stop()
                    yield
                    A.put((t2, k2), (t4, k4))
                    for q_, ci_ in enumerate((0, 0, 1, 2)):
                        pe_mm(py[:, 0:TB], ctbf[:, ci_, j * 128:(j + 1) * 128], bb[q_][0][:, :],
                              jj == 0 and q_ == 0, jj == 3 and q_ == 3, r=['ctbf', bb[q_][1]], w=[kpy])
                    B.put(*bb)

                for pair in ((0, 1), (2, 3)):
                    live = [chunk_gen(jj) for jj in pair]
                    while live:
                        for g in list(live):
                            try:
                                next(g)
                            except StopIteration:
                                live.remove(g)
                        yield
                yv, kyv = A.get()
                stt('dve', yv[:, :], uf[m][0][:, :], pvc('s5d', m), py[:, 0:TB], ALU.mult, ALU.add,
                    r=[uf[m][1], 'pv', kpy], w=[kyv])
                PS.put((py, kpy))
                A.put(uf[m])
                yield
                gb, kgb = B.get()
                if GELU_ACT:
                    act(gb[:, :], yv[:, :], AF.Gelu_apprx_tanh, r=[kyv], w=[kgb])
                    A.put((yv, kyv))
                    gl.append(None)
                else:
                    y2, ky2 = A.get()
                    tt('pool', y2[:, :], yv[:, :], yv[:, :], ALU.mult, r=[kyv], w=[ky2])
                    ts('dve', y2[:, :], y2[:, :], 0.044715, 1.0, ALU.mult, ALU.add, r=[ky2], w=[ky2])
                    tt('pool', y2[:, :], y2[:, :], yv[:, :], ALU.mult, r=[ky2, kyv], w=[ky2])
                    act(y2[:, :], y2[:, :], AF.Tanh, r=[ky2], w=[ky2], scale=0.7978845608028654)
                    stt('dve', yv[:, :], y2[:, :], 1.0, yv[:, :], ALU.add, ALU.mult, r=[ky2, kyv], w=[kyv])
                    ts('dve', gb[:, :], yv[:, :], 0.5, None, ALU.mult, None, r=[kyv], w=[kgb])
                    A.put((y2, ky2))
                    gl.append((yv, kyv))
                gelb.append((gb, kgb))
            yield
            for m2 in range(2):
                ps, kp = PS.get()
                for m in range(2):
                    pe_mm(ps[:, 0:TB], gwbf[:, m * 256 + m2 * 128: m * 256 + (m2 + 1) * 128], gelb[m][0][:, :],
                          m == 0, m == 1, r=['gwbf', gelb[m][1]], w=[kp])
                th, kth = A.get()
                act(th[:, :], ps[:, 0:TB], AF.Tanh, r=[kp, 'hgb'], w=[kth], bias=hgb[:, m2:m2 + 1], scale=0.5)
                PS.put((ps, kp))
                if GELU_ACT:
                    ts('dve', th[:, :], th[:, :], 0.5, 0.5, ALU.mult, ALU.add, r=[kth], w=[kth])
                    tt('dve', yT[6 + m2][0][:, :], gelb[m2][0][:, :], th[:, :], ALU.mult, r=[gelb[m2][1], kth],
                       w=[yT[6 + m2][1]])
                else:
                    ts('dve', th[:, :], th[:, :], 0.25, 0.25, ALU.mult, ALU.add, r=[kth], w=[kth])
                    tt('dve', yT[6 + m2][0][:, :], gl[m2][0][:, :], th[:, :], ALU.mult, r=[gl[m2][1], kth],
                       w=[yT[6 + m2][1]])
                A.put((th, kth))
            for m in range(2):
                if gl[m] is not None:
                    A.put(gl[m])
                B.put(gelb[m], ub[m])

            yield

        if mid_hook[0] is not None:
            mid_hook[0]()
        gens = [(ssd_gen(), 1), (s5_gen(), 1)]
        alive = list(gens)
        while alive:
            for g in list(alive):
                for _ in range(g[1]):
                    try:
                        next(g[0])
                    except StopIteration:
                        alive.remove(g)
                        break
        if STAGE <= 6:
            raise _Stop()
        if DEBUG and l == 0:
            for k in range(8):
                a, ka = A.get()
                cp('dve', a[:, :], yT[k][0][:, :], r=[yT[k][1]], w=[ka])
                dma(dbgd[b * 16 + k], a[:, :], r=[ka], w=[('dbg', b, k)])
                A.put((a, ka))
        rr = []
        for n in range(8):
            slot, kslot = wp_(8 if n < 4 else 9)
            off = (n % 4) * 1024
            ps, kp = PS.get()
            for k in range(8):
                pe_mm(ps[:, 0:TB], slot[:, off + k * 128: off + (k + 1) * 128], yT[k][0][:, :], k == 0, k == 7,
                      r=[kslot, yT[k][1]], w=[kp])
            r_, kr = A.get()
            stt('dve', r_[:, :], hbf[:, n, c0:c0 + TB], ALPHA, ps[:, 0:TB], ALU.mult, ALU.add, r=[hk(b, n), kp], w=[kr])
            PS.put((ps, kp))
            rr.append((r_, kr))
        for x in yT:
            B.put(x)
        layernorm(rr, 'ln1g', 'ln1b', lambda n: (hbf[:, n, c0:c0 + TB], [hk(b, n)], None))
        for x in rr:
            A.put(x)
        if DEBUG and l == 0:
            for k in range(8):
                a, ka = A.get()
                cp('dve', a[:, :], hbf[:, k, c0:c0 + TB], r=[hk(b, k)], w=[ka])
                dma(dbgd[b * 16 + 8 + k], a[:, :], r=[ka], w=[('dbg', b, 8 + k)])
                A.put((a, ka))

    def ffn_block(l, b, last):
        if STAGE <= 7:
            raise _Stop()
        c0 = b * TB
        hid = []
        for hg in range(8):
            wp = get_piece(l, 10 + hg)
            for hc in range(4):
                ps, kp = fm_proj(b, wp[0], wp[1], hc * 1024)
                rl, krl = B.get()
                if (hg * 4 + hc) % 2 == 0:
                    act(rl[:, :], ps[:, 0:TB], AF.Relu, r=[kp], w=[krl])
                else:
                    ts('dve', rl[:, :], ps[:, 0:TB], 0.0, None, ALU.max, None, r=[kp], w=[krl])
                PS.put((ps, kp))
                hd, khd = B.get()
                tt('pool', hd[:, :], rl[:, :], rl[:, :], ALU.mult, r=[krl], w=[khd])
                B.put((rl, krl))
                hid.append((hd, khd))
            yield
        rr = []
        for n in range(8):
            wp = get_piece(l, 18 + n)
            ps, kp = PS.get()
            for j in range(32):
                pe_mm(ps[:, 0:TB], wp[0][:, j * 128:(j + 1) * 128], hid[j][0][:, :], j == 0, j == 31,
                      r=[wp[1], hid[j][1]], w=[kp])
            r_, kr = A.get()
            stt('dve', r_[:, :], hbf[:, n, c0:c0 + TB], ALPHA, ps[:, 0:TB], ALU.mult, ALU.add, r=[hk(b, n), kp], w=[kr])
            PS.put((ps, kp))
            rr.append((r_, kr))
        for x in hid:
            B.put(x)
        def ln_fn():
            if not last:
                layernorm(rr, 'ln2g', 'ln2b', lambda n: (hbf[:, n, c0:c0 + TB], [hk(b, n)], None))
            else:
                def dst(n):
                    o, ko = A.get()

                    def post():
                        dma(outd[:, n, c0:c0 + TB], o[:, :], r=[ko], w=[('out', b, n)])
                        A.put((o, ko))
                    return o[:, :], [ko], post
                layernorm(rr, 'ln2g', 'ln2b', dst)
            for x in rr:
                A.put(x)
        ffn_pend.append(ln_fn)

    ffn_pend = []
    mid_hook = [None]

    def whole():
        for l in range(DEPTH):
            flush_convert()
            layer_setup(l)
            if STAGE <= 0:
                raise _Stop()
            nconv = NPIECE * 4
            steps = 2 * NBLK
            done = 0
            dn = [done]

            def conv_hook(l=l, dn=dn):
                b = conv_hook.b
                if l == 0:
                    lo = 40 + (64 * b) // NBLK
                    hi = 40 + (64 * (b + 1)) // NBLK
                    for q in range(lo, hi):
                        convert_unit(0, q)
                if l + 1 < DEPTH:
                    tgt = (nconv * (b + 1)) // steps
                    for q in range(dn[0], tgt):
                        convert_unit(l + 1, q)
                    dn[0] = tgt
            for b in range(NBLK):
                conv_hook.b = b
                mid_hook[0] = conv_hook
                mixer_block(l, b)
                mid_hook[0] = None
                if l == 0 and b == NBLK - 1:
                    flush_convert()
            done = dn[0]
            for b in range(NBLK):
                g_ffn = ffn_block(l, b, l == DEPTH - 1)
                for _ in range(FFN_DEFER):
                    next(g_ffn)
                if len(ffn_pend) > 0:
                    ffn_pend.pop(0)()
                for _ in g_ffn:
                    pass
                if l + 1 < DEPTH:
                    tgt = (nconv * (NBLK + b + 1)) // steps
                    for q in range(done, tgt):
                        convert_unit(l + 1, q)
                    done = tgt
            while ffn_pend:
                ffn_pend.pop(0)()
    try:
        whole()
    except _Stop:
        A.free = [(t, ('A', i)) for i, t in enumerate(Apool)]
        for b in range(NBLK):
            for k in range(8):
                a, ka = A.get()
                cp('dve', a[:, :], hbf[:, k, b * TB:(b + 1) * TB], r=[hk(b, k)], w=[ka])
                dma(outd[:, k, b * TB:(b + 1) * TB], a[:, :], r=[ka], w=[('out', b, k)])
                A.put((a, ka))

    ops = P.ops
    for o in ops:
        for j in o['deps']:
            ops[j]['sig'] = True
    engs = ['pe', 'act', 'dve', 'pool', 'sp']
    cnt = {e: 0 for e in engs}
    ndma = 0
    for o in ops:
        if o['dma']:
            o['dsem'] = ndma % ND
            o['dval'] = 16 * (ndma // ND + 1)
            ndma += 1
        elif o['sig']:
            cnt[o['eng']] += 1
            o['cnt'] = cnt[o['eng']]
    print("ops", len(ops), "sig counts", cnt, "dmas", ndma, "minfree A/B/PS", A.minfree, B.minfree, PS.minfree)
    esem = {e: es.enter_context(nc.semaphore("sem_" + e)) for e in engs if e != 'sp'}
    dsem = [es.enter_context(nc.semaphore(f"dsem{i}")) for i in range(ND)]
    out_dmas = [o for o in ops if o['dma']]

    def emit(ename, eng):
        waited = {}
        for o in ops:
            if o['eng'] != ename:
                continue
            need = {}
            for j in o['deps']:
                oj = ops[j]
                if oj['dma']:
                    key = ('d', oj['dsem'])
                    val = oj['dval']
                else:
                    key = ('e', oj['eng'])
                    val = oj['cnt']
                if val > need.get(key, 0):
                    need[key] = val
            if o['dma'] and o['dval'] > 16:
                key = ('d', o['dsem'])
                need[key] = max(need.get(key, 0), o['dval'] - 16)
            for key, val in need.items():
                if val > waited.get(key, 0):
                    sem = dsem[key[1]] if key[0] == 'd' else esem[key[1]]
                    eng.wait_ge(sem, val)
                    waited[key] = val
            ins = o['fn'](eng)
            if o['dma']:
                ins.then_inc(dsem[o['dsem']], 16)
            elif o['sig']:
                ins.then_inc(esem[ename], 1)
        if ename == 'sp':
            last = {}
            for o in out_dmas:
                last[o['dsem']] = max(last.get(o['dsem'], 0), o['dval'])
            for s, v in last.items():
                eng.wait_ge(dsem[s], v)

    with nc.Block() as block:
        @block.tensor
        def _(e):
            emit('pe', e)

        @block.scalar
        def _(e):
            emit('act', e)

        @block.vector
        def _(e):
            emit('dve', e)

        @block.gpsimd
        def _(e):
            emit('pool', e)

        @block.sync
        def _(e):
            emit('sp', e)
    es.close()
    return nc


SPL = np.cumsum([0, 384, 384, 384, 384, 384, 896, 6, 256])


def _fmchunk(W, c0):
    return W[:, c0:c0 + 128].reshape(8, 128, 128).transpose(1, 0, 2).reshape(128, 1024)


def _tmgroup(W, cols):
    n = len(cols)
    return W[:, cols].reshape(8, 128, n).transpose(1, 0, 2).reshape(128, 8 * n)


def _pad(a):
    o = np.zeros((128, 4096), np.float32)
    o[:, :a.shape[1]] = a
    return o


def prep_shared(inp, DEPTH):
    f = np.float32
    ws, pvs, pms = [], [], []
    q0, f0, i0, g0, z0, x0, d0, u0 = [int(v) for v in SPL[:8]]
    for l in range(DEPTH):
        W = np.asarray(inp['w_in'][l], f)
        pcs = []
        pcs.append(np.concatenate([_fmchunk(W, q0), _fmchunk(W, q0 + 128), _fmchunk(W, q0 + 256), _fmchunk(W, f0)], 1))
        pcs.append(np.concatenate([_fmchunk(W, f0 + 128), _fmchunk(W, f0 + 256)], 1))
        pcs.append(_tmgroup(W, np.arange(i0, i0 + 384)))
        pcs.append(_tmgroup(W, np.arange(g0, g0 + 384)))
        pcs.append(np.concatenate([_fmchunk(W, x0 + 128 * c) for c in range(4)], 1))
        pcs.append(np.concatenate([_fmchunk(W, x0 + 128 * c) for c in range(4, 7)], 1))
        pcs.append(_tmgroup(W, np.concatenate([np.arange(z0, z0 + 384), np.arange(d0, d0 + 6)])))
        pcs.append(np.concatenate([_fmchunk(W, u0), _fmchunk(W, u0 + 128)], 1))
        Wo = np.asarray(inp['w_out'][l], f)
        pcs.append(np.concatenate([_fmchunk(Wo, 128 * n) for n in range(4)], 1))
        pcs.append(np.concatenate([_fmchunk(Wo, 128 * n) for n in range(4, 8)], 1))
        W1 = np.asarray(inp['w_mlp_in'][l], f)
        for hg in range(8):
            pcs.append(np.concatenate([_fmchunk(W1, 128 * (4 * hg + hc)) for hc in range(4)], 1))
        W2 = np.asarray(inp['w_mlp_out'][l], f)
        for n in range(8):
            pcs.append(W2[:, n * 128:(n + 1) * 128].reshape(32, 128, 128).transpose(1, 0, 2).reshape(128, 4096))
        assert len(pcs) == NPIECE
        ws.append(np.stack([_pad(p) for p in pcs]))
        pvl = np.zeros((128, NPV), f)
        cw = np.asarray(inp['m2_conv_w'][l], f)
        pvl[:, 0:28] = cw.reshape(4, 7, 128).transpose(2, 1, 0).reshape(128, 28)
        pvl[:, 28:35] = np.asarray(inp['m2_conv_b'][l], f).reshape(7, 128).T
        for nm, o in (('ln1_g', 35), ('ln1_b', 43), ('ln2_g', 51), ('ln2_b', 59)):
            pvl[:, o:o + 8] = np.asarray(inp[nm][l], f).reshape(8, 128).T
        pvl[:, 67:69] = np.asarray(inp['s5_glu_b'][l], f).reshape(2, 128).T
        pvl[:, 69:71] = np.asarray(inp['s5_d'][l], f).reshape(2, 128).T
        pvl[:, 71:79] = np.asarray(inp['s5_a_re'][l], f).reshape(1024).reshape(8, 128).T
        pvl[:, 79:87] = np.asarray(inp['s5_a_im'][l], f).reshape(1024).reshape(8, 128).T
        pvl[:, 87:95] = np.repeat(np.asarray(inp['s5_log_dt'][l], f), 64).reshape(8, 128).T
        pvl[:, 95:101] = np.asarray(inp['m2_dt_bias'][l], f)[None, :]
        pvl[:, 101:107] = np.asarray(inp['m2_a_log'][l], f)[None, :]
        pvl[:, 107:113] = np.asarray(inp['m2_d'][l], f)[None, :]
        pvl[:, 113:116] = np.tile(np.asarray(inp['hgrn_norm_w'][l], f), 6).reshape(3, 128).T
        pvl[:, 116:119] = np.asarray(inp['m2_norm_w'][l], f).reshape(3, 128).T
        pvl[:, 119:122] = np.repeat(np.asarray(inp['m2_d'][l], f), 64).reshape(3, 128).T
        pvs.append(pvl)
        pm = np.zeros((5, 128, 1024), f)
        for ri, nm in enumerate(('s5_b_re', 's5_b_im')):
            bb = np.asarray(inp[nm][l], f)
            for g in range(16):
                pm[ri, (g % 8) * 16:(g % 8) * 16 + 16, g * 64:(g + 1) * 64] = bb[g].T
        for ri, nm in enumerate(('s5_c_re', 's5_c_im')):
            cc = np.asarray(inp[nm][l], f)
            ct = pm[2 + ri].reshape(128, 8, 128)
            for g in range(16):
                j, ph = g // 2, (g % 2) * 64
                ct[ph:ph + 64, j, (g % 8) * 16:(g % 8) * 16 + 16] = cc[g].T
        gw = np.asarray(inp['s5_glu_w'][l], f)
        pm[4, :, 0:512] = gw.reshape(2, 128, 256).transpose(1, 0, 2).reshape(128, 512)
        pms.append(pm)
    wsrc = np.concatenate(ws, 0).reshape(DEPTH * NPIECE, 128, 4, 1024).transpose(0, 2, 1, 3)
    wsrc = np.ascontiguousarray(wsrc).reshape(DEPTH * NPIECE * 4, 128, 1024)
    lbl = np.asarray(inp['hgrn_lb_logits'], f).reshape(4, 3, 128).transpose(2, 1, 0).reshape(128, 12)
    s = np.arange(128)
    tri = (s[:, None] <= s[None, :]).astype(f)
    tau = np.arange(384)
    rst = np.tile(((tau % 64) != 0).astype(f)[None, :], (128, 1))
    ramp = np.tile((tau + 1).astype(f)[None, :], (128, 1))
    cfv = np.concatenate([tri, rst, ramp, np.eye(128, dtype=f)], 1)
    negm = np.where(s[:, None] <= s[None, :], 0.0, -30000.0).astype(f)
    mask2 = ((s[:, None] <= s[None, :]) & ((s[:, None] // 64) == (s[None, :] // 64))).astype(f)
    cbv = np.concatenate([np.eye(128, dtype=f), negm, mask2], 1)
    return dict(wsrc=wsrc, pv=np.stack(pvs), pm=np.concatenate(pms, 0), lbl=np.ascontiguousarray(lbl),
                cf=np.ascontiguousarray(cfv), cb=np.ascontiguousarray(cbv))


def run(inp, NBLK, DEPTH, ncores):
    x = np.asarray(inp['x'], np.float32)
    meta = np.asarray(inp['meta_tokens'], np.float32)
    LT = NBLK * TB
    shared = prep_shared(inp, DEPTH)
    in_maps = []
    for b in range(ncores):
        hf = np.zeros((LT, 1024), np.float32)
        hf[0:16] = meta
        hf[16:16 + x.shape[1]] = x[b]
        h0 = np.ascontiguousarray(hf.T.reshape(8, 128, LT).transpose(1, 0, 2))
        m = dict(shared)
        m['h0'] = h0
        in_maps.append(m)
    nc = build(NBLK, DEPTH)
    res = run_bass_kernel_spmd(nc, in_maps, core_ids=list(range(ncores)))
    outs = []
    for b in range(ncores):
        o = np.asarray(res.results[b]['out'], np.float32)
        hf = o.transpose(1, 0, 2).reshape(1024, LT).T
        outs.append(hf[16:16 + x.shape[1]])
    return np.stack(outs).astype(np.float32)


def kernel(**inputs):
    return run(inputs, 11, 4, 4)
```

```python
import math
from contextlib import ExitStack
import numpy as np
import concourse.bass as bass
import concourse.mybir as mybir
from concourse.bass_utils import run_bass_kernel_spmd

F32 = mybir.dt.float32
BF16 = mybir.dt.bfloat16
AF = mybir.ActivationFunctionType
ALU = mybir.AluOpType
AX = mybir.AxisListType

TB = 384
NT = 3
NPIECE = 26
ALPHA = 8.0 ** 0.25
LN_EPS = 1e-5
RMS_EPS = 1e-6
PI = math.pi
NPV = 122
NA = 16
NB = 40
NWS = 3
ND = 24


class Prog:
    def __init__(self):
        self.ops = []
        self.lastw = {}
        self.rd = {}

    def add(self, eng, fn, r=(), w=(), dma=False):
        i = len(self.ops)
        raw, other = set(), set()
        for b in r:
            j = self.lastw.get(b)
            if j is not None:
                raw.add(j)
        for b in w:
            j = self.lastw.get(b)
            if j is not None:
                other.add(j)
            for j in self.rd.get(b, {}).values():
                if isinstance(j, list):
                    other.update(j)
                else:
                    other.add(j)
        deps = set()
        for j in raw | other:
            oj = self.ops[j]
            if oj['dma']:
                deps.add(j)
            elif oj['eng'] == eng and not dma:
                if eng == 'pe':
                    continue
                deps.add(j)
            else:
                deps.add(j)
        for b in r:
            d = self.rd.setdefault(b, {})
            if dma:
                d.setdefault('dma', []).append(i)
            else:
                d[eng] = i
        for b in w:
            self.lastw[b] = i
            self.rd[b] = {}
        self.ops.append(dict(eng=eng, fn=fn, deps=deps, dma=dma, sig=False))
        return i


class _Stop(Exception):
    pass


STAGE = 99
FFN_DEFER = 2
HG_LAG = 2
GELU_ACT = True
VAR = 0
DEBUG = False
TCUT = 0


def build(NBLK, DEPTH):
    LT = NBLK * TB
    nc = bass.Bass("TRN2", target_bir_lowering=False)
    P = Prog()
    es = ExitStack()

    def dram(name, shape, dt, kind):
        return nc.dram_tensor(name, shape, dt, kind=kind).ap()

    h0 = dram("h0", [128, 8, LT], F32, "ExternalInput")
    wsrc = dram("wsrc", [DEPTH * NPIECE * 4, 128, 1024], F32, "ExternalInput")
    pvd = dram("pv", [DEPTH, 128, NPV], F32, "ExternalInput")
    pmd = dram("pm", [DEPTH * 5, 128, 1024], F32, "ExternalInput")
    lbd = dram("lbl", [128, 12], F32, "ExternalInput")
    cfd = dram("cf", [128, 1024], F32, "ExternalInput")
    cbd = dram("cb", [128, 384], F32, "ExternalInput")
    outd = dram("out", [128, 8, LT], F32, "ExternalOutput")
    wbf = dram("wbf", [DEPTH * NPIECE, 128, 4096], BF16, "Internal")
    dbgd = dram("dbg", [NBLK * 16, 128, TB], F32, "ExternalOutput") if DEBUG else None

    def sb(name, shape, dt):
        return es.enter_context(nc.sbuf_tensor(name, shape, dt))

    hbf = sb("hbf", [128, 8, LT], BF16)
    wslot = [sb(f"wslot{i}", [128, 4096], BF16) for i in range(NWS)]
    Fre = sb("Fre", [128, 8, TB], BF16)
    nFim = sb("nFim", [128, 8, TB], BF16)
    cvf = [sb(f"cvf{i}", [128, 1024], F32) for i in range(2)]
    cvb = [sb(f"cvb{i}", [128, 1024], BF16) for i in range(2)]
    pv = sb("pvs", [128, NPV], F32)
    btbf = sb("btbf", [128, 2, 1024], BF16)
    ctbf = sb("ctbf", [128, 3, 1024], BF16)
    gwbf = sb("gwbf", [128, 512], BF16)
    cf = sb("cfs", [128, 1024], F32)
    cb = sb("cbs", [128, 384], BF16)
    onesb = sb("onesb", [128, 128], BF16)
    onesf = sb("onesf", [128, 128], F32)
    lbt = sb("lbt", [128, 3, 4], F32)
    lbw = sb("lbw", [128, 3, 4], F32)
    lba = sb("lba", [128, 3, DEPTH, 3], F32)
    s5s = sb("s5s", [128, 16, 8], F32)
    s5c = sb("s5c", [128, 2, 8], F32)
    hS = [sb(f"hS{i}", [128, 3, 64], F32) for i in range(3)]
    hSb = [sb(f"hSb{i}", [128, 3, 64], BF16) for i in range(3)]
    egl = sb("egl", [128, 3, 8], F32)
    Sm = sb("Sm", [128, 6, 64], F32)
    Smb = sb("Smb", [128, 6, 64], BF16)
    XB = sb("XB", [128, 7, 388], BF16)
    dgw = sb("dgw", [128, 28, 128], BF16)
    dgd = sb("dgd", [128, 3, 128], BF16)
    sm = sb("smalls", [128, 16, 16], F32)
    abc = sb("abc", [128, 8], F32)
    dtt = sb("dtt", [128, 3, 16], F32)
    acs = sb("acs", [128, 3, 48], F32)
    hgb = sb("hgb", [128, 2], F32)
    pib = sb("pib", [128, 2], F32)
    mcol = sb("mcol", [128, 2], F32)
    nident = sb("nident", [128, 128], BF16)
    negmf = sb("negmf", [128, 128], F32)
    Apool = [sb(f"A{i}", [128, TB], F32) for i in range(NA)]
    Bpool = [sb(f"B{i}", [128, TB], BF16) for i in range(NB)]
    print("sbuf bytes remaining", nc.sbuf_bytes_remaining)
    psum = [es.enter_context(nc.psum_tensor(f"ps{i}", [128, 512], F32)) for i in range(8)]

    tri = cf[:, 0:128]
    rstm = cf[:, 128:512]
    ramp = cf[:, 512:896]
    identf = cf[:, 896:1024]
    ident = cb[:, 0:128]
    negm = cb[:, 128:256]
    mask2 = cb[:, 256:384]

    class Pool:
        def __init__(self, tiles, nm):
            self.free = [(t, (nm, i)) for i, t in enumerate(tiles)]
            self.nm = nm

        def get(self):
            if not self.free:
                raise RuntimeError("pool empty " + self.nm)
            it = self.free.pop(0)
            self.minfree = min(getattr(self, 'minfree', 999), len(self.free))
            return it

        def put(self, *its):
            for it in its:
                self.free.append(it)

    A = Pool(Apool, 'A')
    B = Pool(Bpool, 'B')
    PS = Pool(psum, 'ps')
    smi = [0]

    def small():
        i = smi[0] % 16
        smi[0] += 1
        return sm[:, i, :], ('sm', i)

    def pe_mm(out, lhsT, rhs, start, stop, r, w):
        P.add('pe', lambda e: e.matmul(out, lhsT, rhs, start=start, stop=stop), r=r, w=w)

    def pe_tr(out, in_, r, w):
        P.add('pe', lambda e: e.transpose(out, in_, ident), r=list(r) + ['cb'], w=w)

    def act(out, in_, func, r, w, bias=0.0, scale=1.0):
        P.add('act', lambda e: e.activation(out, in_, func, bias=bias, scale=scale), r=r, w=w)

    def tt(eng, out, a, b, op, r, w):
        P.add(eng, lambda e: e.tensor_tensor(out, a, b, op), r=r, w=w)

    def ts(eng, out, a, s1, s2, op0, op1, r, w):
        if op1 is None:
            P.add(eng, lambda e: e.tensor_single_scalar(out, a, s1, op0), r=r, w=w)
        else:
            P.add(eng, lambda e: e.tensor_scalar(out, a, s1, s2, op0, op1), r=r, w=w)

    def stt(eng, out, a, s, b, op0, op1, r, w):
        P.add(eng, lambda e: e.scalar_tensor_tensor(out, a, s, b, op0, op1), r=r, w=w)

    def cp(eng, out, in_, r, w):
        if eng == 'act':
            act(out, in_, AF.Copy, r, w)
        else:
            P.add(eng, lambda e: e.tensor_copy(out, in_), r=r, w=w)

    def dma(out, in_, r, w):
        P.add('sp', lambda e: e.dma_start(out=out, in_=in_), r=r, w=w, dma=True)

    dbg_names = []

    def dump(name, ap, key, n=TB):
        if not DEBUG:
            return
        dt_ = dram("dbg_" + name, [128, n], F32, "ExternalOutput")
        dbg_names.append("dbg_" + name)
        a, ka = A.get()
        cp('dve', a[:, 0:n], ap, r=[key], w=[ka])
        dma(dt_[:, :], a[:, 0:n], r=[ka], w=[('dbgx', name)])
        A.put((a, ka))

    def memset(eng, ap, val, w):
        P.add(eng, lambda e: e.memset(ap, val), r=(), w=w)

    cvi = [0]

    pending = []

    def flush_convert():
        while pending:
            pending.pop(0)()

    def conv_load(l, q):
        i = cvi[0] % 2
        eng = ['act', 'dve'][cvi[0] % 2]
        cvi[0] += 1
        dma(cvf[i][:, :], wsrc[(l * NPIECE * 4 + q)], r=[], w=[('cvf', i)])
        return (l, q, i, eng)

    def conv_finish(u):
        l, q, i, eng = u
        pi, qq = q // 4, q % 4
        cp(eng, cvb[i][:, :], cvf[i][:, :], r=[('cvf', i)], w=[('cvb', i)])
        while pending:
            pending.pop(0)()
        pending.append(lambda: dma(wbf[l * NPIECE + pi][:, qq * 1024:(qq + 1) * 1024], cvb[i][:, :],
                                   r=[('cvb', i)], w=[('wbf', l, pi, qq)]))

    def convert_unit(l, q):
        conv_finish(conv_load(l, q))

    conv_queue = []
    conv_inflight = []

    def conv_pump():
        if conv_inflight:
            conv_finish(conv_inflight.pop(0))
        if conv_queue:
            conv_inflight.append(conv_load(*conv_queue.pop(0)))

    def conv_drain():
        while conv_queue or conv_inflight:
            conv_pump()

    sched = []
    for l in range(DEPTH):
        for b in range(NBLK):
            sched += [(l, pi) for pi in range(10)]
        for b in range(NBLK):
            sched += [(l, pi) for pi in range(10, 26)]
    st = dict(next_load=0, next_use=0)

    def issue_loads(upto):
        while st['next_load'] <= min(upto, len(sched) - 1):
            k = st['next_load']
            l, pi = sched[k]
            s = k % NWS
            dma(wslot[s][:, :], wbf[l * NPIECE + pi], r=[('wbf', l, pi, qq) for qq in range(4)], w=[('w', s)])
            st['next_load'] += 1

    def get_piece(l, pi):
        k = st['next_use']
        assert sched[k] == (l, pi), (sched[k], l, pi)
        issue_loads(k + NWS - 1)
        st['next_use'] += 1
        return wslot[k % NWS], ('w', k % NWS)

    dma(cf[:, :], cfd[:, :], r=[], w=['cf'])
    dma(cvf[0][:, 0:384], cbd[:, :], r=[], w=[('cvf', 0)])
    cp('dve', cb[:, :], cvf[0][:, 0:384], r=[('cvf', 0)], w=['cb'])
    cp('dve', negmf[:, :], cvf[0][:, 128:256], r=[('cvf', 0)], w=['negmf'])
    ts('dve', nident[:, :], ident, -1.0, None, ALU.mult, None, r=['cb'], w=['nident'])
    memset('pool', onesb[:, :], 1.0, w=['onesb'])
    memset('pool', onesf[:, :], 1.0, w=['onesf'])
    memset('pool', pib[:, 0:1], PI, w=['pib'])
    memset('pool', mcol[0:64, 0:1], 1.0, w=['mcol'])
    memset('pool', mcol[64:128, 0:1], 0.0, w=['mcol'])
    memset('pool', mcol[0:64, 1:2], 0.0, w=['mcol'])
    memset('pool', mcol[64:128, 1:2], 1.0, w=['mcol'])
    memset('pool', pib[:, 1:2], -PI, w=['pib'])
    dma(lbt[:, :, :], lbd.rearrange("p (a b) -> p a b", a=3), r=[], w=['lbt'])
    act(lbw[:, :, :], lbt[:, :, :], AF.Exp, r=['lbt'], w=['lbw'])
    ssum, ksum = small()
    P.add('dve', lambda e: e.tensor_reduce(ssum[:, 0:3], lbw[:, :, :], AX.X, ALU.add), r=['lbw'], w=[ksum])
    rs, krs = small()
    P.add('dve', lambda e: e.reciprocal(rs[:, 0:3], ssum[:, 0:3]), r=[ksum], w=[krs])
    tt('dve', lbt[:, :, :], lbw[:, :, :], rs[:, 0:3].unsqueeze(2).broadcast_to([128, 3, 4]), ALU.mult,
       r=['lbw', krs], w=['lbt'])
    memset('dve', lbw[:, :, 0:1], 0.0, w=['lbw'])
    for l in range(1, 4):
        tt('dve', lbw[:, :, l:l + 1], lbw[:, :, l - 1:l], lbt[:, :, l:l + 1], ALU.add, r=['lbw', 'lbt'], w=['lbw'])
    for l in range(DEPTH):
        ts('dve', lba[:, :, l, 0:1], lbw[:, :, l:l + 1], 0.5, 0.5, ALU.mult, ALU.add, r=['lbw'], w=['lba'])
        ts('dve', lba[:, :, l, 1:2], lbw[:, :, l:l + 1], -0.5, 0.5, ALU.mult, ALU.add, r=['lbw'], w=['lba'])
        ts('dve', lba[:, :, l, 2:3], lbw[:, :, l:l + 1], 0.5, -0.5, ALU.mult, ALU.add, r=['lbw'], w=['lba'])

    for b in range(NBLK):
        for k in range(8):
            a, ka = A.get()
            dma(a[:, :], h0[:, k, b * TB:(b + 1) * TB], r=[], w=[ka])
            cp(['act', 'dve', 'pool'][k % 3], hbf[:, k, b * TB:(b + 1) * TB], a[:, :], r=[ka], w=[('h', b, k)])
            A.put((a, ka))

    for q in range(40):
        convert_unit(0, q)
    flush_convert()

    OFF = dict(convw=0, convb=28, ln1g=35, ln1b=43, ln2g=51, ln2b=59, glub=67, s5d=69, are=71, aim=79, ldt=87,
               dtb=95, alog=101, md=107, hnw=113, mnw=116, mdc=119)

    def pvc(name, i=0, n=1):
        o = OFF[name] + i
        return pv[:, o:o + n]

    def layer_setup(l):
        dma(pv[:, :], pvd[l], r=[], w=['pv'])
        for m in (0, 1, 4):
            i = cvi[0] % 2
            cvi[0] += 1
            dma(cvf[i][:, :], pmd[l * 5 + m], r=[], w=[('cvf', i)])
            if m < 2:
                cp('dve', btbf[:, m, :], cvf[i][:, :], r=[('cvf', i)], w=['btbf'])
            else:
                cp('dve', gwbf[:, :], cvf[i][:, 0:512], r=[('cvf', i)], w=['gwbf'])
        S = lambda i: s5s[:, i, :]
        ts('dve', S(0), pvc('are', 0, 8), -1e-4, None, ALU.min, None, r=['pv'], w=['s5s'])
        act(S(1), pvc('ldt', 0, 8), AF.Exp, r=['pv'], w=['s5s'])
        tt('dve', S(2), S(0), S(1), ALU.mult, r=['s5s'], w=['s5s'])
        act(S(3), S(2), AF.Exp, r=['s5s'], w=['s5s'])
        tt('dve', S(4), pvc('aim', 0, 8), S(1), ALU.mult, r=['pv', 's5s'], w=['s5s'])
        C1 = 6.28125
        C2 = 2 * PI - 6.28125
        I32 = mybir.dt.int32

        def reduce_to_pi(src, ksrc):
            qi, kqi = A.get()
            kf, kkf = A.get()
            ph, kph = A.get()
            ts('dve', qi[:, :].bitcast(I32), src[:, :], 1.0 / (2 * PI), None, ALU.mult, None, r=[ksrc], w=[kqi])
            cp('dve', kf[:, :], qi[:, :].bitcast(I32), r=[kqi], w=[kkf])
            stt('dve', ph[:, :], kf[:, :], -C1, src[:, :], ALU.mult, ALU.add, r=[kkf, ksrc], w=[kph])
            stt('dve', ph[:, :], kf[:, :], -C2, ph[:, :], ALU.mult, ALU.add, r=[kkf, kph], w=[kph])
            ts('dve', qi[:, :], ph[:, :], PI, 2 * PI, ALU.is_gt, ALU.mult, r=[kph], w=[kqi])
            tt('dve', ph[:, :], ph[:, :], qi[:, :], ALU.subtract, r=[kph, kqi], w=[kph])
            ts('dve', qi[:, :], ph[:, :], -PI, 2 * PI, ALU.is_lt, ALU.mult, r=[kph], w=[kqi])
            tt('dve', ph[:, :], ph[:, :], qi[:, :], ALU.add, r=[kph, kqi], w=[kph])
            A.put((qi, kqi), (kf, kkf))
            return ph, kph

        for j in range(8):
            a1, k1 = A.get()
            a2, k2 = A.get()
            ts('dve', a1[:, :], ramp, s5s[:, 4, j:j + 1], None, ALU.mult, None, r=['cf', 's5s'], w=[k1])
            ts('dve', a2[:, :], a1[:, :], 0.5 * PI, None, ALU.add, None, r=[k1], w=[k2])
            ph, kph = reduce_to_pi(a1, k1)
            act(nFim[:, j, :], ph[:, :], AF.Sin, r=[kph], w=[('nF', j)], scale=-1.0)
            A.put((ph, kph))
            ph, kph = reduce_to_pi(a2, k2)
            act(Fre[:, j, :], ph[:, :], AF.Sin, r=[kph], w=[('Fr', j)])
            A.put((ph, kph))
            A.put((a1, k1), (a2, k2))
        allF = [('Fr', j) for j in range(8)] + [('nF', j) for j in range(8)]
        tt('dve', S(5), S(3), Fre[:, :, 0], ALU.mult, r=['s5s'] + allF, w=['s5s'])
        tt('dve', S(6), S(3), nFim[:, :, 0], ALU.mult, r=['s5s'] + allF, w=['s5s'])
        tt('dve', S(7), S(0), S(0), ALU.mult, r=['s5s'], w=['s5s'])
        tt('dve', S(8), pvc('aim', 0, 8), pvc('aim', 0, 8), ALU.mult, r=['pv'], w=['s5s'])
        tt('dve', S(7), S(7), S(8), ALU.add, r=['s5s'], w=['s5s'])
        P.add('dve', lambda e: e.reciprocal(S(8), S(7)), r=['s5s'], w=['s5s'])
        ts('dve', S(9), S(5), -1.0, None, ALU.add, None, r=['s5s'], w=['s5s'])
        tt('dve', S(10), S(9), S(0), ALU.mult, r=['s5s'], w=['s5s'])
        tt('dve', S(11), S(6), pvc('aim', 0, 8), ALU.mult, r=['s5s', 'pv'], w=['s5s'])
        tt('dve', S(10), S(10), S(11), ALU.subtract, r=['s5s'], w=['s5s'])
        tt('dve', S(10), S(10), S(8), ALU.mult, r=['s5s'], w=['s5s'])
        tt('dve', S(11), S(6), S(0), ALU.mult, r=['s5s'], w=['s5s'])
        tt('dve', S(12), S(9), pvc('aim', 0, 8), ALU.mult, r=['s5s', 'pv'], w=['s5s'])
        tt('dve', S(11), S(11), S(12), ALU.add, r=['s5s'], w=['s5s'])
        tt('dve', S(11), S(11), S(8), ALU.mult, r=['s5s'], w=['s5s'])
        ts('dve', S(11), S(11), -1.0, None, ALU.mult, None, r=['s5s'], w=['s5s'])
        ire = cvi[0] % 2
        iim = 1 - ire
        cvi[0] += 2
        dma(cvf[ire][:, :], pmd[l * 5 + 2], r=[], w=[('cvf', ire)])
        dma(cvf[iim][:, :], pmd[l * 5 + 3], r=[], w=[('cvf', iim)])
        for j0, j1 in ((0, 3), (3, 6), (6, 8)):
            n = j1 - j0

            def g3(ap):
                return ap.rearrange("p (j c) -> p j c", j=n)
            sre = s5s[:, 10, j0:j1].unsqueeze(2).broadcast_to([128, n, 128])
            sim = s5s[:, 11, j0:j1].unsqueeze(2).broadcast_to([128, n, 128])
            cre = g3(cvf[ire][:, j0 * 128:j1 * 128])
            cim = g3(cvf[iim][:, j0 * 128:j1 * 128])
            obr = g3(ctbf[:, 0, j0 * 128:j1 * 128])
            obi = g3(ctbf[:, 1, j0 * 128:j1 * 128])
            a1, k1 = A.get()
            a2, k2 = A.get()
            t1 = g3(a1[:, 0:n * 128])
            t2 = g3(a2[:, 0:n * 128])
            tt('dve', t1, cre, sre, ALU.mult, r=[('cvf', ire), 's5s'], w=[k1])
            tt('dve', t2, cim, sim, ALU.mult, r=[('cvf', iim), 's5s'], w=[k2])
            tt('dve', obr, t1, t2, ALU.subtract, r=[k1, k2], w=['ctbf'])
            tt('dve', t1, cre, sim, ALU.mult, r=[('cvf', ire), 's5s'], w=[k1])
            tt('dve', t2, cim, sre, ALU.mult, r=[('cvf', iim), 's5s'], w=[k2])
            tt('dve', obi, t1, t2, ALU.add, r=[k1, k2], w=['ctbf'])
            ts('dve', g3(ctbf[:, 2, j0 * 128:j1 * 128]), obi, -1.0, None, ALU.mult, None, r=['ctbf'], w=['ctbf'])
            A.put((a1, k1), (a2, k2))
        for cj in range(28):
            ts('pool' if cj % 2 else 'dve', dgw[:, cj, :], ident, pvc('convw', cj), None, ALU.mult, None,
               r=['cb', 'pv'], w=['dgw'])
        for c in range(3):
            ts('dve', dgd[:, c, :], ident, pvc('mdc', c), None, ALU.mult, None, r=['cb', 'pv'], w=['dgd'])
        act(abc[:, 0:6], pvc('alog', 0, 6), AF.Exp, r=['pv'], w=['abc'])
        ts('dve', abc[:, 0:6], abc[:, 0:6], -1.0, None, ALU.mult, None, r=['abc'], w=['abc'])
        ts('dve', hgb[:, 0:2], pvc('glub', 0, 2), 0.5, None, ALU.mult, None, r=['pv'], w=['hgb'])
        for i in range(3):
            memset('pool', hS[i][:, :, :], 0.0, w=[('hS', i)])
            memset('pool', hSb[i][:, :, :], 0.0, w=[('hSb', i)])
        memset('pool', Sm[:, :, :], 0.0, w=['Sm'])
        memset('pool', Smb[:, :, :], 0.0, w=['Smb'])
        memset('pool', s5c[:, :, :], 0.0, w=['s5c'])
        memset('pool', XB[:, :, 0:3], 0.0, w=[('XB', c) for c in range(7)])

    FMMAP = {}
    for pc in range(3):
        FMMAP[('q', pc)] = (0, pc * 1024)
    FMMAP[('f', 0)] = (0, 3072)
    FMMAP[('f', 1)] = (1, 0)
    FMMAP[('f', 2)] = (1, 1024)
    for c in range(7):
        FMMAP[('x', c)] = (4, c * 1024) if c < 4 else (5, (c - 4) * 1024)
    for m in range(2):
        FMMAP[('u', m)] = (7, m * 1024)

    def hk(b, k):
        return ('h', b, k)

    def fm_proj(b, slot, kslot, off):
        ps, kp = PS.get()
        for k in range(8):
            pe_mm(ps[:, 0:TB], slot[:, off + k * 128: off + (k + 1) * 128], hbf[:, k, b * TB:(b + 1) * TB],
                  k == 0, k == 7, r=[kslot, hk(b, k)], w=[kp])
        return ps, kp

    def tm_proj(b, t, slot, kslot, n):
        ps, kp = PS.get()
        c0 = b * TB + t * 128
        for k in range(8):
            pe_mm(ps[:, 0:n], hbf[:, k, c0:c0 + 128], slot[:, k * n:(k + 1) * n], k == 0, k == 7,
                  r=[kslot, hk(b, k)], w=[kp])
        return ps, kp

    def v3(ap, h):
        return ap.rearrange("p (h v) -> p h v", h=h)

    def bc(ap, n, m):
        return ap.unsqueeze(2).broadcast_to([128, n, m])

    def rstd_small(ss, kss, n, inv, eps):
        ts('dve', ss[:, 0:n], ss[:, 0:n], inv, eps, ALU.mult, ALU.add, r=[kss], w=[kss])
        act(ss[:, 0:n], ss[:, 0:n], AF.Ln, r=[kss], w=[kss])
        act(ss[:, 0:n], ss[:, 0:n], AF.Exp, r=[kss], w=[kss], scale=-0.5)

    def layernorm(rr, gname, bname, dst_fn):
        p1, k1 = PS.get()
        p2, k2 = PS.get()
        for n in range(8):
            rb, krb = B.get()
            rq, krq = B.get()
            cp('act', rb[:, :], rr[n][0][:, :], r=[rr[n][1]], w=[krb])
            act(rq[:, :], rr[n][0][:, :], AF.Square, r=[rr[n][1]], w=[krq])
            pe_mm(p1[:, 0:TB], onesb[:, :], rb[:, :], n == 0, n == 7, r=['onesb', krb], w=[k1])
            pe_mm(p2[:, 0:TB], onesb[:, :], rq[:, :], n == 0, n == 7, r=['onesb', krq], w=[k2])
            B.put((rb, krb), (rq, krq))
        mean, kme = A.get()
        msq, kms = A.get()
        rstd, krs_ = A.get()
        ts('dve', mean[:, :], p1[:, 0:TB], 1.0 / 1024, None, ALU.mult, None, r=[k1], w=[kme])
        tt('pool', msq[:, :], mean[:, :], mean[:, :], ALU.mult, r=[kme], w=[kms])
        stt('dve', rstd[:, :], p2[:, 0:TB], 1.0 / 1024, msq[:, :], ALU.mult, ALU.subtract, r=[k2, kms], w=[krs_])
        ts('dve', rstd[:, :], rstd[:, :], LN_EPS, None, ALU.add, None, r=[krs_], w=[krs_])
        act(rstd[:, :], rstd[:, :], AF.Ln, r=[krs_], w=[krs_])
        act(rstd[:, :], rstd[:, :], AF.Exp, r=[krs_], w=[krs_], scale=-0.5)
        PS.put((p1, k1), (p2, k2))
        for n in range(8):
            t1, kt1 = A.get()
            tt('pool', t1[:, :], rr[n][0][:, :], mean[:, :], ALU.subtract, r=[rr[n][1], kme], w=[kt1])
            tt('dve', t1[:, :], t1[:, :], rstd[:, :], ALU.mult, r=[kt1, krs_], w=[kt1])
            dst, kd, post = dst_fn(n)
            act(dst, t1[:, :], AF.Identity, r=[kt1, 'pv'], w=kd, bias=pvc(bname, n), scale=pvc(gname, n))
            A.put((t1, kt1))
            if post is not None:
                post()
        A.put((mean, kme), (msq, kms), (rstd, krs_))

    def mixer_block(l, b):
        c0 = b * TB
        yT = [B.get() for _ in range(8)]
        wmap = {}

        def wp_(pi):
            if pi not in wmap:
                wmap[pi] = get_piece(l, pi)
            return wmap[pi]
        qs, tth = [], []
        for pc in range(3):
            pi, off = FMMAP[('q', pc)]
            ps, kp = fm_proj(b, wp_(pi)[0], wp_(pi)[1], off)
            a, ka = A.get()
            act(a[:, :], ps[:, 0:TB], AF.Silu, r=[kp], w=[ka])
            PS.put((ps, kp))
            qs.append((a, ka))
        for pc in range(3):
            pi, off = FMMAP[('f', pc)]
            ps, kp = fm_proj(b, wp_(pi)[0], wp_(pi)[1], off)
            a, ka = A.get()
            act(a[:, :], ps[:, 0:TB], AF.Tanh, r=[kp], w=[ka], scale=0.5)
            PS.put((ps, kp))
            tth.append((a, ka))
        if STAGE <= 2:
            raise _Stop()
        qT, kT = [], []
        for pc in range(3):
            t_, kt = tth[pc]
            lf, klf = A.get()
            act(lf[:, :], t_[:, :], AF.Ln, r=[kt, 'lba'], w=[klf], bias=lba[:, pc, l, 0:1], scale=lba[:, pc, l, 1:2])
            kk, kkk = A.get()
            ts('dve', kk[:, :], t_[:, :], lba[:, pc, l, 2:3], lba[:, pc, l, 1:2], ALU.mult, ALU.add,
               r=[kt, 'lba'], w=[kkk])
            A.put(tth[pc])
            G, kG = A.get()
            P.add('dve', lambda e, G=G, lf=lf: e.tensor_tensor_scan(G[:, :], rstm, lf[:, :], 0.0, ALU.mult, ALU.add),
                  r=['cf', klf], w=[kG])
            A.put((lf, klf))
            eG, keG = A.get()
            enG, kenG = A.get()
            act(eG[:, :], G[:, :], AF.Exp, r=[kG], w=[keG])
            act(enG[:, :], G[:, :], AF.Exp, r=[kG], w=[kenG], scale=-1.0)
            if b == 0 and pc == 0 and l == 0:
                dump('G', G[:, :], kG)
                dump('qs', qs[pc][0][:, :], qs[pc][1])
                dump('kk', kk[:, :], kkk)
                dump('eG', eG[:, :], keG)
                dump('enG', enG[:, :], kenG)
            A.put((G, kG))
            cp('pool', egl[:, pc, 0:6], v3(eG[:, :], 6)[:, :, 63], r=[keG], w=['egl'])
            q_, kq = B.get()
            q2_, kq2 = B.get()
            k_, kk_ = B.get()
            stt('dve', q_[:, :], qs[pc][0][:, :], mcol[:, 0:1], eG[:, :], ALU.mult, ALU.mult, r=[qs[pc][1], keG, 'mcol'], w=[kq])
            stt('dve', q2_[:, :], qs[pc][0][:, :], mcol[:, 1:2], eG[:, :], ALU.mult, ALU.mult, r=[qs[pc][1], keG, 'mcol'], w=[kq2])
            tt('pool', k_[:, :], kk[:, :], enG[:, :], ALU.mult, r=[kkk, kenG], w=[kk_])
            A.put(qs[pc], (kk, kkk), (eG, keG), (enG, kenG))
            qT.append(((q_, kq), (q2_, kq2)))
            kT.append((k_, kk_))
        if STAGE <= 1:
            raise _Stop()
        wi = get_piece(l, 2)
        vb = []
        for t in range(NT):
            ps, kp = tm_proj(b, t, wi[0], wi[1], 384)
            v, kv = B.get()
            cp('act', v[:, :], ps[:, 0:384], r=[kp], w=[kv])
            PS.put((ps, kp))
            vb.append((v, kv))
        wg = get_piece(l, 3)
        sg = []
        for t in range(NT):
            ps, kp = tm_proj(b, t, wg[0], wg[1], 384)
            a, ka = A.get()
            act(a[:, :], ps[:, 0:384], AF.Silu, r=[kp], w=[ka])
            PS.put((ps, kp))
            sg.append((a, ka))
        if STAGE <= 3:
            raise _Stop()
        def hg_tile(t):
            if True:
                tc0 = t * 128
                pst, kpt = PS.get()
                pstb = pst.bitcast(BF16)
                for pc in range(3):
                    pe_tr(pstb[:, pc * 128:(pc + 1) * 128], kT[pc][0][:, tc0:tc0 + 128], r=[kT[pc][1]], w=[kpt])
                ktm, kktm = B.get()
                cp('dve', ktm[:, :], pstb[:, 0:384], r=[kpt], w=[kktm])
                PS.put((pst, kpt))
                yield
                pss = [PS.get(), PS.get()]
                for h in range(6):
                    pc, po = h // 2, 64 * (h % 2)
                    ps, kp = pss[h // 3]
                    qm, kqm = qT[pc][h % 2]
                    pe_mm(ps[:, (h % 3) * 128:(h % 3 + 1) * 128], kT[pc][0][:, tc0:tc0 + 128],
                          qm[:, tc0:tc0 + 128], True, True, r=[kT[pc][1], kqm], w=[kp])
                yield
                AT = [B.get(), B.get()]
                for g in range(2):
                    tt('dve', v3(AT[g][0][:, :], 3), v3(pss[g][0][:, 0:384], 3),
                       mask2.unsqueeze(1).broadcast_to([128, 3, 128]), ALU.mult, r=[pss[g][1], 'cb'], w=[AT[g][1]])
                    PS.put(pss[g])
                yield
                gci0 = (b * NT + t) * 2
                for c in range(2):
                    cur, nw = (gci0 + c) % 3, (gci0 + c + 1) % 3
                    ps, kp = PS.get()
                    r0 = 64 * c
                    for h in range(6):
                        pc, po = h // 2, 64 * (h % 2)
                        pe_mm(ps[po:po + 64, pc * 64:(pc + 1) * 64], ktm[r0:r0 + 64, h * 64:(h + 1) * 64],
                              vb[t][0][r0:r0 + 64, h * 64:(h + 1) * 64], True, True, r=[kktm, vb[t][1]], w=[kp])
                    tmp, ktmp = A.get()
                    tt('dve', v3(tmp[:, 0:192], 3), v3(ps[:, 0:192], 3), hS[cur][:, :, :], ALU.add,
                       r=[kp, ('hS', cur)], w=[ktmp])
                    PS.put((ps, kp))
                    tt('dve', hS[nw][:, :, :], v3(tmp[:, 0:192], 3),
                       egl[:, :, 2 * t + c:2 * t + c + 1].broadcast_to([128, 3, 64]), ALU.mult,
                       r=[ktmp, 'egl'], w=[('hS', nw)])
                    tt('dve', hSb[nw][:, :, :], v3(tmp[:, 0:192], 3),
                       egl[:, :, 2 * t + c:2 * t + c + 1].broadcast_to([128, 3, 64]), ALU.mult,
                       r=[ktmp, 'egl'], w=[('hSb', nw)])
                    A.put((tmp, ktmp))
                yield
                pso, kpo = PS.get()
                for h in range(6):
                    pc, po = h // 2, 64 * (h % 2)
                    s0, s1 = gci0 % 3, (gci0 + 1) % 3
                    pe_mm(pso[:, h * 64:(h + 1) * 64], AT[h // 3][0][:, (h % 3) * 128:(h % 3 + 1) * 128],
                          vb[t][0][:, h * 64:(h + 1) * 64], True, False, r=[AT[h // 3][1], vb[t][1]], w=[kpo])
                    qm, kqm = qT[pc][h % 2]
                    pe_mm(pso[0:64, h * 64:(h + 1) * 64], qm[:, tc0:tc0 + 64],
                          hSb[s0][:, pc, :], False, True, r=[kqm, ('hSb', s0)], w=[kpo])
                    pe_mm(pso[64:128, h * 64:(h + 1) * 64], qm[:, tc0 + 64:tc0 + 128],
                          hSb[s1][:, pc, :], False, True, r=[kqm, ('hSb', s1)], w=[kpo])
                B.put(*AT)
                B.put((ktm, kktm))
                yield
                sq, ksq = A.get()
                act(sq[:, :], pso[:, 0:384], AF.Square, r=[kpo], w=[ksq])
                ss, kss = small()
                P.add('dve', lambda e, ss=ss, sq=sq: e.tensor_reduce(ss[:, 0:6], v3(sq[:, :], 6), AX.X, ALU.add),
                      r=[ksq], w=[kss])
                rstd_small(ss, kss, 6, 1.0 / 64, RMS_EPS)
                tt('dve', v3(sq[:, :], 6), v3(pso[:, 0:384], 6), bc(ss[:, 0:6], 6, 64), ALU.mult, r=[kpo, kss], w=[ksq])
                PS.put((pso, kpo))
                yield
                yb_, kyb = B.get()
                tt('pool', yb_[:, :], sq[:, :], sg[t][0][:, :], ALU.mult, r=[ksq, sg[t][1]], w=[kyb])
                A.put((sq, ksq))
                pst, kpt = PS.get()
                pstb = pst.bitcast(BF16)
                for pc in range(3):
                    pe_tr(pstb[:, pc * 128:(pc + 1) * 128], yb_[:, pc * 128:(pc + 1) * 128], r=[kyb], w=[kpt])
                for pc in range(3):
                    ts('dve', yT[pc][0][:, tc0:tc0 + 128], pstb[:, pc * 128:(pc + 1) * 128], pvc('hnw', pc), None,
                       ALU.mult, None, r=[kpt, 'pv'], w=[yT[pc][1]])
                PS.put((pst, kpt))
                yield
                B.put((yb_, kyb))

        def hg_gen():
            tiles = [hg_tile(t) for t in range(NT)]
            started, live, step = 0, [], 0
            while started < NT or live:
                if started < NT and step % HG_LAG == 0:
                    live.append(tiles[started])
                    started += 1
                for g in list(live):
                    try:
                        next(g)
                    except StopIteration:
                        live.remove(g)
                step += 1
                yield
            for x in kT + vb:
                B.put(x)
            for x in qT:
                B.put(x[0], x[1])
            for x in sg:
                A.put(x)
            yield

        xc = []
        sz = []
        def prep_gen():
            allXB = [('XB', c) for c in range(7)]
            if b > 0:
                cp('dve', XB[:, :, 0:3], XB[:, :, 384:387], r=allXB, w=allXB)
            for c in range(7):
                pi, off = FMMAP[('x', c)]
                ps, kp = fm_proj(b, wp_(pi)[0], wp_(pi)[1], off)
                cp('act', XB[:, c, 3:387], ps[:, 0:TB], r=[kp], w=[('XB', c)])
                PS.put((ps, kp))
                yield
                ps2, kp2 = PS.get()
                for j in range(4):
                    pe_mm(ps2[:, 0:TB], dgw[:, c * 4 + j, :], XB[:, c, j:j + TB], j == 0, j == 3,
                          r=['dgw', ('XB', c)], w=[kp2])
                x_, kx = B.get()
                act(x_[:, :], ps2[:, 0:TB], AF.Silu, r=[kp2, 'pv'], w=[kx], bias=pvc('convb', c))
                PS.put((ps2, kp2))
                xc.append((x_, kx))
                yield
            w6 = get_piece(l, 6)
            for t in range(NT):
                ps, kp = tm_proj(b, t, w6[0], w6[1], 390)
                a, ka = A.get()
                act(a[:, :], ps[:, 0:384], AF.Silu, r=[kp], w=[ka])
                cp('act', dtt[:, t, 0:6], ps[:, 384:390], r=[kp], w=[('dtt', t)])
                tt('dve', dtt[:, t, 0:6], dtt[:, t, 0:6], pvc('dtb', 0, 6), ALU.add, r=[('dtt', t), 'pv'], w=[('dtt', t)])
                yield
                PS.put((ps, kp))
                sz.append((a, ka))
            for t in range(NT):
                act(dtt[:, t, 0:6], dtt[:, t, 0:6], AF.Exp, r=[('dtt', t)], w=[('dtt', t)])
            for t in range(NT):
                act(dtt[:, t, 0:6], dtt[:, t, 0:6], AF.Ln, r=[('dtt', t)], w=[('dtt', t)], bias=1.0)
                tt('dve', dtt[:, t, 8:14], dtt[:, t, 0:6], abc[:, 0:6], ALU.mult, r=[('dtt', t), 'abc'], w=[('dtt', t)])
            if STAGE == 45:
                raise _Stop()
            yield

        gens0 = [(hg_gen(), 1), (prep_gen(), 2)]
        alive0 = list(gens0)
        while alive0:
            for g in list(alive0):
                for _ in range(g[1]):
                    try:
                        next(g[0])
                    except StopIteration:
                        alive0.remove(g)
                        break

        if STAGE <= 4:
            raise _Stop()
        def ssd_gen():
            for t in range(NT):
                tc0 = t * 128
                kdt = ('dtt', t)
                dtv = dtt[:, t, 0:6]
                dA = dtt[:, t, 8:14]
                pst, kpt = PS.get()
                pstb = pst.bitcast(BF16)
                for j in range(5):
                    pe_tr(pstb[:, j * 128:(j + 1) * 128], xc[j][0][:, tc0:tc0 + 128], r=[xc[j][1]], w=[kpt])
                if STAGE == 461 and t == TCUT:
                    raise _Stop()
                xtm, kxtm = B.get()
                if VAR == 1:
                    _sp = B.get()
                btm, kbtm = B.get()
                cp('act', xtm[:, :], pstb[:, 0:384], r=[kpt], w=[kxtm])
                if STAGE == 462 and t == TCUT:
                    raise _Stop()
                cp('act', btm[:, 0:256], pstb[:, 384:640], r=[kpt], w=[kbtm])
                PS.put((pst, kpt))
                if STAGE == 460 and t == TCUT:
                    raise _Stop()
                yield
                pa, kpa = PS.get()
                pe_mm(pa[:, 0:6], tri, dA, True, True, r=['cf', kdt], w=[kpa])
                pe_mm(pa[:, 8:14], onesf[:, :], dA, True, True, r=['onesf', kdt], w=[kpa])
                ac = acs[:, t, :]
                kac = ('acs', t)
                cp('dve', ac[:, 0:14], pa[:, 0:14], r=[kpa], w=[kac])
                PS.put((pa, kpa))
                if STAGE == 46 and t == TCUT:
                    raise _Stop()
                act(ac[:, 16:30], ac[:, 0:14], AF.Exp, r=[kac], w=[kac])
                ts('dve', ac[:, 32:38], ac[:, 0:6], -1.0, None, ALU.mult, None, r=[kac], w=[kac])
                tt('dve', ac[:, 40:46], ac[:, 8:14], ac[:, 0:6], ALU.subtract, r=[kac], w=[kac])
                act(ac[:, 40:46], ac[:, 40:46], AF.Exp, r=[kac], w=[kac])
                eac = ac[:, 16:22]
                etot = ac[:, 24:30]
                dec = ac[:, 40:46]
                yield
                dAb = [A.get(), A.get()]
                for g in range(2):
                    cp('pool', v3(dAb[g][0][:, :], 3), bc(dA[:, 3 * g:3 * g + 3], 3, 128), r=[kdt], w=[dAb[g][1]])
                pL = [PS.get(), PS.get()]
                for h in range(6):
                    ps, kp = pL[h // 3]
                    o = (h % 3) * 128
                    pe_mm(ps[:, o:o + 128], dAb[h // 3][0][:, o:o + 128], tri, True, False, r=[dAb[h // 3][1], 'cf'], w=[kp])
                    pe_mm(ps[:, o:o + 128], identf, negmf[:, :], False, True, r=['cf', 'negmf'], w=[kp])
                LTs = [A.get(), A.get()]
                for h in range(6):
                    o = (h % 3) * 128
                    act(LTs[h // 3][0][:, o:o + 128], pL[h // 3][0][:, o:o + 128], AF.Exp, r=[pL[h // 3][1], kac],
                        w=[LTs[h // 3][1]], bias=ac[:, 32 + h:33 + h])
                PS.put(*pL)
                A.put(*dAb)
                if STAGE == 47 and t == TCUT:
                    raise _Stop()
                yield
                psg, kpsg = PS.get()
                for g in range(2):
                    pe_mm(psg[:, g * 128:(g + 1) * 128], xc[3 + g][0][:, tc0:tc0 + 128], xc[5 + g][0][:, tc0:tc0 + 128],
                          True, True, r=[xc[3 + g][1], xc[5 + g][1]], w=[kpsg])
                AT = [B.get(), B.get()]
                for g in range(2):
                    tt('dve', v3(AT[g][0][:, :], 3), v3(LTs[g][0][:, :], 3),
                       psg[:, g * 128:(g + 1) * 128].unsqueeze(1).broadcast_to([128, 3, 128]), ALU.mult,
                       r=[LTs[g][1], kpsg], w=[AT[g][1]])
                PS.put((psg, kpsg))
                A.put(*LTs)
                if STAGE == 48 and t == TCUT:
                    raise _Stop()
                yield
                xdt, kxdt = B.get()
                xw, kxw = B.get()
                tt('pool', v3(xdt[:, :], 6), v3(xtm[:, :], 6), bc(dtv, 6, 64), ALU.mult, r=[kxtm, kdt], w=[kxdt])
                tt('pool', v3(xw[:, :], 6), v3(xdt[:, :], 6), bc(dec, 6, 64), ALU.mult, r=[kxdt, kac], w=[kxw])
                psy, kpsy = PS.get()
                psyo, kpsyo = PS.get()
                psS, kpsS = PS.get()
                for h in range(6):
                    g = h // 3
                    hs = slice(h * 64, (h + 1) * 64)
                    pe_mm(psy[:, hs], xc[h // 2][0][:, tc0:tc0 + 128], dgd[:, h // 2, (h % 2) * 64:(h % 2 + 1) * 64],
                          True, False, r=[xc[h // 2][1], 'dgd'], w=[kpsy])
                    pe_mm(psy[:, hs], AT[g][0][:, (h % 3) * 128:(h % 3 + 1) * 128], xdt[:, hs], False, True,
                          r=[AT[g][1], kxdt], w=[kpsy])
                for h in range(6):
                    g = h // 3
                    hs = slice(h * 64, (h + 1) * 64)
                    pe_mm(psyo[:, hs], xc[5 + g][0][:, tc0:tc0 + 128], Smb[:, h, :], True, True,
                          r=[xc[5 + g][1], 'Smb'], w=[kpsyo])
                for h in range(6):
                    g = h // 3
                    hs = slice(h * 64, (h + 1) * 64)
                    pe_mm(psS[:, hs], btm[:, g * 128:(g + 1) * 128], xw[:, hs], True, True, r=[kbtm, kxw], w=[kpsS])
                B.put(*AT)
                if STAGE == 49 and t == TCUT:
                    raise _Stop()
                yield
                t1, kt1 = A.get()
                t2, kt2 = A.get()
                tt('dve', v3(t1[:, :], 6), v3(psyo[:, 0:384], 6), bc(eac, 6, 64), ALU.mult, r=[kpsyo, kac], w=[kt1])
                PS.put((psyo, kpsyo))
                tt('dve', t1[:, :], psy[:, 0:384], t1[:, :], ALU.add, r=[kpsy, kt1], w=[kt1])
                PS.put((psy, kpsy))
                tt('pool', t1[:, :], t1[:, :], sz[t][0][:, :], ALU.mult, r=[kt1, sz[t][1]], w=[kt1])
                act(t2[:, :], t1[:, :], AF.Square, r=[kt1], w=[kt2])
                ss, kss = small()
                P.add('dve', lambda e, ss=ss, t2=t2: e.tensor_reduce(ss[:, 0:2], v3(t2[:, :], 2), AX.X, ALU.add),
                      r=[kt2], w=[kss])
                rstd_small(ss, kss, 2, 1.0 / 192, RMS_EPS)
                yb_, kyb = B.get()
                tt('pool', v3(yb_[:, :], 2), v3(t1[:, :], 2), bc(ss[:, 0:2], 2, 192), ALU.mult, r=[kt1, kss], w=[kyb])
                if STAGE == 50 and t == TCUT:
                    raise _Stop()
                yield
                tt('dve', v3(t2[:, :], 6), Sm[:, :, :], bc(etot, 6, 64), ALU.mult, r=['Sm', kac], w=[kt2])
                tt('dve', Sm[:, :, :], v3(t2[:, :], 6), v3(psS[:, 0:384], 6), ALU.add, r=[kt2, kpsS], w=['Sm'])
                PS.put((psS, kpsS))
                cp('act', Smb[:, :, :], Sm[:, :, :], r=['Sm'], w=['Smb'])
                if STAGE == 51 and t == TCUT:
                    raise _Stop()
                yield
                A.put((t1, kt1), (t2, kt2))
                pst, kpt = PS.get()
                pstb = pst.bitcast(BF16)
                for pc in range(3):
                    pe_tr(pstb[:, pc * 128:(pc + 1) * 128], yb_[:, pc * 128:(pc + 1) * 128], r=[kyb], w=[kpt])
                for pc in range(3):
                    act(yT[3 + pc][0][:, tc0:tc0 + 128], pstb[:, pc * 128:(pc + 1) * 128], AF.Copy,
                        r=[kpt, 'pv'], w=[yT[3 + pc][1]], scale=pvc('mnw', pc))
                PS.put((pst, kpt))
                yield
                B.put((yb_, kyb), (xtm, kxtm), (btm, kbtm), (xdt, kxdt), (xw, kxw))
                if STAGE == 52 + t:
                    raise _Stop()
            for x in xc:
                B.put(x)
            for x in sz:
                A.put(x)


        if STAGE <= 5:
            raise _Stop()
        def s5_gen():
            w7 = get_piece(l, 7)
            uf, ub = [], []
            for m in range(2):
                pi, off = FMMAP[('u', m)]
                ps, kp = fm_proj(b, w7[0], w7[1], off)
                a, ka = A.get()
                u_, ku = B.get()
                cp('act', a[:, :], ps[:, 0:TB], r=[kp], w=[ka])
                cp('act', u_[:, :], ps[:, 0:TB], r=[kp], w=[ku])
                PS.put((ps, kp))
                uf.append((a, ka))
                ub.append((u_, ku))
            if STAGE == 60:
                raise _Stop()
            gl = []
            gelb = []
            for m in range(2):
                py, kpy = PS.get()

                def chunk_gen(jj, m=m, py=py, kpy=kpy):
                    j = 4 * m + jj
                    kF, kN = ('Fr', j), ('nF', j)
                    pA, kpA = PS.get()
                    pB, kpB = PS.get()
                    pe_mm(pA[:, 0:TB], btbf[:, 0, j * 128:(j + 1) * 128], ub[m][0][:, :], True, True,
                          r=['btbf', ub[m][1]], w=[kpA])
                    pe_mm(pB[:, 0:TB], btbf[:, 1, j * 128:(j + 1) * 128], ub[m][0][:, :], True, True,
                          r=['btbf', ub[m][1]], w=[kpB])
                    yield
                    a_bf, kab = B.get()
                    b_bf, kbb = B.get()
                    cp('act', a_bf[:, :], pA[:, 0:TB], r=[kpA], w=[kab])
                    cp('act', b_bf[:, :], pB[:, 0:TB], r=[kpB], w=[kbb])
                    PS.put((pA, kpA), (pB, kpB))
                    qq = [B.get() for _ in range(4)]
                    tt('dve', qq[0][0][:, :], Fre[:, j, :], a_bf[:, :], ALU.mult, r=[kF, kab], w=[qq[0][1]])
                    tt('pool', qq[1][0][:, :], nFim[:, j, :], b_bf[:, :], ALU.mult, r=[kN, kbb], w=[qq[1][1]])
                    tt('dve', qq[2][0][:, :], Fre[:, j, :], b_bf[:, :], ALU.mult, r=[kF, kbb], w=[qq[2][1]])
                    tt('pool', qq[3][0][:, :], nFim[:, j, :], a_bf[:, :], ALU.mult, r=[kN, kab], w=[qq[3][1]])
                    B.put((a_bf, kab), (b_bf, kbb))
                    yield
                    pre, kpre = PS.get()
                    pim, kpim = PS.get()
                    pe_mm(pre[:, 0:TB], ident, qq[0][0][:, :], True, False, r=['cb', qq[0][1]], w=[kpre])
                    pe_mm(pre[:, 0:TB], nident[:, :], qq[1][0][:, :], False, True, r=['nident', qq[1][1]], w=[kpre])
                    pe_mm(pim[:, 0:TB], ident, qq[2][0][:, :], True, False, r=['cb', qq[2][1]], w=[kpim])
                    pe_mm(pim[:, 0:TB], ident, qq[3][0][:, :], False, True, r=['cb', qq[3][1]], w=[kpim])
                    B.put(*qq)
                    if STAGE == 61 and jj == 0 and m == 0:
                        raise _Stop()
                    yield
                    t2, k2 = A.get()
                    t4, k4 = A.get()
                    rj = s5s[:, 3, j:j + 1].broadcast_to([128, TB])
                    P.add('dve', lambda e, o=t2, rj=rj, p=pre, i=s5c[:, 0, j:j + 1]:
                          e.tensor_tensor_scan(o[:, :], rj, p[:, 0:TB], i, ALU.mult, ALU.add),
                          r=['s5s', kpre, ('s5c', j)], w=[k2])
                    P.add('dve', lambda e, o=t4, rj=rj, p=pim, i=s5c[:, 1, j:j + 1]:
                          e.tensor_tensor_scan(o[:, :], rj, p[:, 0:TB], i, ALU.mult, ALU.add),
                          r=['s5s', kpim, ('s5c', j)], w=[k4])
                    PS.put((pre, kpre), (pim, kpim))
                    if STAGE == 62 and jj == 0 and m == 0:
                        raise _Stop()
                    yield
                    bb = [B.get() for _ in range(4)]
                    tt('pool', bb[0][0][:, :], Fre[:, j, :], t2[:, :], ALU.mult, r=[kF, k2], w=[bb[0][1]])
                    tt('dve', bb[1][0][:, :], nFim[:, j, :], t4[:, :], ALU.mult, r=[kN, k4], w=[bb[1][1]])
                    tt('pool', bb[2][0][:, :], nFim[:, j, :], t2[:, :], ALU.mult, r=[kN, k2], w=[bb[2][1]])
                    tt('pool', bb[3][0][:, :], Fre[:, j, :], t4[:, :], ALU.mult, r=[kF, k4], w=[bb[3][1]])
                    cc, kcc = small()
                    e = TB - 1
                    tt('dve', cc[:, 0:1], Fre[:, j, e:e + 1], t2[:, e:e + 1], ALU.mult, r=[kF, k2], w=[kcc])
                    tt('dve', cc[:, 1:2], nFim[:, j, e:e + 1], t2[:, e:e + 1], ALU.mult, r=[kN, k2], w=[kcc])
                    tt('dve', cc[:, 2:3], nFim[:, j, e:e + 1], t4[:, e:e + 1], ALU.mult, r=[kN, k4], w=[kcc])
                    tt('dve', cc[:, 3:4], Fre[:, j, e:e + 1], t4[:, e:e + 1], ALU.mult, r=[kF, k4], w=[kcc])
                    tt('dve', s5c[:, 0, j:j + 1], cc[:, 0:1], cc[:, 2:3], ALU.add, r=[kcc], w=[('s5c', j)])
                    tt('dve', s5c[:, 1, j:j + 1], cc[:, 3:4], cc[:, 1:2], ALU.subtract, r=[kcc], w=[('s5c', j)])
                    if STAGE == 63 and jj == 0 and m == 0:
                        raise _Stop()
                    yield
                    A.put((t2, k2), (t4, k4))
                    for q_, ci_ in enumerate((0, 0, 1, 2)):
                        pe_mm(py[:, 0:TB], ctbf[:, ci_, j * 128:(j + 1) * 128], bb[q_][0][:, :],
                              jj == 0 and q_ == 0, jj == 3 and q_ == 3, r=['ctbf', bb[q_][1]], w=[kpy])
                    B.put(*bb)

                for pair in ((0, 1), (2, 3)):
                    live = [chunk_gen(jj) for jj in pair]
                    while live:
                        for g in list(live):
                            try:
                                next(g)
                            except StopIteration:
                                live.remove(g)
                        yield
                yv, kyv = A.get()
                stt('dve', yv[:, :], uf[m][0][:, :], pvc('s5d', m), py[:, 0:TB], ALU.mult, ALU.add,
                    r=[uf[m][1], 'pv', kpy], w=[kyv])
                PS.put((py, kpy))
                A.put(uf[m])
                yield
                gb, kgb = B.get()
                if GELU_ACT:
                    act(gb[:, :], yv[:, :], AF.Gelu_apprx_tanh, r=[kyv], w=[kgb])
                    A.put((yv, kyv))
                    gl.append(None)
                else:
                    y2, ky2 = A.get()
                    tt('pool', y2[:, :], yv[:, :], yv[:, :], ALU.mult, r=[kyv], w=[ky2])
                    ts('dve', y2[:, :], y2[:, :], 0.044715, 1.0, ALU.mult, ALU.add, r=[ky2], w=[ky2])
                    tt('pool', y2[:, :], y2[:, :], yv[:, :], ALU.mult, r=[ky2, kyv], w=[ky2])
                    act(y2[:, :], y2[:, :], AF.Tanh, r=[ky2], w=[ky2], scale=0.7978845608028654)
                    stt('dve', yv[:, :], y2[:, :], 1.0, yv[:, :], ALU.add, ALU.mult, r=[ky2, kyv], w=[kyv])
                    ts('dve', gb[:, :], yv[:, :], 0.5, None, ALU.mult, None, r=[kyv], w=[kgb])
                    A.put((y2, ky2))
                    gl.append((yv, kyv))
                gelb.append((gb, kgb))
            yield
            for m2 in range(2):
                ps, kp = PS.get()
                for m in range(2):
                    pe_mm(ps[:, 0:TB], gwbf[:, m * 256 + m2 * 128: m * 256 + (m2 + 1) * 128], gelb[m][0][:, :],
                          m == 0, m == 1, r=['gwbf', gelb[m][1]], w=[kp])
                th, kth = A.get()
                act(th[:, :], ps[:, 0:TB], AF.Tanh, r=[kp, 'hgb'], w=[kth], bias=hgb[:, m2:m2 + 1], scale=0.5)
                PS.put((ps, kp))
                if GELU_ACT:
                    ts('dve', th[:, :], th[:, :], 0.5, 0.5, ALU.mult, ALU.add, r=[kth], w=[kth])
                    tt('dve', yT[6 + m2][0][:, :], gelb[m2][0][:, :], th[:, :], ALU.mult, r=[gelb[m2][1], kth],
                       w=[yT[6 + m2][1]])
                else:
                    ts('dve', th[:, :], th[:, :], 0.25, 0.25, ALU.mult, ALU.add, r=[kth], w=[kth])
                    tt('dve', yT[6 + m2][0][:, :], gl[m2][0][:, :], th[:, :], ALU.mult, r=[gl[m2][1], kth],
                       w=[yT[6 + m2][1]])
                A.put((th, kth))
            for m in range(2):
                if gl[m] is not None:
                    A.put(gl[m])
                B.put(gelb[m], ub[m])

            yield

        if mid_hook[0] is not None:
            mid_hook[0]()
        gens = [(ssd_gen(), 1), (s5_gen(), 1)]
        alive = list(gens)
        while alive:
            for g in list(alive):
                for _ in range(g[1]):
                    try:
                        next(g[0])
                    except StopIteration:
                        alive.remove(g)
                        break
            conv_pump()
        conv_drain()
        if STAGE <= 6:
            raise _Stop()
        if DEBUG and l == 0:
            for k in range(8):
                a, ka = A.get()
                cp('dve', a[:, :], yT[k][0][:, :], r=[yT[k][1]], w=[ka])
                dma(dbgd[b * 16 + k], a[:, :], r=[ka], w=[('dbg', b, k)])
                A.put((a, ka))
        rr = []
        for n in range(8):
            slot, kslot = wp_(8 if n < 4 else 9)
            off = (n % 4) * 1024
            ps, kp = PS.get()
            for k in range(8):
                pe_mm(ps[:, 0:TB], slot[:, off + k * 128: off + (k + 1) * 128], yT[k][0][:, :], k == 0, k == 7,
                      r=[kslot, yT[k][1]], w=[kp])
            r_, kr = A.get()
            stt('dve', r_[:, :], hbf[:, n, c0:c0 + TB], ALPHA, ps[:, 0:TB], ALU.mult, ALU.add, r=[hk(b, n), kp], w=[kr])
            PS.put((ps, kp))
            rr.append((r_, kr))
        for x in yT:
            B.put(x)
        layernorm(rr, 'ln1g', 'ln1b', lambda n: (hbf[:, n, c0:c0 + TB], [hk(b, n)], None))
        for x in rr:
            A.put(x)
        if DEBUG and l == 0:
            for k in range(8):
                a, ka = A.get()
                cp('dve', a[:, :], hbf[:, k, c0:c0 + TB], r=[hk(b, k)], w=[ka])
                dma(dbgd[b * 16 + 8 + k], a[:, :], r=[ka], w=[('dbg', b, 8 + k)])
                A.put((a, ka))

    def ffn_block(l, b, last):
        if STAGE <= 7:
            raise _Stop()
        c0 = b * TB
        hid = []
        for hg in range(8):
            wp = get_piece(l, 10 + hg)
            for hc in range(4):
                ps, kp = fm_proj(b, wp[0], wp[1], hc * 1024)
                rl, krl = B.get()
                if (hg * 4 + hc) % 2 == 0:
                    act(rl[:, :], ps[:, 0:TB], AF.Relu, r=[kp], w=[krl])
                else:
                    ts('dve', rl[:, :], ps[:, 0:TB], 0.0, None, ALU.max, None, r=[kp], w=[krl])
                PS.put((ps, kp))
                hd, khd = B.get()
                tt('pool', hd[:, :], rl[:, :], rl[:, :], ALU.mult, r=[krl], w=[khd])
                B.put((rl, krl))
                hid.append((hd, khd))
            yield
        rr = []
        for n in range(8):
            wp = get_piece(l, 18 + n)
            ps, kp = PS.get()
            for j in range(32):
                pe_mm(ps[:, 0:TB], wp[0][:, j * 128:(j + 1) * 128], hid[j][0][:, :], j == 0, j == 31,
                      r=[wp[1], hid[j][1]], w=[kp])
            r_, kr = A.get()
            stt('dve', r_[:, :], hbf[:, n, c0:c0 + TB], ALPHA, ps[:, 0:TB], ALU.mult, ALU.add, r=[hk(b, n), kp], w=[kr])
            PS.put((ps, kp))
            rr.append((r_, kr))
        for x in hid:
            B.put(x)
        def ln_fn():
            if not last:
                layernorm(rr, 'ln2g', 'ln2b', lambda n: (hbf[:, n, c0:c0 + TB], [hk(b, n)], None))
            else:
                def dst(n):
                    o, ko = A.get()

                    def post():
                        dma(outd[:, n, c0:c0 + TB], o[:, :], r=[ko], w=[('out', b, n)])
                        A.put((o, ko))
                    return o[:, :], [ko], post
                layernorm(rr, 'ln2g', 'ln2b', dst)
            for x in rr:
                A.put(x)
        ffn_pend.append(ln_fn)

    ffn_pend = []
    mid_hook = [None]

    def whole():
        for l in range(DEPTH):
            flush_convert()
            layer_setup(l)
            if STAGE <= 0:
                raise _Stop()
            nconv = NPIECE * 4
            steps = 2 * NBLK
            done = 0
            dn = [done]

            def conv_hook(l=l, dn=dn):
                b = conv_hook.b
                if l == 0:
                    lo = 40 + (64 * b) // NBLK
                    hi = 40 + (64 * (b + 1)) // NBLK
                    for q in range(lo, hi):
                        conv_queue.append((0, q))
                if l + 1 < DEPTH:
                    tgt = (nconv * (b + 1)) // steps
                    for q in range(dn[0], tgt):
                        conv_queue.append((l + 1, q))
                    dn[0] = tgt
            for b in range(NBLK):
                conv_hook.b = b
                mid_hook[0] = conv_hook
                mixer_block(l, b)
                mid_hook[0] = None
                if l == 0 and b == NBLK - 1:
                    flush_convert()
            done = dn[0]
            for b in range(NBLK):
                g_ffn = ffn_block(l, b, l == DEPTH - 1)
                for _ in range(FFN_DEFER):
                    next(g_ffn)
                if len(ffn_pend) > 0:
                    ffn_pend.pop(0)()
                for _ in g_ffn:
                    pass
                if l + 1 < DEPTH:
                    tgt = (nconv * (NBLK + b + 1)) // steps
                    for q in range(done, tgt):
                        convert_unit(l + 1, q)
                    done = tgt
            while ffn_pend:
                ffn_pend.pop(0)()
    try:
        whole()
    except _Stop:
        A.free = [(t, ('A', i)) for i, t in enumerate(Apool)]
        for b in range(NBLK):
            for k in range(8):
                a, ka = A.get()
                cp('dve', a[:, :], hbf[:, k, b * TB:(b + 1) * TB], r=[hk(b, k)], w=[ka])
                dma(outd[:, k, b * TB:(b + 1) * TB], a[:, :], r=[ka], w=[('out', b, k)])
                A.put((a, ka))

    ops = P.ops
    for o in ops:
        for j in o['deps']:
            ops[j]['sig'] = True
    engs = ['pe', 'act', 'dve', 'pool', 'sp']
    cnt = {e: 0 for e in engs}
    ndma = 0
    for o in ops:
        if o['dma']:
            o['dsem'] = ndma % ND
            o['dval'] = 16 * (ndma // ND + 1)
            ndma += 1
        elif o['sig']:
            cnt[o['eng']] += 1
            o['cnt'] = cnt[o['eng']]
    print("ops", len(ops), "sig counts", cnt, "dmas", ndma, "minfree A/B/PS", A.minfree, B.minfree, PS.minfree)
    esem = {e: es.enter_context(nc.semaphore("sem_" + e)) for e in engs if e != 'sp'}
    dsem = [es.enter_context(nc.semaphore(f"dsem{i}")) for i in range(ND)]
    out_dmas = [o for o in ops if o['dma']]

    def emit(ename, eng):
        waited = {}
        for o in ops:
            if o['eng'] != ename:
                continue
            need = {}
            for j in o['deps']:
                oj = ops[j]
                if oj['dma']:
                    key = ('d', oj['dsem'])
                    val = oj['dval']
                else:
                    key = ('e', oj['eng'])
                    val = oj['cnt']
                if val > need.get(key, 0):
                    need[key] = val
            if o['dma'] and o['dval'] > 16:
                key = ('d', o['dsem'])
                need[key] = max(need.get(key, 0), o['dval'] - 16)
            for key, val in need.items():
                if val > waited.get(key, 0):
                    sem = dsem[key[1]] if key[0] == 'd' else esem[key[1]]
                    eng.wait_ge(sem, val)
                    waited[key] = val
            ins = o['fn'](eng)
            if o['dma']:
                ins.then_inc(dsem[o['dsem']], 16)
            elif o['sig']:
                ins.then_inc(esem[ename], 1)
        if ename == 'sp':
            last = {}
            for o in out_dmas:
                last[o['dsem']] = max(last.get(o['dsem'], 0), o['dval'])
            for s, v in last.items():
                eng.wait_ge(dsem[s], v)

    with nc.Block() as block:
        @block.tensor
        def _(e):
            emit('pe', e)

        @block.scalar
        def _(e):
            emit('act', e)

        @block.vector
        def _(e):
            emit('dve', e)

        @block.gpsimd
        def _(e):
            emit('pool', e)

        @block.sync
        def _(e):
            emit('sp', e)
    es.close()
    return nc


SPL = np.cumsum([0, 384, 384, 384, 384, 384, 896, 6, 256])


def _fmchunk(W, c0):
    return W[:, c0:c0 + 128].reshape(8, 128, 128).transpose(1, 0, 2).reshape(128, 1024)


def _tmgroup(W, cols):
    n = len(cols)
    return W[:, cols].reshape(8, 128, n).transpose(1, 0, 2).reshape(128, 8 * n)


def _pad(a):
    o = np.zeros((128, 4096), np.float32)
    o[:, :a.shape[1]] = a
    return o


def prep_shared(inp, DEPTH):
    f = np.float32
    ws, pvs, pms = [], [], []
    q0, f0, i0, g0, z0, x0, d0, u0 = [int(v) for v in SPL[:8]]
    for l in range(DEPTH):
        W = np.asarray(inp['w_in'][l], f)
        pcs = []
        pcs.append(np.concatenate([_fmchunk(W, q0), _fmchunk(W, q0 + 128), _fmchunk(W, q0 + 256), _fmchunk(W, f0)], 1))
        pcs.append(np.concatenate([_fmchunk(W, f0 + 128), _fmchunk(W, f0 + 256)], 1))
        pcs.append(_tmgroup(W, np.arange(i0, i0 + 384)))
        pcs.append(_tmgroup(W, np.arange(g0, g0 + 384)))
        pcs.append(np.concatenate([_fmchunk(W, x0 + 128 * c) for c in range(4)], 1))
        pcs.append(np.concatenate([_fmchunk(W, x0 + 128 * c) for c in range(4, 7)], 1))
        pcs.append(_tmgroup(W, np.concatenate([np.arange(z0, z0 + 384), np.arange(d0, d0 + 6)])))
        pcs.append(np.concatenate([_fmchunk(W, u0), _fmchunk(W, u0 + 128)], 1))
        Wo = np.asarray(inp['w_out'][l], f)
        pcs.append(np.concatenate([_fmchunk(Wo, 128 * n) for n in range(4)], 1))
        pcs.append(np.concatenate([_fmchunk(Wo, 128 * n) for n in range(4, 8)], 1))
        W1 = np.asarray(inp['w_mlp_in'][l], f)
        for hg in range(8):
            pcs.append(np.concatenate([_fmchunk(W1, 128 * (4 * hg + hc)) for hc in range(4)], 1))
        W2 = np.asarray(inp['w_mlp_out'][l], f)
        for n in range(8):
            pcs.append(W2[:, n * 128:(n + 1) * 128].reshape(32, 128, 128).transpose(1, 0, 2).reshape(128, 4096))
        assert len(pcs) == NPIECE
        ws.append(np.stack([_pad(p) for p in pcs]))
        pvl = np.zeros((128, NPV), f)
        cw = np.asarray(inp['m2_conv_w'][l], f)
        pvl[:, 0:28] = cw.reshape(4, 7, 128).transpose(2, 1, 0).reshape(128, 28)
        pvl[:, 28:35] = np.asarray(inp['m2_conv_b'][l], f).reshape(7, 128).T
        for nm, o in (('ln1_g', 35), ('ln1_b', 43), ('ln2_g', 51), ('ln2_b', 59)):
            pvl[:, o:o + 8] = np.asarray(inp[nm][l], f).reshape(8, 128).T
        pvl[:, 67:69] = np.asarray(inp['s5_glu_b'][l], f).reshape(2, 128).T
        pvl[:, 69:71] = np.asarray(inp['s5_d'][l], f).reshape(2, 128).T
        pvl[:, 71:79] = np.asarray(inp['s5_a_re'][l], f).reshape(1024).reshape(8, 128).T
        pvl[:, 79:87] = np.asarray(inp['s5_a_im'][l], f).reshape(1024).reshape(8, 128).T
        pvl[:, 87:95] = np.repeat(np.asarray(inp['s5_log_dt'][l], f), 64).reshape(8, 128).T
        pvl[:, 95:101] = np.asarray(inp['m2_dt_bias'][l], f)[None, :]
        pvl[:, 101:107] = np.asarray(inp['m2_a_log'][l], f)[None, :]
        pvl[:, 107:113] = np.asarray(inp['m2_d'][l], f)[None, :]
        pvl[:, 113:116] = np.tile(np.asarray(inp['hgrn_norm_w'][l], f), 6).reshape(3, 128).T
        pvl[:, 116:119] = np.asarray(inp['m2_norm_w'][l], f).reshape(3, 128).T
        pvl[:, 119:122] = np.repeat(np.asarray(inp['m2_d'][l], f), 64).reshape(3, 128).T
        pvs.append(pvl)
        pm = np.zeros((5, 128, 1024), f)
        for ri, nm in enumerate(('s5_b_re', 's5_b_im')):
            bb = np.asarray(inp[nm][l], f)
            for g in range(16):
                pm[ri, (g % 8) * 16:(g % 8) * 16 + 16, g * 64:(g + 1) * 64] = bb[g].T
        for ri, nm in enumerate(('s5_c_re', 's5_c_im')):
            cc = np.asarray(inp[nm][l], f)
            ct = pm[2 + ri].reshape(128, 8, 128)
            for g in range(16):
                j, ph = g // 2, (g % 2) * 64
                ct[ph:ph + 64, j, (g % 8) * 16:(g % 8) * 16 + 16] = cc[g].T
        gw = np.asarray(inp['s5_glu_w'][l], f)
        pm[4, :, 0:512] = gw.reshape(2, 128, 256).transpose(1, 0, 2).reshape(128, 512)
        pms.append(pm)
    wsrc = np.concatenate(ws, 0).reshape(DEPTH * NPIECE, 128, 4, 1024).transpose(0, 2, 1, 3)
    wsrc = np.ascontiguousarray(wsrc).reshape(DEPTH * NPIECE * 4, 128, 1024)
    lbl = np.asarray(inp['hgrn_lb_logits'], f).reshape(4, 3, 128).transpose(2, 1, 0).reshape(128, 12)
    s = np.arange(128)
    tri = (s[:, None] <= s[None, :]).astype(f)
    tau = np.arange(384)
    rst = np.tile(((tau % 64) != 0).astype(f)[None, :], (128, 1))
    ramp = np.tile((tau + 1).astype(f)[None, :], (128, 1))
    cfv = np.concatenate([tri, rst, ramp, np.eye(128, dtype=f)], 1)
    negm = np.where(s[:, None] <= s[None, :], 0.0, -30000.0).astype(f)
    mask2 = ((s[:, None] <= s[None, :]) & ((s[:, None] // 64) == (s[None, :] // 64))).astype(f)
    cbv = np.concatenate([np.eye(128, dtype=f), negm, mask2], 1)
    return dict(wsrc=wsrc, pv=np.stack(pvs), pm=np.concatenate(pms, 0), lbl=np.ascontiguousarray(lbl),
                cf=np.ascontiguousarray(cfv), cb=np.ascontiguousarray(cbv))


def run(inp, NBLK, DEPTH, ncores):
    x = np.asarray(inp['x'], np.float32)
    meta = np.asarray(inp['meta_tokens'], np.float32)
    LT = NBLK * TB
    shared = prep_shared(inp, DEPTH)
    in_maps = []
    for b in range(ncores):
        hf = np.zeros((LT, 1024), np.float32)
        hf[0:16] = meta
        hf[16:16 + x.shape[1]] = x[b]
        h0 = np.ascontiguousarray(hf.T.reshape(8, 128, LT).transpose(1, 0, 2))
        m = dict(shared)
        m['h0'] = h0
        in_maps.append(m)
    nc = build(NBLK, DEPTH)
    res = run_bass_kernel_spmd(nc, in_maps, core_ids=list(range(ncores)))
    outs = []
    for b in range(ncores):
        o = np.asarray(res.results[b]['out'], np.float32)
        hf = o.transpose(1, 0, 2).reshape(1024, LT).T
        outs.append(hf[16:16 + x.shape[1]])
    return np.stack(outs).astype(np.float32)


def kernel(**inputs):
    return run(inputs, 11, 4, 4)
```

```python
import math
from contextlib import ExitStack
import numpy as np
import concourse.bass as bass
import concourse.mybir as mybir
from concourse.bass_utils import run_bass_kernel_spmd

F32 = mybir.dt.float32
BF16 = mybir.dt.bfloat16
AF = mybir.ActivationFunctionType
ALU = mybir.AluOpType
AX = mybir.AxisListType

TB = 384
NT = 3
NPIECE = 26
ALPHA = 8.0 ** 0.25
LN_EPS = 1e-5
RMS_EPS = 1e-6
PI = math.pi
NPV = 122
NA = 16
NB = 40
NWS = 3
ND = 24


class Prog:
    def __init__(self):
        self.ops = []
        self.lastw = {}
        self.rd = {}

    def add(self, eng, fn, r=(), w=(), dma=False):
        i = len(self.ops)
        raw, other = set(), set()
        for b in r:
            j = self.lastw.get(b)
            if j is not None:
                raw.add(j)
        for b in w:
            j = self.lastw.get(b)
            if j is not None:
                other.add(j)
            for j in self.rd.get(b, {}).values():
                if isinstance(j, list):
                    other.update(j)
                else:
                    other.add(j)
        deps = set()
        for j in raw | other:
            oj = self.ops[j]
            if oj['dma']:
                deps.add(j)
            elif oj['eng'] == eng and not dma:
                if eng == 'pe':
                    continue
                deps.add(j)
            else:
                deps.add(j)
        for b in r:
            d = self.rd.setdefault(b, {})
            if dma:
                d.setdefault('dma', []).append(i)
            else:
                d[eng] = i
        for b in w:
            self.lastw[b] = i
            self.rd[b] = {}
        self.ops.append(dict(eng=eng, fn=fn, deps=deps, dma=dma, sig=False))
        return i


class _Stop(Exception):
    pass


STAGE = 99
FFN_DEFER = 2
HG_LAG = 2
GELU_ACT = True
VAR = 0
DEBUG = False
TCUT = 0


def build(NBLK, DEPTH):
    LT = NBLK * TB
    nc = bass.Bass("TRN2", target_bir_lowering=False)
    P = Prog()
    es = ExitStack()

    def dram(name, shape, dt, kind):
        return nc.dram_tensor(name, shape, dt, kind=kind).ap()

    h0 = dram("h0", [128, 8, LT], F32, "ExternalInput")
    wsrc = dram("wsrc", [DEPTH * NPIECE * 4, 128, 1024], F32, "ExternalInput")
    pvd = dram("pv", [DEPTH, 128, NPV], F32, "ExternalInput")
    pmd = dram("pm", [DEPTH * 5, 128, 1024], F32, "ExternalInput")
    lbd = dram("lbl", [128, 12], F32, "ExternalInput")
    cfd = dram("cf", [128, 1024], F32, "ExternalInput")
    cbd = dram("cb", [128, 384], F32, "ExternalInput")
    outd = dram("out", [128, 8, LT], F32, "ExternalOutput")
    wbf = dram("wbf", [DEPTH * NPIECE, 128, 4096], BF16, "Internal")
    dbgd = dram("dbg", [NBLK * 16, 128, TB], F32, "ExternalOutput") if DEBUG else None

    def sb(name, shape, dt):
        return es.enter_context(nc.sbuf_tensor(name, shape, dt))

    hbf = sb("hbf", [128, 8, LT], BF16)
    wslot = [sb(f"wslot{i}", [128, 4096], BF16) for i in range(NWS)]
    Fre = sb("Fre", [128, 8, TB], BF16)
    nFim = sb("nFim", [128, 8, TB], BF16)
    cvf = [sb(f"cvf{i}", [128, 1024], F32) for i in range(2)]
    cvb = [sb(f"cvb{i}", [128, 1024], BF16) for i in range(2)]
    pv = sb("pvs", [128, NPV], F32)
    btbf = sb("btbf", [128, 2, 1024], BF16)
    ctbf = sb("ctbf", [128, 3, 1024], BF16)
    gwbf = sb("gwbf", [128, 512], BF16)
    cf = sb("cfs", [128, 1024], F32)
    cb = sb("cbs", [128, 384], BF16)
    onesb = sb("onesb", [128, 128], BF16)
    onesf = sb("onesf", [128, 128], F32)
    lbt = sb("lbt", [128, 3, 4], F32)
    lbw = sb("lbw", [128, 3, 4], F32)
    lba = sb("lba", [128, 3, DEPTH, 3], F32)
    s5s = sb("s5s", [128, 16, 8], F32)
    s5c = sb("s5c", [128, 2, 8], F32)
    hS = [sb(f"hS{i}", [128, 3, 64], F32) for i in range(3)]
    hSb = [sb(f"hSb{i}", [128, 3, 64], BF16) for i in range(3)]
    egl = sb("egl", [128, 3, 8], F32)
    Sm = sb("Sm", [128, 6, 64], F32)
    Smb = sb("Smb", [128, 6, 64], BF16)
    XB = sb("XB", [128, 7, 388], BF16)
    dgw = sb("dgw", [128, 28, 128], BF16)
    dgd = sb("dgd", [128, 3, 128], BF16)
    sm = sb("smalls", [128, 16, 16], F32)
    abc = sb("abc", [128, 8], F32)
    dtt = sb("dtt", [128, 3, 16], F32)
    acs = sb("acs", [128, 3, 48], F32)
    hgb = sb("hgb", [128, 2], F32)
    pib = sb("pib", [128, 2], F32)
    mcol = sb("mcol", [128, 2], F32)
    nident = sb("nident", [128, 128], BF16)
    negmf = sb("negmf", [128, 128], F32)
    Apool = [sb(f"A{i}", [128, TB], F32) for i in range(NA)]
    Bpool = [sb(f"B{i}", [128, TB], BF16) for i in range(NB)]
    print("sbuf bytes remaining", nc.sbuf_bytes_remaining)
    psum = [es.enter_context(nc.psum_tensor(f"ps{i}", [128, 512], F32)) for i in range(8)]

    tri = cf[:, 0:128]
    rstm = cf[:, 128:512]
    ramp = cf[:, 512:896]
    identf = cf[:, 896:1024]
    ident = cb[:, 0:128]
    negm = cb[:, 128:256]
    mask2 = cb[:, 256:384]

    class Pool:
        def __init__(self, tiles, nm):
            self.free = [(t, (nm, i)) for i, t in enumerate(tiles)]
            self.nm = nm

        def get(self):
            if not self.free:
                raise RuntimeError("pool empty " + self.nm)
            it = self.free.pop(0)
            self.minfree = min(getattr(self, 'minfree', 999), len(self.free))
            return it

        def put(self, *its):
            for it in its:
                self.free.append(it)

    A = Pool(Apool, 'A')
    B = Pool(Bpool, 'B')
    PS = Pool(psum, 'ps')
    smi = [0]

    def small():
        i = smi[0] % 16
        smi[0] += 1
        return sm[:, i, :], ('sm', i)

    def pe_mm(out, lhsT, rhs, start, stop, r, w):
        P.add('pe', lambda e: e.matmul(out, lhsT, rhs, start=start, stop=stop), r=r, w=w)

    def pe_tr(out, in_, r, w):
        P.add('pe', lambda e: e.transpose(out, in_, ident), r=list(r) + ['cb'], w=w)

    def act(out, in_, func, r, w, bias=0.0, scale=1.0):
        P.add('act', lambda e: e.activation(out, in_, func, bias=bias, scale=scale), r=r, w=w)

    def tt(eng, out, a, b, op, r, w):
        P.add(eng, lambda e: e.tensor_tensor(out, a, b, op), r=r, w=w)

    def ts(eng, out, a, s1, s2, op0, op1, r, w):
        if op1 is None:
            P.add(eng, lambda e: e.tensor_single_scalar(out, a, s1, op0), r=r, w=w)
        else:
            P.add(eng, lambda e: e.tensor_scalar(out, a, s1, s2, op0, op1), r=r, w=w)

    def stt(eng, out, a, s, b, op0, op1, r, w):
        P.add(eng, lambda e: e.scalar_tensor_tensor(out, a, s, b, op0, op1), r=r, w=w)

    def cp(eng, out, in_, r, w):
        if eng == 'act':
            act(out, in_, AF.Copy, r, w)
        else:
            P.add(eng, lambda e: e.tensor_copy(out, in_), r=r, w=w)

    def dma(out, in_, r, w):
        P.add('sp', lambda e: e.dma_start(out=out, in_=in_), r=r, w=w, dma=True)

    dbg_names = []

    def dump(name, ap, key, n=TB):
        if not DEBUG:
            return
        dt_ = dram("dbg_" + name, [128, n], F32, "ExternalOutput")
        dbg_names.append("dbg_" + name)
        a, ka = A.get()
        cp('dve', a[:, 0:n], ap, r=[key], w=[ka])
        dma(dt_[:, :], a[:, 0:n], r=[ka], w=[('dbgx', name)])
        A.put((a, ka))

    def memset(eng, ap, val, w):
        P.add(eng, lambda e: e.memset(ap, val), r=(), w=w)

    cvi = [0]

    pending = []

    def flush_convert():
        while pending:
            pending.pop(0)()

    def conv_load(l, q):
        i = cvi[0] % 2
        eng = ['act', 'dve'][cvi[0] % 2]
        cvi[0] += 1
        dma(cvf[i][:, :], wsrc[(l * NPIECE * 4 + q)], r=[], w=[('cvf', i)])
        return (l, q, i, eng)

    def conv_finish(u):
        l, q, i, eng = u
        pi, qq = q // 4, q % 4
        cp(eng, cvb[i][:, :], cvf[i][:, :], r=[('cvf', i)], w=[('cvb', i)])
        while pending:
            pending.pop(0)()
        pending.append(lambda: dma(wbf[l * NPIECE + pi][:, qq * 1024:(qq + 1) * 1024], cvb[i][:, :],
                                   r=[('cvb', i)], w=[('wbf', l, pi, qq)]))

    def convert_unit(l, q):
        conv_finish(conv_load(l, q))

    conv_queue = []
    conv_inflight = []

    def conv_pump():
        if conv_inflight:
            conv_finish(conv_inflight.pop(0))
        if conv_queue:
            conv_inflight.append(conv_load(*conv_queue.pop(0)))

    def conv_drain():
        while conv_queue or conv_inflight:
            conv_pump()

    sched = []
    for l in range(DEPTH):
        for b in range(NBLK):
            sched += [(l, pi) for pi in range(10)]
        for b in range(NBLK):
            sched += [(l, pi) for pi in range(10, 26)]
    st = dict(next_load=0, next_use=0)

    def issue_loads(upto):
        while st['next_load'] <= min(upto, len(sched) - 1):
            k = st['next_load']
            l, pi = sched[k]
            s = k % NWS
            dma(wslot[s][:, :], wbf[l * NPIECE + pi], r=[('wbf', l, pi, qq) for qq in range(4)], w=[('w', s)])
            st['next_load'] += 1

    def get_piece(l, pi):
        k = st['next_use']
        assert sched[k] == (l, pi), (sched[k], l, pi)
        issue_loads(k + NWS - 1)
        st['next_use'] += 1
        return wslot[k % NWS], ('w', k % NWS)

    dma(cf[:, :], cfd[:, :], r=[], w=['cf'])
    dma(cvf[0][:, 0:384], cbd[:, :], r=[], w=[('cvf', 0)])
    cp('dve', cb[:, :], cvf[0][:, 0:384], r=[('cvf', 0)], w=['cb'])
    cp('dve', negmf[:, :], cvf[0][:, 128:256], r=[('cvf', 0)], w=['negmf'])
    ts('dve', nident[:, :], ident, -1.0, None, ALU.mult, None, r=['cb'], w=['nident'])
    memset('pool', onesb[:, :], 1.0, w=['onesb'])
    memset('pool', onesf[:, :], 1.0, w=['onesf'])
    memset('pool', pib[:, 0:1], PI, w=['pib'])
    memset('pool', mcol[0:64, 0:1], 1.0, w=['mcol'])
    memset('pool', mcol[64:128, 0:1], 0.0, w=['mcol'])
    memset('pool', mcol[0:64, 1:2], 0.0, w=['mcol'])
    memset('pool', mcol[64:128, 1:2], 1.0, w=['mcol'])
    memset('pool', pib[:, 1:2], -PI, w=['pib'])
    dma(lbt[:, :, :], lbd.rearrange("p (a b) -> p a b", a=3), r=[], w=['lbt'])
    act(lbw[:, :, :], lbt[:, :, :], AF.Exp, r=['lbt'], w=['lbw'])
    ssum, ksum = small()
    P.add('dve', lambda e: e.tensor_reduce(ssum[:, 0:3], lbw[:, :, :], AX.X, ALU.add), r=['lbw'], w=[ksum])
    rs, krs = small()
    P.add('dve', lambda e: e.reciprocal(rs[:, 0:3], ssum[:, 0:3]), r=[ksum], w=[krs])
    tt('dve', lbt[:, :, :], lbw[:, :, :], rs[:, 0:3].unsqueeze(2).broadcast_to([128, 3, 4]), ALU.mult,
       r=['lbw', krs], w=['lbt'])
    memset('dve', lbw[:, :, 0:1], 0.0, w=['lbw'])
    for l in range(1, 4):
        tt('dve', lbw[:, :, l:l + 1], lbw[:, :, l - 1:l], lbt[:, :, l:l + 1], ALU.add, r=['lbw', 'lbt'], w=['lbw'])
    for l in range(DEPTH):
        ts('dve', lba[:, :, l, 0:1], lbw[:, :, l:l + 1], 0.5, 0.5, ALU.mult, ALU.add, r=['lbw'], w=['lba'])
        ts('dve', lba[:, :, l, 1:2], lbw[:, :, l:l + 1], -0.5, 0.5, ALU.mult, ALU.add, r=['lbw'], w=['lba'])
        ts('dve', lba[:, :, l, 2:3], lbw[:, :, l:l + 1], 0.5, -0.5, ALU.mult, ALU.add, r=['lbw'], w=['lba'])

    for b in range(NBLK):
        for k in range(8):
            a, ka = A.get()
            dma(a[:, :], h0[:, k, b * TB:(b + 1) * TB], r=[], w=[ka])
            cp(['act', 'dve', 'pool'][k % 3], hbf[:, k, b * TB:(b + 1) * TB], a[:, :], r=[ka], w=[('h', b, k)])
            A.put((a, ka))

    for q in range(40):
        convert_unit(0, q)
    flush_convert()

    OFF = dict(convw=0, convb=28, ln1g=35, ln1b=43, ln2g=51, ln2b=59, glub=67, s5d=69, are=71, aim=79, ldt=87,
               dtb=95, alog=101, md=107, hnw=113, mnw=116, mdc=119)

    def pvc(name, i=0, n=1):
        o = OFF[name] + i
        return pv[:, o:o + n]

    def layer_setup(l):
        dma(pv[:, :], pvd[l], r=[], w=['pv'])
        for m in (0, 1, 4):
            i = cvi[0] % 2
            cvi[0] += 1
            dma(cvf[i][:, :], pmd[l * 5 + m], r=[], w=[('cvf', i)])
            if m < 2:
                cp('dve', btbf[:, m, :], cvf[i][:, :], r=[('cvf', i)], w=['btbf'])
            else:
                cp('dve', gwbf[:, :], cvf[i][:, 0:512], r=[('cvf', i)], w=['gwbf'])
        S = lambda i: s5s[:, i, :]
        ts('dve', S(0), pvc('are', 0, 8), -1e-4, None, ALU.min, None, r=['pv'], w=['s5s'])
        act(S(1), pvc('ldt', 0, 8), AF.Exp, r=['pv'], w=['s5s'])
        tt('dve', S(2), S(0), S(1), ALU.mult, r=['s5s'], w=['s5s'])
        act(S(3), S(2), AF.Exp, r=['s5s'], w=['s5s'])
        tt('dve', S(4), pvc('aim', 0, 8), S(1), ALU.mult, r=['pv', 's5s'], w=['s5s'])
        C1 = 6.28125
        C2 = 2 * PI - 6.28125
        I32 = mybir.dt.int32

        def reduce_to_pi(src, ksrc):
            qi, kqi = A.get()
            kf, kkf = A.get()
            ph, kph = A.get()
            ts('dve', qi[:, :].bitcast(I32), src[:, :], 1.0 / (2 * PI), None, ALU.mult, None, r=[ksrc], w=[kqi])
            cp('dve', kf[:, :], qi[:, :].bitcast(I32), r=[kqi], w=[kkf])
            stt('dve', ph[:, :], kf[:, :], -C1, src[:, :], ALU.mult, ALU.add, r=[kkf, ksrc], w=[kph])
            stt('dve', ph[:, :], kf[:, :], -C2, ph[:, :], ALU.mult, ALU.add, r=[kkf, kph], w=[kph])
            ts('dve', qi[:, :], ph[:, :], PI, 2 * PI, ALU.is_gt, ALU.mult, r=[kph], w=[kqi])
            tt('dve', ph[:, :], ph[:, :], qi[:, :], ALU.subtract, r=[kph, kqi], w=[kph])
            ts('dve', qi[:, :], ph[:, :], -PI, 2 * PI, ALU.is_lt, ALU.mult, r=[kph], w=[kqi])
            tt('dve', ph[:, :], ph[:, :], qi[:, :], ALU.add, r=[kph, kqi], w=[kph])
            A.put((qi, kqi), (kf, kkf))
            return ph, kph

        for j in range(8):
            a1, k1 = A.get()
            a2, k2 = A.get()
            ts('dve', a1[:, :], ramp, s5s[:, 4, j:j + 1], None, ALU.mult, None, r=['cf', 's5s'], w=[k1])
            ts('dve', a2[:, :], a1[:, :], 0.5 * PI, None, ALU.add, None, r=[k1], w=[k2])
            ph, kph = reduce_to_pi(a1, k1)
            act(nFim[:, j, :], ph[:, :], AF.Sin, r=[kph], w=[('nF', j)], scale=-1.0)
            A.put((ph, kph))
            ph, kph = reduce_to_pi(a2, k2)
            act(Fre[:, j, :], ph[:, :], AF.Sin, r=[kph], w=[('Fr', j)])
            A.put((ph, kph))
            A.put((a1, k1), (a2, k2))
        allF = [('Fr', j) for j in range(8)] + [('nF', j) for j in range(8)]
        tt('dve', S(5), S(3), Fre[:, :, 0], ALU.mult, r=['s5s'] + allF, w=['s5s'])
        tt('dve', S(6), S(3), nFim[:, :, 0], ALU.mult, r=['s5s'] + allF, w=['s5s'])
        tt('dve', S(7), S(0), S(0), ALU.mult, r=['s5s'], w=['s5s'])
        tt('dve', S(8), pvc('aim', 0, 8), pvc('aim', 0, 8), ALU.mult, r=['pv'], w=['s5s'])
        tt('dve', S(7), S(7), S(8), ALU.add, r=['s5s'], w=['s5s'])
        P.add('dve', lambda e: e.reciprocal(S(8), S(7)), r=['s5s'], w=['s5s'])
        ts('dve', S(9), S(5), -1.0, None, ALU.add, None, r=['s5s'], w=['s5s'])
        tt('dve', S(10), S(9), S(0), ALU.mult, r=['s5s'], w=['s5s'])
        tt('dve', S(11), S(6), pvc('aim', 0, 8), ALU.mult, r=['s5s', 'pv'], w=['s5s'])
        tt('dve', S(10), S(10), S(11), ALU.subtract, r=['s5s'], w=['s5s'])
        tt('dve', S(10), S(10), S(8), ALU.mult, r=['s5s'], w=['s5s'])
        tt('dve', S(11), S(6), S(0), ALU.mult, r=['s5s'], w=['s5s'])
        tt('dve', S(12), S(9), pvc('aim', 0, 8), ALU.mult, r=['s5s', 'pv'], w=['s5s'])
        tt('dve', S(11), S(11), S(12), ALU.add, r=['s5s'], w=['s5s'])
        tt('dve', S(11), S(11), S(8), ALU.mult, r=['s5s'], w=['s5s'])
        ts('dve', S(11), S(11), -1.0, None, ALU.mult, None, r=['s5s'], w=['s5s'])
        ire = cvi[0] % 2
        iim = 1 - ire
        cvi[0] += 2
        dma(cvf[ire][:, :], pmd[l * 5 + 2], r=[], w=[('cvf', ire)])
        dma(cvf[iim][:, :], pmd[l * 5 + 3], r=[], w=[('cvf', iim)])
        for j0, j1 in ((0, 3), (3, 6), (6, 8)):
            n = j1 - j0

            def g3(ap):
                return ap.rearrange("p (j c) -> p j c", j=n)
            sre = s5s[:, 10, j0:j1].unsqueeze(2).broadcast_to([128, n, 128])
            sim = s5s[:, 11, j0:j1].unsqueeze(2).broadcast_to([128, n, 128])
            cre = g3(cvf[ire][:, j0 * 128:j1 * 128])
            cim = g3(cvf[iim][:, j0 * 128:j1 * 128])
            obr = g3(ctbf[:, 0, j0 * 128:j1 * 128])
            obi = g3(ctbf[:, 1, j0 * 128:j1 * 128])
            a1, k1 = A.get()
            a2, k2 = A.get()
            t1 = g3(a1[:, 0:n * 128])
            t2 = g3(a2[:, 0:n * 128])
            tt('dve', t1, cre, sre, ALU.mult, r=[('cvf', ire), 's5s'], w=[k1])
            tt('dve', t2, cim, sim, ALU.mult, r=[('cvf', iim), 's5s'], w=[k2])
            tt('dve', obr, t1, t2, ALU.subtract, r=[k1, k2], w=['ctbf'])
            tt('dve', t1, cre, sim, ALU.mult, r=[('cvf', ire), 's5s'], w=[k1])
            tt('dve', t2, cim, sre, ALU.mult, r=[('cvf', iim), 's5s'], w=[k2])
            tt('dve', obi, t1, t2, ALU.add, r=[k1, k2], w=['ctbf'])
            ts('dve', g3(ctbf[:, 2, j0 * 128:j1 * 128]), obi, -1.0, None, ALU.mult, None, r=['ctbf'], w=['ctbf'])
            A.put((a1, k1), (a2, k2))
        for cj in range(28):
            ts('pool' if cj % 2 else 'dve', dgw[:, cj, :], ident, pvc('convw', cj), None, ALU.mult, None,
               r=['cb', 'pv'], w=['dgw'])
        for c in range(3):
            ts('dve', dgd[:, c, :], ident, pvc('mdc', c), None, ALU.mult, None, r=['cb', 'pv'], w=['dgd'])
        act(abc[:, 0:6], pvc('alog', 0, 6), AF.Exp, r=['pv'], w=['abc'])
        ts('dve', abc[:, 0:6], abc[:, 0:6], -1.0, None, ALU.mult, None, r=['abc'], w=['abc'])
        ts('dve', hgb[:, 0:2], pvc('glub', 0, 2), 0.5, None, ALU.mult, None, r=['pv'], w=['hgb'])
        for i in range(3):
            memset('pool', hS[i][:, :, :], 0.0, w=[('hS', i)])
            memset('pool', hSb[i][:, :, :], 0.0, w=[('hSb', i)])
        memset('pool', Sm[:, :, :], 0.0, w=['Sm'])
        memset('pool', Smb[:, :, :], 0.0, w=['Smb'])
        memset('pool', s5c[:, :, :], 0.0, w=['s5c'])
        memset('pool', XB[:, :, 0:3], 0.0, w=[('XB', c) for c in range(7)])

    FMMAP = {}
    for pc in range(3):
        FMMAP[('q', pc)] = (0, pc * 1024)
    FMMAP[('f', 0)] = (0, 3072)
    FMMAP[('f', 1)] = (1, 0)
    FMMAP[('f', 2)] = (1, 1024)
    for c in range(7):
        FMMAP[('x', c)] = (4, c * 1024) if c < 4 else (5, (c - 4) * 1024)
    for m in range(2):
        FMMAP[('u', m)] = (7, m * 1024)

    def hk(b, k):
        return ('h', b, k)

    def fm_proj(b, slot, kslot, off):
        ps, kp = PS.get()
        for k in range(8):
            pe_mm(ps[:, 0:TB], slot[:, off + k * 128: off + (k + 1) * 128], hbf[:, k, b * TB:(b + 1) * TB],
                  k == 0, k == 7, r=[kslot, hk(b, k)], w=[kp])
        return ps, kp

    def tm_proj(b, t, slot, kslot, n):
        ps, kp = PS.get()
        c0 = b * TB + t * 128
        for k in range(8):
            pe_mm(ps[:, 0:n], hbf[:, k, c0:c0 + 128], slot[:, k * n:(k + 1) * n], k == 0, k == 7,
                  r=[kslot, hk(b, k)], w=[kp])
        return ps, kp

    def v3(ap, h):
        return ap.rearrange("p (h v) -> p h v", h=h)

    def bc(ap, n, m):
        return ap.unsqueeze(2).broadcast_to([128, n, m])

    def rstd_small(ss, kss, n, inv, eps):
        ts('dve', ss[:, 0:n], ss[:, 0:n], inv, eps, ALU.mult, ALU.add, r=[kss], w=[kss])
        act(ss[:, 0:n], ss[:, 0:n], AF.Ln, r=[kss], w=[kss])
        act(ss[:, 0:n], ss[:, 0:n], AF.Exp, r=[kss], w=[kss], scale=-0.5)

    def layernorm(rr, gname, bname, dst_fn):
        p1, k1 = PS.get()
        p2, k2 = PS.get()
        for n in range(8):
            rb, krb = B.get()
            rq, krq = B.get()
            cp('act', rb[:, :], rr[n][0][:, :], r=[rr[n][1]], w=[krb])
            act(rq[:, :], rr[n][0][:, :], AF.Square, r=[rr[n][1]], w=[krq])
            pe_mm(p1[:, 0:TB], onesb[:, :], rb[:, :], n == 0, n == 7, r=['onesb', krb], w=[k1])
            pe_mm(p2[:, 0:TB], onesb[:, :], rq[:, :], n == 0, n == 7, r=['onesb', krq], w=[k2])
            B.put((rb, krb), (rq, krq))
        mean, kme = A.get()
        msq, kms = A.get()
        rstd, krs_ = A.get()
        ts('dve', mean[:, :], p1[:, 0:TB], 1.0 / 1024, None, ALU.mult, None, r=[k1], w=[kme])
        tt('pool', msq[:, :], mean[:, :], mean[:, :], ALU.mult, r=[kme], w=[kms])
        stt('dve', rstd[:, :], p2[:, 0:TB], 1.0 / 1024, msq[:, :], ALU.mult, ALU.subtract, r=[k2, kms], w=[krs_])
        ts('dve', rstd[:, :], rstd[:, :], LN_EPS, None, ALU.add, None, r=[krs_], w=[krs_])
        act(rstd[:, :], rstd[:, :], AF.Ln, r=[krs_], w=[krs_])
        act(rstd[:, :], rstd[:, :], AF.Exp, r=[krs_], w=[krs_], scale=-0.5)
        PS.put((p1, k1), (p2, k2))
        for n in range(8):
            t1, kt1 = A.get()
            tt('pool', t1[:, :], rr[n][0][:, :], mean[:, :], ALU.subtract, r=[rr[n][1], kme], w=[kt1])
            tt('dve', t1[:, :], t1[:, :], rstd[:, :], ALU.mult, r=[kt1, krs_], w=[kt1])
            dst, kd, post = dst_fn(n)
            act(dst, t1[:, :], AF.Identity, r=[kt1, 'pv'], w=kd, bias=pvc(bname, n), scale=pvc(gname, n))
            A.put((t1, kt1))
            if post is not None:
                post()
        A.put((mean, kme), (msq, kms), (rstd, krs_))

    def mixer_block(l, b):
        c0 = b * TB
        yT = [B.get() for _ in range(8)]
        wmap = {}

        def wp_(pi):
            if pi not in wmap:
                wmap[pi] = get_piece(l, pi)
            return wmap[pi]
        qs, tth = [], []
        for pc in range(3):
            pi, off = FMMAP[('q', pc)]
            ps, kp = fm_proj(b, wp_(pi)[0], wp_(pi)[1], off)
            a, ka = A.get()
            act(a[:, :], ps[:, 0:TB], AF.Silu, r=[kp], w=[ka])
            PS.put((ps, kp))
            qs.append((a, ka))
        for pc in range(3):
            pi, off = FMMAP[('f', pc)]
            ps, kp = fm_proj(b, wp_(pi)[0], wp_(pi)[1], off)
            a, ka = A.get()
            act(a[:, :], ps[:, 0:TB], AF.Tanh, r=[kp], w=[ka], scale=0.5)
            PS.put((ps, kp))
            tth.append((a, ka))
        if STAGE <= 2:
            raise _Stop()
        qT, kT = [], []
        for pc in range(3):
            t_, kt = tth[pc]
            lf, klf = A.get()
            act(lf[:, :], t_[:, :], AF.Ln, r=[kt, 'lba'], w=[klf], bias=lba[:, pc, l, 0:1], scale=lba[:, pc, l, 1:2])
            kk, kkk = A.get()
            ts('dve', kk[:, :], t_[:, :], lba[:, pc, l, 2:3], lba[:, pc, l, 1:2], ALU.mult, ALU.add,
               r=[kt, 'lba'], w=[kkk])
            A.put(tth[pc])
            G, kG = A.get()
            P.add('dve', lambda e, G=G, lf=lf: e.tensor_tensor_scan(G[:, :], rstm, lf[:, :], 0.0, ALU.mult, ALU.add),
                  r=['cf', klf], w=[kG])
            A.put((lf, klf))
            eG, keG = A.get()
            enG, kenG = A.get()
            act(eG[:, :], G[:, :], AF.Exp, r=[kG], w=[keG])
            act(enG[:, :], G[:, :], AF.Exp, r=[kG], w=[kenG], scale=-1.0)
            if b == 0 and pc == 0 and l == 0:
                dump('G', G[:, :], kG)
                dump('qs', qs[pc][0][:, :], qs[pc][1])
                dump('kk', kk[:, :], kkk)
                dump('eG', eG[:, :], keG)
                dump('enG', enG[:, :], kenG)
            A.put((G, kG))
            cp('pool', egl[:, pc, 0:6], v3(eG[:, :], 6)[:, :, 63], r=[keG], w=['egl'])
            q_, kq = B.get()
            q2_, kq2 = B.get()
            k_, kk_ = B.get()
            stt('dve', q_[:, :], qs[pc][0][:, :], mcol[:, 0:1], eG[:, :], ALU.mult, ALU.mult, r=[qs[pc][1], keG, 'mcol'], w=[kq])
            stt('dve', q2_[:, :], qs[pc][0][:, :], mcol[:, 1:2], eG[:, :], ALU.mult, ALU.mult, r=[qs[pc][1], keG, 'mcol'], w=[kq2])
            tt('pool', k_[:, :], kk[:, :], enG[:, :], ALU.mult, r=[kkk, kenG], w=[kk_])
            A.put(qs[pc], (kk, kkk), (eG, keG), (enG, kenG))
            qT.append(((q_, kq), (q2_, kq2)))
            kT.append((k_, kk_))
        if STAGE <= 1:
            raise _Stop()
        wi = get_piece(l, 2)
        vb = []
        for t in range(NT):
            ps, kp = tm_proj(b, t, wi[0], wi[1], 384)
            v, kv = B.get()
            cp('act', v[:, :], ps[:, 0:384], r=[kp], w=[kv])
            PS.put((ps, kp))
            vb.append((v, kv))
        wg = get_piece(l, 3)
        sg = []
        for t in range(NT):
            ps, kp = tm_proj(b, t, wg[0], wg[1], 384)
            a, ka = A.get()
            act(a[:, :], ps[:, 0:384], AF.Silu, r=[kp], w=[ka])
            PS.put((ps, kp))
            sg.append((a, ka))
        if STAGE <= 3:
            raise _Stop()
        def hg_tile(t):
            if True:
                tc0 = t * 128
                pst, kpt = PS.get()
                pstb = pst.bitcast(BF16)
                for pc in range(3):
                    pe_tr(pstb[:, pc * 128:(pc + 1) * 128], kT[pc][0][:, tc0:tc0 + 128], r=[kT[pc][1]], w=[kpt])
                ktm, kktm = B.get()
                cp('dve', ktm[:, :], pstb[:, 0:384], r=[kpt], w=[kktm])
                PS.put((pst, kpt))
                yield
                pss = [PS.get(), PS.get()]
                for h in range(6):
                    pc, po = h // 2, 64 * (h % 2)
                    ps, kp = pss[h // 3]
                    qm, kqm = qT[pc][h % 2]
                    pe_mm(ps[:, (h % 3) * 128:(h % 3 + 1) * 128], kT[pc][0][:, tc0:tc0 + 128],
                          qm[:, tc0:tc0 + 128], True, True, r=[kT[pc][1], kqm], w=[kp])
                yield
                AT = [B.get(), B.get()]
                for g in range(2):
                    tt('dve', v3(AT[g][0][:, :], 3), v3(pss[g][0][:, 0:384], 3),
                       mask2.unsqueeze(1).broadcast_to([128, 3, 128]), ALU.mult, r=[pss[g][1], 'cb'], w=[AT[g][1]])
                    PS.put(pss[g])
                yield
                gci0 = (b * NT + t) * 2
                for c in range(2):
                    cur, nw = (gci0 + c) % 3, (gci0 + c + 1) % 3
                    ps, kp = PS.get()
                    r0 = 64 * c
                    for h in range(6):
                        pc, po = h // 2, 64 * (h % 2)
                        pe_mm(ps[po:po + 64, pc * 64:(pc + 1) * 64], ktm[r0:r0 + 64, h * 64:(h + 1) * 64],
                              vb[t][0][r0:r0 + 64, h * 64:(h + 1) * 64], True, True, r=[kktm, vb[t][1]], w=[kp])
                    tmp, ktmp = A.get()
                    tt('dve', v3(tmp[:, 0:192], 3), v3(ps[:, 0:192], 3), hS[cur][:, :, :], ALU.add,
                       r=[kp, ('hS', cur)], w=[ktmp])
                    PS.put((ps, kp))
                    tt('dve', hS[nw][:, :, :], v3(tmp[:, 0:192], 3),
                       egl[:, :, 2 * t + c:2 * t + c + 1].broadcast_to([128, 3, 64]), ALU.mult,
                       r=[ktmp, 'egl'], w=[('hS', nw)])
                    tt('dve', hSb[nw][:, :, :], v3(tmp[:, 0:192], 3),
                       egl[:, :, 2 * t + c:2 * t + c + 1].broadcast_to([128, 3, 64]), ALU.mult,
                       r=[ktmp, 'egl'], w=[('hSb', nw)])
                    A.put((tmp, ktmp))
                yield
                pso, kpo = PS.get()
                for h in range(6):
                    pc, po = h // 2, 64 * (h % 2)
                    s0, s1 = gci0 % 3, (gci0 + 1) % 3
                    pe_mm(pso[:, h * 64:(h + 1) * 64], AT[h // 3][0][:, (h % 3) * 128:(h % 3 + 1) * 128],
                          vb[t][0][:, h * 64:(h + 1) * 64], True, False, r=[AT[h // 3][1], vb[t][1]], w=[kpo])
                    qm, kqm = qT[pc][h % 2]
                    pe_mm(pso[0:64, h * 64:(h + 1) * 64], qm[:, tc0:tc0 + 64],
                          hSb[s0][:, pc, :], False, True, r=[kqm, ('hSb', s0)], w=[kpo])
                    pe_mm(pso[64:128, h * 64:(h + 1) * 64], qm[:, tc0 + 64:tc0 + 128],
                          hSb[s1][:, pc, :], False, True, r=[kqm, ('hSb', s1)], w=[kpo])
                B.put(*AT)
                B.put((ktm, kktm))
                yield
                sq, ksq = A.get()
                act(sq[:, :], pso[:, 0:384], AF.Square, r=[kpo], w=[ksq])
                ss, kss = small()
                P.add('dve', lambda e, ss=ss, sq=sq: e.tensor_reduce(ss[:, 0:6], v3(sq[:, :], 6), AX.X, ALU.add),
                      r=[ksq], w=[kss])
                rstd_small(ss, kss, 6, 1.0 / 64, RMS_EPS)
                tt('dve', v3(sq[:, :], 6), v3(pso[:, 0:384], 6), bc(ss[:, 0:6], 6, 64), ALU.mult, r=[kpo, kss], w=[ksq])
                PS.put((pso, kpo))
                yield
                yb_, kyb = B.get()
                tt('pool', yb_[:, :], sq[:, :], sg[t][0][:, :], ALU.mult, r=[ksq, sg[t][1]], w=[kyb])
                A.put((sq, ksq))
                pst, kpt = PS.get()
                pstb = pst.bitcast(BF16)
                for pc in range(3):
                    pe_tr(pstb[:, pc * 128:(pc + 1) * 128], yb_[:, pc * 128:(pc + 1) * 128], r=[kyb], w=[kpt])
                for pc in range(3):
                    ts('dve', yT[pc][0][:, tc0:tc0 + 128], pstb[:, pc * 128:(pc + 1) * 128], pvc('hnw', pc), None,
                       ALU.mult, None, r=[kpt, 'pv'], w=[yT[pc][1]])
                PS.put((pst, kpt))
                yield
                B.put((yb_, kyb))

        def hg_gen():
            tiles = [hg_tile(t) for t in range(NT)]
            started, live, step = 0, [], 0
            while started < NT or live:
                if started < NT and step % HG_LAG == 0:
                    live.append(tiles[started])
                    started += 1
                for g in list(live):
                    try:
                        next(g)
                    except StopIteration:
                        live.remove(g)
                step += 1
                yield
            for x in kT + vb:
                B.put(x)
            for x in qT:
                B.put(x[0], x[1])
            for x in sg:
                A.put(x)
            yield

        xc = []
        sz = []
        def prep_gen():
            allXB = [('XB', c) for c in range(7)]
            if b > 0:
                cp('dve', XB[:, :, 0:3], XB[:, :, 384:387], r=allXB, w=allXB)
            for c in range(7):
                pi, off = FMMAP[('x', c)]
                ps, kp = fm_proj(b, wp_(pi)[0], wp_(pi)[1], off)
                cp('act', XB[:, c, 3:387], ps[:, 0:TB], r=[kp], w=[('XB', c)])
                PS.put((ps, kp))
                yield
                ps2, kp2 = PS.get()
                for j in range(4):
                    pe_mm(ps2[:, 0:TB], dgw[:, c * 4 + j, :], XB[:, c, j:j + TB], j == 0, j == 3,
                          r=['dgw', ('XB', c)], w=[kp2])
                x_, kx = B.get()
                act(x_[:, :], ps2[:, 0:TB], AF.Silu, r=[kp2, 'pv'], w=[kx], bias=pvc('convb', c))
                PS.put((ps2, kp2))
                xc.append((x_, kx))
                yield
            w6 = get_piece(l, 6)
            for t in range(NT):
                ps, kp = tm_proj(b, t, w6[0], w6[1], 390)
                a, ka = A.get()
                act(a[:, :], ps[:, 0:384], AF.Silu, r=[kp], w=[ka])
                cp('act', dtt[:, t, 0:6], ps[:, 384:390], r=[kp], w=[('dtt', t)])
                tt('dve', dtt[:, t, 0:6], dtt[:, t, 0:6], pvc('dtb', 0, 6), ALU.add, r=[('dtt', t), 'pv'], w=[('dtt', t)])
                yield
                PS.put((ps, kp))
                sz.append((a, ka))
            for t in range(NT):
                act(dtt[:, t, 0:6], dtt[:, t, 0:6], AF.Exp, r=[('dtt', t)], w=[('dtt', t)])
            for t in range(NT):
                act(dtt[:, t, 0:6], dtt[:, t, 0:6], AF.Ln, r=[('dtt', t)], w=[('dtt', t)], bias=1.0)
                tt('dve', dtt[:, t, 8:14], dtt[:, t, 0:6], abc[:, 0:6], ALU.mult, r=[('dtt', t), 'abc'], w=[('dtt', t)])
            if STAGE == 45:
                raise _Stop()
            yield

        gens0 = [(hg_gen(), 1), (prep_gen(), 2)]
        alive0 = list(gens0)
        while alive0:
            for g in list(alive0):
                for _ in range(g[1]):
                    try:
                        next(g[0])
                    except StopIteration:
                        alive0.remove(g)
                        break

        if STAGE <= 4:
            raise _Stop()
        def ssd_gen():
            for t in range(NT):
                tc0 = t * 128
                kdt = ('dtt', t)
                dtv = dtt[:, t, 0:6]
                dA = dtt[:, t, 8:14]
                pst, kpt = PS.get()
                pstb = pst.bitcast(BF16)
                for j in range(5):
                    pe_tr(pstb[:, j * 128:(j + 1) * 128], xc[j][0][:, tc0:tc0 + 128], r=[xc[j][1]], w=[kpt])
                if STAGE == 461 and t == TCUT:
                    raise _Stop()
                xtm, kxtm = B.get()
                if VAR == 1:
                    _sp = B.get()
                btm, kbtm = B.get()
                cp('act', xtm[:, :], pstb[:, 0:384], r=[kpt], w=[kxtm])
                if STAGE == 462 and t == TCUT:
                    raise _Stop()
                cp('act', btm[:, 0:256], pstb[:, 384:640], r=[kpt], w=[kbtm])
                PS.put((pst, kpt))
                if STAGE == 460 and t == TCUT:
                    raise _Stop()
                yield
                pa, kpa = PS.get()
                pe_mm(pa[:, 0:6], tri, dA, True, True, r=['cf', kdt], w=[kpa])
                pe_mm(pa[:, 8:14], onesf[:, :], dA, True, True, r=['onesf', kdt], w=[kpa])
                ac = acs[:, t, :]
                kac = ('acs', t)
                cp('dve', ac[:, 0:14], pa[:, 0:14], r=[kpa], w=[kac])
                PS.put((pa, kpa))
                if STAGE == 46 and t == TCUT:
                    raise _Stop()
                act(ac[:, 16:30], ac[:, 0:14], AF.Exp, r=[kac], w=[kac])
                ts('dve', ac[:, 32:38], ac[:, 0:6], -1.0, None, ALU.mult, None, r=[kac], w=[kac])
                tt('dve', ac[:, 40:46], ac[:, 8:14], ac[:, 0:6], ALU.subtract, r=[kac], w=[kac])
                act(ac[:, 40:46], ac[:, 40:46], AF.Exp, r=[kac], w=[kac])
                eac = ac[:, 16:22]
                etot = ac[:, 24:30]
                dec = ac[:, 40:46]
                yield
                dAb = [A.get(), A.get()]
                for g in range(2):
                    cp('pool', v3(dAb[g][0][:, :], 3), bc(dA[:, 3 * g:3 * g + 3], 3, 128), r=[kdt], w=[dAb[g][1]])
                pL = [PS.get(), PS.get()]
                for h in range(6):
                    ps, kp = pL[h // 3]
                    o = (h % 3) * 128
                    pe_mm(ps[:, o:o + 128], dAb[h // 3][0][:, o:o + 128], tri, True, False, r=[dAb[h // 3][1], 'cf'], w=[kp])
                    pe_mm(ps[:, o:o + 128], identf, negmf[:, :], False, True, r=['cf', 'negmf'], w=[kp])
                LTs = [A.get(), A.get()]
                for h in range(6):
                    o = (h % 3) * 128
                    act(LTs[h // 3][0][:, o:o + 128], pL[h // 3][0][:, o:o + 128], AF.Exp, r=[pL[h // 3][1], kac],
                        w=[LTs[h // 3][1]], bias=ac[:, 32 + h:33 + h])
                PS.put(*pL)
                A.put(*dAb)
                if STAGE == 47 and t == TCUT:
                    raise _Stop()
                yield
                psg, kpsg = PS.get()
                for g in range(2):
                    pe_mm(psg[:, g * 128:(g + 1) * 128], xc[3 + g][0][:, tc0:tc0 + 128], xc[5 + g][0][:, tc0:tc0 + 128],
                          True, True, r=[xc[3 + g][1], xc[5 + g][1]], w=[kpsg])
                AT = [B.get(), B.get()]
                for g in range(2):
                    tt('dve', v3(AT[g][0][:, :], 3), v3(LTs[g][0][:, :], 3),
                       psg[:, g * 128:(g + 1) * 128].unsqueeze(1).broadcast_to([128, 3, 128]), ALU.mult,
                       r=[LTs[g][1], kpsg], w=[AT[g][1]])
                PS.put((psg, kpsg))
                A.put(*LTs)
                if STAGE == 48 and t == TCUT:
                    raise _Stop()
                yield
                xdt, kxdt = B.get()
                xw, kxw = B.get()
                tt('pool', v3(xdt[:, :], 6), v3(xtm[:, :], 6), bc(dtv, 6, 64), ALU.mult, r=[kxtm, kdt], w=[kxdt])
                tt('pool', v3(xw[:, :], 6), v3(xdt[:, :], 6), bc(dec, 6, 64), ALU.mult, r=[kxdt, kac], w=[kxw])
                psy, kpsy = PS.get()
                psyo, kpsyo = PS.get()
                psS, kpsS = PS.get()
                for h in range(6):
                    g = h // 3
                    hs = slice(h * 64, (h + 1) * 64)
                    pe_mm(psy[:, hs], xc[h // 2][0][:, tc0:tc0 + 128], dgd[:, h // 2, (h % 2) * 64:(h % 2 + 1) * 64],
                          True, False, r=[xc[h // 2][1], 'dgd'], w=[kpsy])
                    pe_mm(psy[:, hs], AT[g][0][:, (h % 3) * 128:(h % 3 + 1) * 128], xdt[:, hs], False, True,
                          r=[AT[g][1], kxdt], w=[kpsy])
                for h in range(6):
                    g = h // 3
                    hs = slice(h * 64, (h + 1) * 64)
                    pe_mm(psyo[:, hs], xc[5 + g][0][:, tc0:tc0 + 128], Smb[:, h, :], True, True,
                          r=[xc[5 + g][1], 'Smb'], w=[kpsyo])
                for h in range(6):
                    g = h // 3
                    hs = slice(h * 64, (h + 1) * 64)
                    pe_mm(psS[:, hs], btm[:, g * 128:(g + 1) * 128], xw[:, hs], True, True, r=[kbtm, kxw], w=[kpsS])
                B.put(*AT)
                if STAGE == 49 and t == TCUT:
                    raise _Stop()
                yield
                t1, kt1 = A.get()
                t2, kt2 = A.get()
                tt('dve', v3(t1[:, :], 6), v3(psyo[:, 0:384], 6), bc(eac, 6, 64), ALU.mult, r=[kpsyo, kac], w=[kt1])
                PS.put((psyo, kpsyo))
                tt('dve', t1[:, :], psy[:, 0:384], t1[:, :], ALU.add, r=[kpsy, kt1], w=[kt1])
                PS.put((psy, kpsy))
                tt('pool', t1[:, :], t1[:, :], sz[t][0][:, :], ALU.mult, r=[kt1, sz[t][1]], w=[kt1])
                act(t2[:, :], t1[:, :], AF.Square, r=[kt1], w=[kt2])
                ss, kss = small()
                P.add('dve', lambda e, ss=ss, t2=t2: e.tensor_reduce(ss[:, 0:2], v3(t2[:, :], 2), AX.X, ALU.add),
                      r=[kt2], w=[kss])
                rstd_small(ss, kss, 2, 1.0 / 192, RMS_EPS)
                yb_, kyb = B.get()
                tt('pool', v3(yb_[:, :], 2), v3(t1[:, :], 2), bc(ss[:, 0:2], 2, 192), ALU.mult, r=[kt1, kss], w=[kyb])
                if STAGE == 50 and t == TCUT:
                    raise _Stop()
                yield
                tt('dve', v3(t2[:, :], 6), Sm[:, :, :], bc(etot, 6, 64), ALU.mult, r=['Sm', kac], w=[kt2])
                tt('dve', Sm[:, :, :], v3(t2[:, :], 6), v3(psS[:, 0:384], 6), ALU.add, r=[kt2, kpsS], w=['Sm'])
                PS.put((psS, kpsS))
                cp('act', Smb[:, :, :], Sm[:, :, :], r=['Sm'], w=['Smb'])
                if STAGE == 51 and t == TCUT:
                    raise _Stop()
                yield
                A.put((t1, kt1), (t2, kt2))
                pst, kpt = PS.get()
                pstb = pst.bitcast(BF16)
                for pc in range(3):
                    pe_tr(pstb[:, pc * 128:(pc + 1) * 128], yb_[:, pc * 128:(pc + 1) * 128], r=[kyb], w=[kpt])
                for pc in range(3):
                    act(yT[3 + pc][0][:, tc0:tc0 + 128], pstb[:, pc * 128:(pc + 1) * 128], AF.Copy,
                        r=[kpt, 'pv'], w=[yT[3 + pc][1]], scale=pvc('mnw', pc))
                PS.put((pst, kpt))
                yield
                B.put((yb_, kyb), (xtm, kxtm), (btm, kbtm), (xdt, kxdt), (xw, kxw))
                if STAGE == 52 + t:
                    raise _Stop()
            for x in xc:
                B.put(x)
            for x in sz:
                A.put(x)


        if STAGE <= 5:
            raise _Stop()
        def s5_gen():
            w7 = get_piece(l, 7)
            uf, ub = [], []
            for m in range(2):
                pi, off = FMMAP[('u', m)]
                ps, kp = fm_proj(b, w7[0], w7[1], off)
                a, ka = A.get()
                u_, ku = B.get()
                cp('act', a[:, :], ps[:, 0:TB], r=[kp], w=[ka])
                cp('act', u_[:, :], ps[:, 0:TB], r=[kp], w=[ku])
                PS.put((ps, kp))
                uf.append((a, ka))
                ub.append((u_, ku))
            if STAGE == 60:
                raise _Stop()
            gl = []
            gelb = []
            for m in range(2):
                py, kpy = PS.get()

                def chunk_gen(jj, m=m, py=py, kpy=kpy):
                    j = 4 * m + jj
                    kF, kN = ('Fr', j), ('nF', j)
                    pA, kpA = PS.get()
                    pB, kpB = PS.get()
                    pe_mm(pA[:, 0:TB], btbf[:, 0, j * 128:(j + 1) * 128], ub[m][0][:, :], True, True,
                          r=['btbf', ub[m][1]], w=[kpA])
                    pe_mm(pB[:, 0:TB], btbf[:, 1, j * 128:(j + 1) * 128], ub[m][0][:, :], True, True,
                          r=['btbf', ub[m][1]], w=[kpB])
                    yield
                    a_bf, kab = B.get()
                    b_bf, kbb = B.get()
                    cp('act', a_bf[:, :], pA[:, 0:TB], r=[kpA], w=[kab])
                    cp('act', b_bf[:, :], pB[:, 0:TB], r=[kpB], w=[kbb])
                    PS.put((pA, kpA), (pB, kpB))
                    qq = [B.get() for _ in range(4)]
                    tt('dve', qq[0][0][:, :], Fre[:, j, :], a_bf[:, :], ALU.mult, r=[kF, kab], w=[qq[0][1]])
                    tt('pool', qq[1][0][:, :], nFim[:, j, :], b_bf[:, :], ALU.mult, r=[kN, kbb], w=[qq[1][1]])
                    tt('dve', qq[2][0][:, :], Fre[:, j, :], b_bf[:, :], ALU.mult, r=[kF, kbb], w=[qq[2][1]])
                    tt('pool', qq[3][0][:, :], nFim[:, j, :], a_bf[:, :], ALU.mult, r=[kN, kab], w=[qq[3][1]])
                    B.put((a_bf, kab), (b_bf, kbb))
                    yield
                    pre, kpre = PS.get()
                    pim, kpim = PS.get()
                    pe_mm(pre[:, 0:TB], ident, qq[0][0][:, :], True, False, r=['cb', qq[0][1]], w=[kpre])
                    pe_mm(pre[:, 0:TB], nident[:, :], qq[1][0][:, :], False, True, r=['nident', qq[1][1]], w=[kpre])
                    pe_mm(pim[:, 0:TB], ident, qq[2][0][:, :], True, False, r=['cb', qq[2][1]], w=[kpim])
                    pe_mm(pim[:, 0:TB], ident, qq[3][0][:, :], False, True, r=['cb', qq[3][1]], w=[kpim])
                    B.put(*qq)
                    if STAGE == 61 and jj == 0 and m == 0:
                        raise _Stop()
                    yield
                    t2, k2 = A.get()
                    t4, k4 = A.get()
                    rj = s5s[:, 3, j:j + 1].broadcast_to([128, TB])
                    P.add('dve', lambda e, o=t2, rj=rj, p=pre, i=s5c[:, 0, j:j + 1]:
                          e.tensor_tensor_scan(o[:, :], rj, p[:, 0:TB], i, ALU.mult, ALU.add),
                          r=['s5s', kpre, ('s5c', j)], w=[k2])
                    P.add('dve', lambda e, o=t4, rj=rj, p=pim, i=s5c[:, 1, j:j + 1]:
                          e.tensor_tensor_scan(o[:, :], rj, p[:, 0:TB], i, ALU.mult, ALU.add),
                          r=['s5s', kpim, ('s5c', j)], w=[k4])
                    PS.put((pre, kpre), (pim, kpim))
                    if STAGE == 62 and jj == 0 and m == 0:
                        raise _Stop()
                    yield
                    bb = [B.get() for _ in range(4)]
                    tt('pool', bb[0][0][:, :], Fre[:, j, :], t2[:, :], ALU.mult, r=[kF, k2], w=[bb[0][1]])
                    tt('dve', bb[1][0][:, :], nFim[:, j, :], t4[:, :], ALU.mult, r=[kN, k4], w=[bb[1][1]])
                    tt('pool', bb[2][0][:, :], nFim[:, j, :], t2[:, :], ALU.mult, r=[kN, k2], w=[bb[2][1]])
                    tt('pool', bb[3][0][:, :], Fre[:, j, :], t4[:, :], ALU.mult, r=[kF, k4], w=[bb[3][1]])
                    cc, kcc = small()
                    e = TB - 1
                    tt('dve', cc[:, 0:1], Fre[:, j, e:e + 1], t2[:, e:e + 1], ALU.mult, r=[kF, k2], w=[kcc])
                    tt('dve', cc[:, 1:2], nFim[:, j, e:e + 1], t2[:, e:e + 1], ALU.mult, r=[kN, k2], w=[kcc])
                    tt('dve', cc[:, 2:3], nFim[:, j, e:e + 1], t4[:, e:e + 1], ALU.mult, r=[kN, k4], w=[kcc])
                    tt('dve', cc[:, 3:4], Fre[:, j, e:e + 1], t4[:, e:e + 1], ALU.mult, r=[kF, k4], w=[kcc])
                    tt('dve', s5c[:, 0, j:j + 1], cc[:, 0:1], cc[:, 2:3], ALU.add, r=[kcc], w=[('s5c', j)])
                    tt('dve', s5c[:, 1, j:j + 1], cc[:, 3:4], cc[:, 1:2], ALU.subtract, r=[kcc], w=[('s5c', j)])
                    if STAGE == 63 and jj == 0 and m == 0:
                        raise _Stop()
                    yield
                    A.put((t2, k2), (t4, k4))
                    for q_, ci_ in enumerate((0, 0, 1, 2)):
                        pe_mm(py[:, 0:TB], ctbf[:, ci_, j * 128:(j + 1) * 128], bb[q_][0][:, :],
                              jj == 0 and q_ == 0, jj == 3 and q_ == 3, r=['ctbf', bb[q_][1]], w=[kpy])
                    B.put(*bb)

                for pair in ((0, 1), (2, 3)):
                    live = [chunk_gen(jj) for jj in pair]
                    while live:
                        for g in list(live):
                            try:
                                next(g)
                            except StopIteration:
                                live.remove(g)
                        yield
                yv, kyv = A.get()
                stt('dve', yv[:, :], uf[m][0][:, :], pvc('s5d', m), py[:, 0:TB], ALU.mult, ALU.add,
                    r=[uf[m][1], 'pv', kpy], w=[kyv])
                PS.put((py, kpy))
                A.put(uf[m])
                yield
                gb, kgb = B.get()
                if GELU_ACT:
                    act(gb[:, :], yv[:, :], AF.Gelu_apprx_tanh, r=[kyv], w=[kgb])
                    A.put((yv, kyv))
                    gl.append(None)
                else:
                    y2, ky2 = A.get()
                    tt('pool', y2[:, :], yv[:, :], yv[:, :], ALU.mult, r=[kyv], w=[ky2])
                    ts('dve', y2[:, :], y2[:, :], 0.044715, 1.0, ALU.mult, ALU.add, r=[ky2], w=[ky2])
                    tt('pool', y2[:, :], y2[:, :], yv[:, :], ALU.mult, r=[ky2, kyv], w=[ky2])
                    act(y2[:, :], y2[:, :], AF.Tanh, r=[ky2], w=[ky2], scale=0.7978845608028654)
                    stt('dve', yv[:, :], y2[:, :], 1.0, yv[:, :], ALU.add, ALU.mult, r=[ky2, kyv], w=[kyv])
                    ts('dve', gb[:, :], yv[:, :], 0.5, None, ALU.mult, None, r=[kyv], w=[kgb])
                    A.put((y2, ky2))
                    gl.append((yv, kyv))
                gelb.append((gb, kgb))
            yield
            for m2 in range(2):
                ps, kp = PS.get()
                for m in range(2):
                    pe_mm(ps[:, 0:TB], gwbf[:, m * 256 + m2 * 128: m * 256 + (m2 + 1) * 128], gelb[m][0][:, :],
                          m == 0, m == 1, r=['gwbf', gelb[m][1]], w=[kp])
                th, kth = A.get()
                act(th[:, :], ps[:, 0:TB], AF.Tanh, r=[kp, 'hgb'], w=[kth], bias=hgb[:, m2:m2 + 1], scale=0.5)
                PS.put((ps, kp))
                if GELU_ACT:
                    ts('dve', th[:, :], th[:, :], 0.5, 0.5, ALU.mult, ALU.add, r=[kth], w=[kth])
                    tt('dve', yT[6 + m2][0][:, :], gelb[m2][0][:, :], th[:, :], ALU.mult, r=[gelb[m2][1], kth],
                       w=[yT[6 + m2][1]])
                else:
                    ts('dve', th[:, :], th[:, :], 0.25, 0.25, ALU.mult, ALU.add, r=[kth], w=[kth])
                    tt('dve', yT[6 + m2][0][:, :], gl[m2][0][:, :], th[:, :], ALU.mult, r=[gl[m2][1], kth],
                       w=[yT[6 + m2][1]])
                A.put((th, kth))
            for m in range(2):
                if gl[m] is not None:
                    A.put(gl[m])
                B.put(gelb[m], ub[m])

            yield

        if mid_hook[0] is not None:
            mid_hook[0]()
        gens = [(ssd_gen(), 1), (s5_gen(), 1)]
        alive = list(gens)
        while alive:
            for g in list(alive):
                for _ in range(g[1]):
                    try:
                        next(g[0])
                    except StopIteration:
                        alive.remove(g)
                        break
            conv_pump()
        conv_drain()
        if STAGE <= 6:
            raise _Stop()
        if DEBUG and l == 0:
            for k in range(8):
                a, ka = A.get()
                cp('dve', a[:, :], yT[k][0][:, :], r=[yT[k][1]], w=[ka])
                dma(dbgd[b * 16 + k], a[:, :], r=[ka], w=[('dbg', b, k)])
                A.put((a, ka))
        rr = []
        for n in range(8):
            slot, kslot = wp_(8 if n < 4 else 9)
            off = (n % 4) * 1024
            ps, kp = PS.get()
            for k in range(8):
                pe_mm(ps[:, 0:TB], slot[:, off + k * 128: off + (k + 1) * 128], yT[k][0][:, :], k == 0, k == 7,
                      r=[kslot, yT[k][1]], w=[kp])
            r_, kr = A.get()
            stt('dve', r_[:, :], hbf[:, n, c0:c0 + TB], ALPHA, ps[:, 0:TB], ALU.mult, ALU.add, r=[hk(b, n), kp], w=[kr])
            PS.put((ps, kp))
            rr.append((r_, kr))
        for x in yT:
            B.put(x)
        layernorm(rr, 'ln1g', 'ln1b', lambda n: (hbf[:, n, c0:c0 + TB], [hk(b, n)], None))
        for x in rr:
            A.put(x)
        if DEBUG and l == 0:
            for k in range(8):
                a, ka = A.get()
                cp('dve', a[:, :], hbf[:, k, c0:c0 + TB], r=[hk(b, k)], w=[ka])
                dma(dbgd[b * 16 + 8 + k], a[:, :], r=[ka], w=[('dbg', b, 8 + k)])
                A.put((a, ka))

    def ffn_block(l, b, last):
        if STAGE <= 7:
            raise _Stop()
        c0 = b * TB
        hid = []
        for hg in range(8):
            wp = get_piece(l, 10 + hg)
            for hc in range(4):
                ps, kp = fm_proj(b, wp[0], wp[1], hc * 1024)
                rl, krl = B.get()
                if (hg * 4 + hc) % 2 == 0:
                    act(rl[:, :], ps[:, 0:TB], AF.Relu, r=[kp], w=[krl])
                else:
                    ts('dve', rl[:, :], ps[:, 0:TB], 0.0, None, ALU.max, None, r=[kp], w=[krl])
                PS.put((ps, kp))
                hd, khd = B.get()
                tt('pool', hd[:, :], rl[:, :], rl[:, :], ALU.mult, r=[krl], w=[khd])
                B.put((rl, krl))
                hid.append((hd, khd))
            conv_pump()
            yield
        rr = []
        for n in range(8):
            wp = get_piece(l, 18 + n)
            ps, kp = PS.get()
            for j in range(32):
                pe_mm(ps[:, 0:TB], wp[0][:, j * 128:(j + 1) * 128], hid[j][0][:, :], j == 0, j == 31,
                      r=[wp[1], hid[j][1]], w=[kp])
            r_, kr = A.get()
            stt('dve', r_[:, :], hbf[:, n, c0:c0 + TB], ALPHA, ps[:, 0:TB], ALU.mult, ALU.add, r=[hk(b, n), kp], w=[kr])
            PS.put((ps, kp))
            rr.append((r_, kr))
            conv_pump()
        for x in hid:
            B.put(x)
        def ln_fn():
            if not last:
                layernorm(rr, 'ln2g', 'ln2b', lambda n: (hbf[:, n, c0:c0 + TB], [hk(b, n)], None))
            else:
                def dst(n):
                    o, ko = A.get()

                    def post():
                        dma(outd[:, n, c0:c0 + TB], o[:, :], r=[ko], w=[('out', b, n)])
                        A.put((o, ko))
                    return o[:, :], [ko], post
                layernorm(rr, 'ln2g', 'ln2b', dst)
            for x in rr:
                A.put(x)
        ffn_pend.append(ln_fn)

    ffn_pend = []
    mid_hook = [None]

    def whole():
        for l in range(DEPTH):
            flush_convert()
            layer_setup(l)
            if STAGE <= 0:
                raise _Stop()
            nconv = NPIECE * 4
            steps = 2 * NBLK
            done = 0
            dn = [done]

            def conv_hook(l=l, dn=dn):
                b = conv_hook.b
                if l == 0:
                    lo = 40 + (64 * b) // NBLK
                    hi = 40 + (64 * (b + 1)) // NBLK
                    for q in range(lo, hi):
                        conv_queue.append((0, q))
                if l + 1 < DEPTH:
                    tgt = (nconv * (b + 1)) // steps
                    for q in range(dn[0], tgt):
                        conv_queue.append((l + 1, q))
                    dn[0] = tgt
            for b in range(NBLK):
                conv_hook.b = b
                mid_hook[0] = conv_hook
                mixer_block(l, b)
                mid_hook[0] = None
                if l == 0 and b == NBLK - 1:
                    flush_convert()
            done = dn[0]
            for b in range(NBLK):
                if l + 1 < DEPTH:
                    tgt = (nconv * (NBLK + b + 1)) // steps
                    for q in range(done, tgt):
                        conv_queue.append((l + 1, q))
                    done = tgt
                g_ffn = ffn_block(l, b, l == DEPTH - 1)
                for _ in range(FFN_DEFER):
                    next(g_ffn)
                if len(ffn_pend) > 0:
                    ffn_pend.pop(0)()
                for _ in g_ffn:
                    pass
                conv_drain()
            while ffn_pend:
                ffn_pend.pop(0)()
    try:
        whole()
    except _Stop:
        A.free = [(t, ('A', i)) for i, t in enumerate(Apool)]
        for b in range(NBLK):
            for k in range(8):
                a, ka = A.get()
                cp('dve', a[:, :], hbf[:, k, b * TB:(b + 1) * TB], r=[hk(b, k)], w=[ka])
                dma(outd[:, k, b * TB:(b + 1) * TB], a[:, :], r=[ka], w=[('out', b, k)])
                A.put((a, ka))

    ops = P.ops
    for o in ops:
        for j in o['deps']:
            ops[j]['sig'] = True
    engs = ['pe', 'act', 'dve', 'pool', 'sp']
    cnt = {e: 0 for e in engs}
    ndma = 0
    for o in ops:
        if o['dma']:
            o['dsem'] = ndma % ND
            o['dval'] = 16 * (ndma // ND + 1)
            ndma += 1
        elif o['sig']:
            cnt[o['eng']] += 1
            o['cnt'] = cnt[o['eng']]
    print("ops", len(ops), "sig counts", cnt, "dmas", ndma, "minfree A/B/PS", A.minfree, B.minfree, PS.minfree)
    esem = {e: es.enter_context(nc.semaphore("sem_" + e)) for e in engs if e != 'sp'}
    dsem = [es.enter_context(nc.semaphore(f"dsem{i}")) for i in range(ND)]
    out_dmas = [o for o in ops if o['dma']]

    def emit(ename, eng):
        waited = {}
        for o in ops:
            if o['eng'] != ename:
                continue
            need = {}
            for j in o['deps']:
                oj = ops[j]
                if oj['dma']:
                    key = ('d', oj['dsem'])
                    val = oj['dval']
                else:
                    key = ('e', oj['eng'])
                    val = oj['cnt']
                if val > need.get(key, 0):
                    need[key] = val
            if o['dma'] and o['dval'] > 16:
                key = ('d', o['dsem'])
                need[key] = max(need.get(key, 0), o['dval'] - 16)
            for key, val in need.items():
                if val > waited.get(key, 0):
                    sem = dsem[key[1]] if key[0] == 'd' else esem[key[1]]
                    eng.wait_ge(sem, val)
                    waited[key] = val
            ins = o['fn'](eng)
            if o['dma']:
                ins.then_inc(dsem[o['dsem']], 16)
            elif o['sig']:
                ins.then_inc(esem[ename], 1)
        if ename == 'sp':
            last = {}
            for o in out_dmas:
                last[o['dsem']] = max(last.get(o['dsem'], 0), o['dval'])
            for s, v in last.items():
                eng.wait_ge(dsem[s], v)

    with nc.Block() as block:
        @block.tensor
        def _(e):
            emit('pe', e)

        @block.scalar
        def _(e):
            emit('act', e)

        @block.vector
        def _(e):
            emit('dve', e)

        @block.gpsimd
        def _(e):
            emit('pool', e)

        @block.sync
        def _(e):
            emit('sp', e)
    es.close()
    return nc


SPL = np.cumsum([0, 384, 384, 384, 384, 384, 896, 6, 256])


def _fmchunk(W, c0):
    return W[:, c0:c0 + 128].reshape(8, 128, 128).transpose(1, 0, 2).reshape(128, 1024)


def _tmgroup(W, cols):
    n = len(cols)
    return W[:, cols].reshape(8, 128, n).transpose(1, 0, 2).reshape(128, 8 * n)


def _pad(a):
    o = np.zeros((128, 4096), np.float32)
    o[:, :a.shape[1]] = a
    return o


def prep_shared(inp, DEPTH):
    f = np.float32
    ws, pvs, pms = [], [], []
    q0, f0, i0, g0, z0, x0, d0, u0 = [int(v) for v in SPL[:8]]
    for l in range(DEPTH):
        W = np.asarray(inp['w_in'][l], f)
        pcs = []
        pcs.append(np.concatenate([_fmchunk(W, q0), _fmchunk(W, q0 + 128), _fmchunk(W, q0 + 256), _fmchunk(W, f0)], 1))
        pcs.append(np.concatenate([_fmchunk(W, f0 + 128), _fmchunk(W, f0 + 256)], 1))
        pcs.append(_tmgroup(W, np.arange(i0, i0 + 384)))
        pcs.append(_tmgroup(W, np.arange(g0, g0 + 384)))
        pcs.append(np.concatenate([_fmchunk(W, x0 + 128 * c) for c in range(4)], 1))
        pcs.append(np.concatenate([_fmchunk(W, x0 + 128 * c) for c in range(4, 7)], 1))
        pcs.append(_tmgroup(W, np.concatenate([np.arange(z0, z0 + 384), np.arange(d0, d0 + 6)])))
        pcs.append(np.concatenate([_fmchunk(W, u0), _fmchunk(W, u0 + 128)], 1))
        Wo = np.asarray(inp['w_out'][l], f)
        pcs.append(np.concatenate([_fmchunk(Wo, 128 * n) for n in range(4)], 1))
        pcs.append(np.concatenate([_fmchunk(Wo, 128 * n) for n in range(4, 8)], 1))
        W1 = np.asarray(inp['w_mlp_in'][l], f)
        for hg in range(8):
            pcs.append(np.concatenate([_fmchunk(W1, 128 * (4 * hg + hc)) for hc in range(4)], 1))
        W2 = np.asarray(inp['w_mlp_out'][l], f)
        for n in range(8):
            pcs.append(W2[:, n * 128:(n + 1) * 128].reshape(32, 128, 128).transpose(1, 0, 2).reshape(128, 4096))
        assert len(pcs) == NPIECE
        ws.append(np.stack([_pad(p) for p in pcs]))
        pvl = np.zeros((128, NPV), f)
        cw = np.asarray(inp['m2_conv_w'][l], f)
        pvl[:, 0:28] = cw.reshape(4, 7, 128).transpose(2, 1, 0).reshape(128, 28)
        pvl[:, 28:35] = np.asarray(inp['m2_conv_b'][l], f).reshape(7, 128).T
        for nm, o in (('ln1_g', 35), ('ln1_b', 43), ('ln2_g', 51), ('ln2_b', 59)):
            pvl[:, o:o + 8] = np.asarray(inp[nm][l], f).reshape(8, 128).T
        pvl[:, 67:69] = np.asarray(inp['s5_glu_b'][l], f).reshape(2, 128).T
        pvl[:, 69:71] = np.asarray(inp['s5_d'][l], f).reshape(2, 128).T
        pvl[:, 71:79] = np.asarray(inp['s5_a_re'][l], f).reshape(1024).reshape(8, 128).T
        pvl[:, 79:87] = np.asarray(inp['s5_a_im'][l], f).reshape(1024).reshape(8, 128).T
        pvl[:, 87:95] = np.repeat(np.asarray(inp['s5_log_dt'][l], f), 64).reshape(8, 128).T
        pvl[:, 95:101] = np.asarray(inp['m2_dt_bias'][l], f)[None, :]
        pvl[:, 101:107] = np.asarray(inp['m2_a_log'][l], f)[None, :]
        pvl[:, 107:113] = np.asarray(inp['m2_d'][l], f)[None, :]
        pvl[:, 113:116] = np.tile(np.asarray(inp['hgrn_norm_w'][l], f), 6).reshape(3, 128).T
        pvl[:, 116:119] = np.asarray(inp['m2_norm_w'][l], f).reshape(3, 128).T
        pvl[:, 119:122] = np.repeat(np.asarray(inp['m2_d'][l], f), 64).reshape(3, 128).T
        pvs.append(pvl)
        pm = np.zeros((5, 128, 1024), f)
        for ri, nm in enumerate(('s5_b_re', 's5_b_im')):
            bb = np.asarray(inp[nm][l], f)
            for g in range(16):
                pm[ri, (g % 8) * 16:(g % 8) * 16 + 16, g * 64:(g + 1) * 64] = bb[g].T
        for ri, nm in enumerate(('s5_c_re', 's5_c_im')):
            cc = np.asarray(inp[nm][l], f)
            ct = pm[2 + ri].reshape(128, 8, 128)
            for g in range(16):
                j, ph = g // 2, (g % 2) * 64
                ct[ph:ph + 64, j, (g % 8) * 16:(g % 8) * 16 + 16] = cc[g].T
        gw = np.asarray(inp['s5_glu_w'][l], f)
        pm[4, :, 0:512] = gw.reshape(2, 128, 256).transpose(1, 0, 2).reshape(128, 512)
        pms.append(pm)
    wsrc = np.concatenate(ws, 0).reshape(DEPTH * NPIECE, 128, 4, 1024).transpose(0, 2, 1, 3)
    wsrc = np.ascontiguousarray(wsrc).reshape(DEPTH * NPIECE * 4, 128, 1024)
    lbl = np.asarray(inp['hgrn_lb_logits'], f).reshape(4, 3, 128).transpose(2, 1, 0).reshape(128, 12)
    s = np.arange(128)
    tri = (s[:, None] <= s[None, :]).astype(f)
    tau = np.arange(384)
    rst = np.tile(((tau % 64) != 0).astype(f)[None, :], (128, 1))
    ramp = np.tile((tau + 1).astype(f)[None, :], (128, 1))
    cfv = np.concatenate([tri, rst, ramp, np.eye(128, dtype=f)], 1)
    negm = np.where(s[:, None] <= s[None, :], 0.0, -30000.0).astype(f)
    mask2 = ((s[:, None] <= s[None, :]) & ((s[:, None] // 64) == (s[None, :] // 64))).astype(f)
    cbv = np.concatenate([np.eye(128, dtype=f), negm, mask2], 1)
    return dict(wsrc=wsrc, pv=np.stack(pvs), pm=np.concatenate(pms, 0), lbl=np.ascontiguousarray(lbl),
                cf=np.ascontiguousarray(cfv), cb=np.ascontiguousarray(cbv))


def run(inp, NBLK, DEPTH, ncores):
    x = np.asarray(inp['x'], np.float32)
    meta = np.asarray(inp['meta_tokens'], np.float32)
    LT = NBLK * TB
    shared = prep_shared(inp, DEPTH)
    in_maps = []
    for b in range(ncores):
        hf = np.zeros((LT, 1024), np.float32)
        hf[0:16] = meta
        hf[16:16 + x.shape[1]] = x[b]
        h0 = np.ascontiguousarray(hf.T.reshape(8, 128, LT).transpose(1, 0, 2))
        m = dict(shared)
        m['h0'] = h0
        in_maps.append(m)
    nc = build(NBLK, DEPTH)
    res = run_bass_kernel_spmd(nc, in_maps, core_ids=list(range(ncores)))
    outs = []
    for b in range(ncores):
        o = np.asarray(res.results[b]['out'], np.float32)
        hf = o.transpose(1, 0, 2).reshape(1024, LT).T
        outs.append(hf[16:16 + x.shape[1]])
    return np.stack(outs).astype(np.float32)


def kernel(**inputs):
    return run(inputs, 11, 4, 4)
```

```python
import math
from contextlib import ExitStack
import numpy as np
import concourse.bass as bass
import concourse.mybir as mybir
from concourse.bass_utils import run_bass_kernel_spmd

F32 = mybir.dt.float32
BF16 = mybir.dt.bfloat16
AF = mybir.ActivationFunctionType
ALU = mybir.AluOpType
AX = mybir.AxisListType

TB = 384
NT = 3
NPIECE = 26
ALPHA = 8.0 ** 0.25
LN_EPS = 1e-5
RMS_EPS = 1e-6
PI = math.pi
NPV = 122
NA = 16
NB = 40
NWS = 3
ND = 24


class Prog:
    def __init__(self):
        self.ops = []
        self.lastw = {}
        self.rd = {}

    def add(self, eng, fn, r=(), w=(), dma=False):
        i = len(self.ops)
        raw, other = set(), set()
        for b in r:
            j = self.lastw.get(b)
            if j is not None:
                raw.add(j)
        for b in w:
            j = self.lastw.get(b)
            if j is not None:
                other.add(j)
            for j in self.rd.get(b, {}).values():
                if isinstance(j, list):
                    other.update(j)
                else:
                    other.add(j)
        deps = set()
        for j in raw | other:
            oj = self.ops[j]
            if oj['dma']:
                deps.add(j)
            elif oj['eng'] == eng and not dma:
                if eng == 'pe':
                    continue
                deps.add(j)
            else:
                deps.add(j)
        for b in r:
            d = self.rd.setdefault(b, {})
            if dma:
                d.setdefault('dma', []).append(i)
            else:
                d[eng] = i
        for b in w:
            self.lastw[b] = i
            self.rd[b] = {}
        self.ops.append(dict(eng=eng, fn=fn, deps=deps, dma=dma, sig=False))
        return i


class _Stop(Exception):
    pass


STAGE = 99
FFN_DEFER = 2
HG_LAG = 2
GELU_ACT = True
VAR = 0
DEBUG = False
TCUT = 0


def build(NBLK, DEPTH):
    LT = NBLK * TB
    nc = bass.Bass("TRN2", target_bir_lowering=False)
    P = Prog()
    es = ExitStack()

    def dram(name, shape, dt, kind):
        return nc.dram_tensor(name, shape, dt, kind=kind).ap()

    h0 = dram("h0", [128, 8, LT], F32, "ExternalInput")
    wsrc = dram("wsrc", [DEPTH * NPIECE * 4, 128, 1024], F32, "ExternalInput")
    pvd = dram("pv", [DEPTH, 128, NPV], F32, "ExternalInput")
    pmd = dram("pm", [DEPTH * 5, 128, 1024], F32, "ExternalInput")
    lbd = dram("lbl", [128, 12], F32, "ExternalInput")
    cfd = dram("cf", [128, 1024], F32, "ExternalInput")
    cbd = dram("cb", [128, 384], F32, "ExternalInput")
    outd = dram("out", [128, 8, LT], F32, "ExternalOutput")
    wbf = dram("wbf", [DEPTH * NPIECE, 128, 4096], BF16, "Internal")
    dbgd = dram("dbg", [NBLK * 16, 128, TB], F32, "ExternalOutput") if DEBUG else None

    def sb(name, shape, dt):
        return es.enter_context(nc.sbuf_tensor(name, shape, dt))

    hbf = sb("hbf", [128, 8, LT], BF16)
    wslot = [sb(f"wslot{i}", [128, 4096], BF16) for i in range(NWS)]
    Fre = sb("Fre", [128, 8, TB], BF16)
    nFim = sb("nFim", [128, 8, TB], BF16)
    cvf = [sb(f"cvf{i}", [128, 1024], F32) for i in range(2)]
    cvb = [sb(f"cvb{i}", [128, 1024], BF16) for i in range(2)]
    pv = sb("pvs", [128, NPV], F32)
    btbf = sb("btbf", [128, 2, 1024], BF16)
    ctbf = sb("ctbf", [128, 3, 1024], BF16)
    gwbf = sb("gwbf", [128, 512], BF16)
    cf = sb("cfs", [128, 1024], F32)
    cb = sb("cbs", [128, 384], BF16)
    onesb = sb("onesb", [128, 128], BF16)
    onesf = sb("onesf", [128, 128], F32)
    lbt = sb("lbt", [128, 3, 4], F32)
    lbw = sb("lbw", [128, 3, 4], F32)
    lba = sb("lba", [128, 3, DEPTH, 3], F32)
    s5s = sb("s5s", [128, 16, 8], F32)
    s5c = sb("s5c", [128, 2, 8], F32)
    hS = [sb(f"hS{i}", [128, 3, 64], F32) for i in range(3)]
    hSb = [sb(f"hSb{i}", [128, 3, 64], BF16) for i in range(3)]
    egl = sb("egl", [128, 3, 8], F32)
    Sm = sb("Sm", [128, 6, 64], F32)
    Smb = sb("Smb", [128, 6, 64], BF16)
    XB = sb("XB", [128, 7, 388], BF16)
    dgw = sb("dgw", [128, 28, 128], BF16)
    dgd = sb("dgd", [128, 3, 128], BF16)
    sm = sb("smalls", [128, 16, 16], F32)
    abc = sb("abc", [128, 8], F32)
    dtt = sb("dtt", [128, 3, 16], F32)
    acs = sb("acs", [128, 3, 48], F32)
    hgb = sb("hgb", [128, 2], F32)
    pib = sb("pib", [128, 2], F32)
    mcol = sb("mcol", [128, 2], F32)
    nident = sb("nident", [128, 128], BF16)
    negmf = sb("negmf", [128, 128], F32)
    Apool = [sb(f"A{i}", [128, TB], F32) for i in range(NA)]
    Bpool = [sb(f"B{i}", [128, TB], BF16) for i in range(NB)]
    print("sbuf bytes remaining", nc.sbuf_bytes_remaining)
    psum = [es.enter_context(nc.psum_tensor(f"ps{i}", [128, 512], F32)) for i in range(8)]

    tri = cf[:, 0:128]
    rstm = cf[:, 128:512]
    ramp = cf[:, 512:896]
    identf = cf[:, 896:1024]
    ident = cb[:, 0:128]
    negm = cb[:, 128:256]
    mask2 = cb[:, 256:384]

    class Pool:
        def __init__(self, tiles, nm):
            self.free = [(t, (nm, i)) for i, t in enumerate(tiles)]
            self.nm = nm

        def get(self):
            if not self.free:
                raise RuntimeError("pool empty " + self.nm)
            it = self.free.pop(0)
            self.minfree = min(getattr(self, 'minfree', 999), len(self.free))
            return it

        def put(self, *its):
            for it in its:
                self.free.append(it)

    A = Pool(Apool, 'A')
    B = Pool(Bpool, 'B')
    PS = Pool(psum, 'ps')
    smi = [0]

    def small():
        i = smi[0] % 16
        smi[0] += 1
        return sm[:, i, :], ('sm', i)

    def pe_mm(out, lhsT, rhs, start, stop, r, w):
        P.add('pe', lambda e: e.matmul(out, lhsT, rhs, start=start, stop=stop), r=r, w=w)

    def pe_tr(out, in_, r, w):
        P.add('pe', lambda e: e.transpose(out, in_, ident), r=list(r) + ['cb'], w=w)

    def act(out, in_, func, r, w, bias=0.0, scale=1.0):
        P.add('act', lambda e: e.activation(out, in_, func, bias=bias, scale=scale), r=r, w=w)

    def tt(eng, out, a, b, op, r, w):
        P.add(eng, lambda e: e.tensor_tensor(out, a, b, op), r=r, w=w)

    def ts(eng, out, a, s1, s2, op0, op1, r, w):
        if op1 is None:
            P.add(eng, lambda e: e.tensor_single_scalar(out, a, s1, op0), r=r, w=w)
        else:
            P.add(eng, lambda e: e.tensor_scalar(out, a, s1, s2, op0, op1), r=r, w=w)

    def stt(eng, out, a, s, b, op0, op1, r, w):
        P.add(eng, lambda e: e.scalar_tensor_tensor(out, a, s, b, op0, op1), r=r, w=w)

    def cp(eng, out, in_, r, w):
        if eng == 'act':
            act(out, in_, AF.Copy, r, w)
        else:
            P.add(eng, lambda e: e.tensor_copy(out, in_), r=r, w=w)

    def dma(out, in_, r, w):
        P.add('sp', lambda e: e.dma_start(out=out, in_=in_), r=r, w=w, dma=True)

    dbg_names = []

    def dump(name, ap, key, n=TB):
        if not DEBUG:
            return
        dt_ = dram("dbg_" + name, [128, n], F32, "ExternalOutput")
        dbg_names.append("dbg_" + name)
        a, ka = A.get()
        cp('dve', a[:, 0:n], ap, r=[key], w=[ka])
        dma(dt_[:, :], a[:, 0:n], r=[ka], w=[('dbgx', name)])
        A.put((a, ka))

    def memset(eng, ap, val, w):
        P.add(eng, lambda e: e.memset(ap, val), r=(), w=w)

    cvi = [0]

    pending = []

    def flush_convert():
        while pending:
            pending.pop(0)()

    def conv_load(l, q):
        i = cvi[0] % 2
        eng = ['act', 'dve'][cvi[0] % 2]
        cvi[0] += 1
        dma(cvf[i][:, :], wsrc[(l * NPIECE * 4 + q)], r=[], w=[('cvf', i)])
        return (l, q, i, eng)

    def conv_finish(u):
        l, q, i, eng = u
        pi, qq = q // 4, q % 4
        cp(eng, cvb[i][:, :], cvf[i][:, :], r=[('cvf', i)], w=[('cvb', i)])
        while pending:
            pending.pop(0)()
        pending.append(lambda: dma(wbf[l * NPIECE + pi][:, qq * 1024:(qq + 1) * 1024], cvb[i][:, :],
                                   r=[('cvb', i)], w=[('wbf', l, pi, qq)]))

    def convert_unit(l, q):
        conv_finish(conv_load(l, q))

    conv_queue = []
    conv_inflight = []

    def conv_pump():
        if conv_inflight:
            conv_finish(conv_inflight.pop(0))
        if conv_queue:
            conv_inflight.append(conv_load(*conv_queue.pop(0)))

    def conv_drain():
        while conv_queue or conv_inflight:
            conv_pump()

    sched = []
    for l in range(DEPTH):
        for b in range(NBLK):
            sched += [(l, pi) for pi in range(10)]
        for b in range(NBLK):
            sched += [(l, pi) for pi in range(10, 26)]
    st = dict(next_load=0, next_use=0)

    def issue_loads(upto):
        while st['next_load'] <= min(upto, len(sched) - 1):
            k = st['next_load']
            l, pi = sched[k]
            s = k % NWS
            dma(wslot[s][:, :], wbf[l * NPIECE + pi], r=[('wbf', l, pi, qq) for qq in range(4)], w=[('w', s)])
            st['next_load'] += 1

    def get_piece(l, pi):
        k = st['next_use']
        assert sched[k] == (l, pi), (sched[k], l, pi)
        issue_loads(k + NWS - 1)
        st['next_use'] += 1
        return wslot[k % NWS], ('w', k % NWS)

    dma(cf[:, :], cfd[:, :], r=[], w=['cf'])
    dma(cvf[0][:, 0:384], cbd[:, :], r=[], w=[('cvf', 0)])
    cp('dve', cb[:, :], cvf[0][:, 0:384], r=[('cvf', 0)], w=['cb'])
    cp('dve', negmf[:, :], cvf[0][:, 128:256], r=[('cvf', 0)], w=['negmf'])
    ts('dve', nident[:, :], ident, -1.0, None, ALU.mult, None, r=['cb'], w=['nident'])
    memset('pool', onesb[:, :], 1.0, w=['onesb'])
    memset('pool', onesf[:, :], 1.0, w=['onesf'])
    memset('pool', pib[:, 0:1], PI, w=['pib'])
    memset('pool', mcol[0:64, 0:1], 1.0, w=['mcol'])
    memset('pool', mcol[64:128, 0:1], 0.0, w=['mcol'])
    memset('pool', mcol[0:64, 1:2], 0.0, w=['mcol'])
    memset('pool', mcol[64:128, 1:2], 1.0, w=['mcol'])
    memset('pool', pib[:, 1:2], -PI, w=['pib'])
    dma(lbt[:, :, :], lbd.rearrange("p (a b) -> p a b", a=3), r=[], w=['lbt'])
    act(lbw[:, :, :], lbt[:, :, :], AF.Exp, r=['lbt'], w=['lbw'])
    ssum, ksum = small()
    P.add('dve', lambda e: e.tensor_reduce(ssum[:, 0:3], lbw[:, :, :], AX.X, ALU.add), r=['lbw'], w=[ksum])
    rs, krs = small()
    P.add('dve', lambda e: e.reciprocal(rs[:, 0:3], ssum[:, 0:3]), r=[ksum], w=[krs])
    tt('dve', lbt[:, :, :], lbw[:, :, :], rs[:, 0:3].unsqueeze(2).broadcast_to([128, 3, 4]), ALU.mult,
       r=['lbw', krs], w=['lbt'])
    memset('dve', lbw[:, :, 0:1], 0.0, w=['lbw'])
    for l in range(1, 4):
        tt('dve', lbw[:, :, l:l + 1], lbw[:, :, l - 1:l], lbt[:, :, l:l + 1], ALU.add, r=['lbw', 'lbt'], w=['lbw'])
    for l in range(DEPTH):
        ts('dve', lba[:, :, l, 0:1], lbw[:, :, l:l + 1], 0.5, 0.5, ALU.mult, ALU.add, r=['lbw'], w=['lba'])
        ts('dve', lba[:, :, l, 1:2], lbw[:, :, l:l + 1], -0.5, 0.5, ALU.mult, ALU.add, r=['lbw'], w=['lba'])
        ts('dve', lba[:, :, l, 2:3], lbw[:, :, l:l + 1], 0.5, -0.5, ALU.mult, ALU.add, r=['lbw'], w=['lba'])

    for b in range(NBLK):
        for k in range(8):
            a, ka = A.get()
            dma(a[:, :], h0[:, k, b * TB:(b + 1) * TB], r=[], w=[ka])
            cp(['act', 'dve', 'pool'][k % 3], hbf[:, k, b * TB:(b + 1) * TB], a[:, :], r=[ka], w=[('h', b, k)])
            A.put((a, ka))

    for q in range(40):
        convert_unit(0, q)
    flush_convert()

    OFF = dict(convw=0, convb=28, ln1g=35, ln1b=43, ln2g=51, ln2b=59, glub=67, s5d=69, are=71, aim=79, ldt=87,
               dtb=95, alog=101, md=107, hnw=113, mnw=116, mdc=119)

    def pvc(name, i=0, n=1):
        o = OFF[name] + i
        return pv[:, o:o + n]

    def layer_setup(l):
        dma(pv[:, :], pvd[l], r=[], w=['pv'])
        for m in (0, 1, 4):
            i = cvi[0] % 2
            cvi[0] += 1
            dma(cvf[i][:, :], pmd[l * 5 + m], r=[], w=[('cvf', i)])
            if m < 2:
                cp('dve', btbf[:, m, :], cvf[i][:, :], r=[('cvf', i)], w=['btbf'])
            else:
                cp('dve', gwbf[:, :], cvf[i][:, 0:512], r=[('cvf', i)], w=['gwbf'])
        S = lambda i: s5s[:, i, :]
        ts('dve', S(0), pvc('are', 0, 8), -1e-4, None, ALU.min, None, r=['pv'], w=['s5s'])
        act(S(1), pvc('ldt', 0, 8), AF.Exp, r=['pv'], w=['s5s'])
        tt('dve', S(2), S(0), S(1), ALU.mult, r=['s5s'], w=['s5s'])
        act(S(3), S(2), AF.Exp, r=['s5s'], w=['s5s'])
        tt('dve', S(4), pvc('aim', 0, 8), S(1), ALU.mult, r=['pv', 's5s'], w=['s5s'])
        C1 = 6.28125
        C2 = 2 * PI - 6.28125
        I32 = mybir.dt.int32

        def reduce_to_pi(src, ksrc):
            qi, kqi = A.get()
            kf, kkf = A.get()
            ph, kph = A.get()
            ts('dve', qi[:, :].bitcast(I32), src[:, :], 1.0 / (2 * PI), None, ALU.mult, None, r=[ksrc], w=[kqi])
            cp('dve', kf[:, :], qi[:, :].bitcast(I32), r=[kqi], w=[kkf])
            stt('dve', ph[:, :], kf[:, :], -C1, src[:, :], ALU.mult, ALU.add, r=[kkf, ksrc], w=[kph])
            stt('dve', ph[:, :], kf[:, :], -C2, ph[:, :], ALU.mult, ALU.add, r=[kkf, kph], w=[kph])
            ts('dve', qi[:, :], ph[:, :], PI, 2 * PI, ALU.is_gt, ALU.mult, r=[kph], w=[kqi])
            tt('dve', ph[:, :], ph[:, :], qi[:, :], ALU.subtract, r=[kph, kqi], w=[kph])
            ts('dve', qi[:, :], ph[:, :], -PI, 2 * PI, ALU.is_lt, ALU.mult, r=[kph], w=[kqi])
            tt('dve', ph[:, :], ph[:, :], qi[:, :], ALU.add, r=[kph, kqi], w=[kph])
            A.put((qi, kqi), (kf, kkf))
            return ph, kph

        for j in range(8):
            a1, k1 = A.get()
            a2, k2 = A.get()
            ts('dve', a1[:, :], ramp, s5s[:, 4, j:j + 1], None, ALU.mult, None, r=['cf', 's5s'], w=[k1])
            ts('dve', a2[:, :], a1[:, :], 0.5 * PI, None, ALU.add, None, r=[k1], w=[k2])
            ph, kph = reduce_to_pi(a1, k1)
            act(nFim[:, j, :], ph[:, :], AF.Sin, r=[kph], w=[('nF', j)], scale=-1.0)
            A.put((ph, kph))
            ph, kph = reduce_to_pi(a2, k2)
            act(Fre[:, j, :], ph[:, :], AF.Sin, r=[kph], w=[('Fr', j)])
            A.put((ph, kph))
            A.put((a1, k1), (a2, k2))
        allF = [('Fr', j) for j in range(8)] + [('nF', j) for j in range(8)]
        tt('dve', S(5), S(3), Fre[:, :, 0], ALU.mult, r=['s5s'] + allF, w=['s5s'])
        tt('dve', S(6), S(3), nFim[:, :, 0], ALU.mult, r=['s5s'] + allF, w=['s5s'])
        tt('dve', S(7), S(0), S(0), ALU.mult, r=['s5s'], w=['s5s'])
        tt('dve', S(8), pvc('aim', 0, 8), pvc('aim', 0, 8), ALU.mult, r=['pv'], w=['s5s'])
        tt('dve', S(7), S(7), S(8), ALU.add, r=['s5s'], w=['s5s'])
        P.add('dve', lambda e: e.reciprocal(S(8), S(7)), r=['s5s'], w=['s5s'])
        ts('dve', S(9), S(5), -1.0, None, ALU.add, None, r=['s5s'], w=['s5s'])
        tt('dve', S(10), S(9), S(0), ALU.mult, r=['s5s'], w=['s5s'])
        tt('dve', S(11), S(6), pvc('aim', 0, 8), ALU.mult, r=['s5s', 'pv'], w=['s5s'])
        tt('dve', S(10), S(10), S(11), ALU.subtract, r=['s5s'], w=['s5s'])
        tt('dve', S(10), S(10), S(8), ALU.mult, r=['s5s'], w=['s5s'])
        tt('dve', S(11), S(6), S(0), ALU.mult, r=['s5s'], w=['s5s'])
        tt('dve', S(12), S(9), pvc('aim', 0, 8), ALU.mult, r=['s5s', 'pv'], w=['s5s'])
        tt('dve', S(11), S(11), S(12), ALU.add, r=['s5s'], w=['s5s'])
        tt('dve', S(11), S(11), S(8), ALU.mult, r=['s5s'], w=['s5s'])
        ts('dve', S(11), S(11), -1.0, None, ALU.mult, None, r=['s5s'], w=['s5s'])
        ire = cvi[0] % 2
        iim = 1 - ire
        cvi[0] += 2
        dma(cvf[ire][:, :], pmd[l * 5 + 2], r=[], w=[('cvf', ire)])
        dma(cvf[iim][:, :], pmd[l * 5 + 3], r=[], w=[('cvf', iim)])
        for j0, j1 in ((0, 3), (3, 6), (6, 8)):
            n = j1 - j0

            def g3(ap):
                return ap.rearrange("p (j c) -> p j c", j=n)
            sre = s5s[:, 10, j0:j1].unsqueeze(2).broadcast_to([128, n, 128])
            sim = s5s[:, 11, j0:j1].unsqueeze(2).broadcast_to([128, n, 128])
            cre = g3(cvf[ire][:, j0 * 128:j1 * 128])
            cim = g3(cvf[iim][:, j0 * 128:j1 * 128])
            obr = g3(ctbf[:, 0, j0 * 128:j1 * 128])
            obi = g3(ctbf[:, 1, j0 * 128:j1 * 128])
            a1, k1 = A.get()
            a2, k2 = A.get()
            t1 = g3(a1[:, 0:n * 128])
            t2 = g3(a2[:, 0:n * 128])
            tt('dve', t1, cre, sre, ALU.mult, r=[('cvf', ire), 's5s'], w=[k1])
            tt('dve', t2, cim, sim, ALU.mult, r=[('cvf', iim), 's5s'], w=[k2])
            tt('dve', obr, t1, t2, ALU.subtract, r=[k1, k2], w=['ctbf'])
            tt('dve', t1, cre, sim, ALU.mult, r=[('cvf', ire), 's5s'], w=[k1])
            tt('dve', t2, cim, sre, ALU.mult, r=[('cvf', iim), 's5s'], w=[k2])
            tt('dve', obi, t1, t2, ALU.add, r=[k1, k2], w=['ctbf'])
            ts('dve', g3(ctbf[:, 2, j0 * 128:j1 * 128]), obi, -1.0, None, ALU.mult, None, r=['ctbf'], w=['ctbf'])
            A.put((a1, k1), (a2, k2))
        for cj in range(28):
            ts('pool' if cj % 2 else 'dve', dgw[:, cj, :], ident, pvc('convw', cj), None, ALU.mult, None,
               r=['cb', 'pv'], w=['dgw'])
        for c in range(3):
            ts('dve', dgd[:, c, :], ident, pvc('mdc', c), None, ALU.mult, None, r=['cb', 'pv'], w=['dgd'])
        act(abc[:, 0:6], pvc('alog', 0, 6), AF.Exp, r=['pv'], w=['abc'])
        ts('dve', abc[:, 0:6], abc[:, 0:6], -1.0, None, ALU.mult, None, r=['abc'], w=['abc'])
        ts('dve', hgb[:, 0:2], pvc('glub', 0, 2), 0.5, None, ALU.mult, None, r=['pv'], w=['hgb'])
        for i in range(3):
            memset('pool', hS[i][:, :, :], 0.0, w=[('hS', i)])
            memset('pool', hSb[i][:, :, :], 0.0, w=[('hSb', i)])
        memset('pool', Sm[:, :, :], 0.0, w=['Sm'])
        memset('pool', Smb[:, :, :], 0.0, w=['Smb'])
        memset('pool', s5c[:, :, :], 0.0, w=['s5c'])
        memset('pool', XB[:, :, 0:3], 0.0, w=[('XB', c) for c in range(7)])

    FMMAP = {}
    for pc in range(3):
        FMMAP[('q', pc)] = (0, pc * 1024)
    FMMAP[('f', 0)] = (0, 3072)
    FMMAP[('f', 1)] = (1, 0)
    FMMAP[('f', 2)] = (1, 1024)
    for c in range(7):
        FMMAP[('x', c)] = (4, c * 1024) if c < 4 else (5, (c - 4) * 1024)
    for m in range(2):
        FMMAP[('u', m)] = (7, m * 1024)

    def hk(b, k):
        return ('h', b, k)

    def fm_proj(b, slot, kslot, off):
        ps, kp = PS.get()
        for k in range(8):
            pe_mm(ps[:, 0:TB], slot[:, off + k * 128: off + (k + 1) * 128], hbf[:, k, b * TB:(b + 1) * TB],
                  k == 0, k == 7, r=[kslot, hk(b, k)], w=[kp])
        return ps, kp

    def tm_proj(b, t, slot, kslot, n):
        ps, kp = PS.get()
        c0 = b * TB + t * 128
        for k in range(8):
            pe_mm(ps[:, 0:n], hbf[:, k, c0:c0 + 128], slot[:, k * n:(k + 1) * n], k == 0, k == 7,
                  r=[kslot, hk(b, k)], w=[kp])
        return ps, kp

    def v3(ap, h):
        return ap.rearrange("p (h v) -> p h v", h=h)

    def bc(ap, n, m):
        return ap.unsqueeze(2).broadcast_to([128, n, m])

    def rstd_small(ss, kss, n, inv, eps):
        ts('dve', ss[:, 0:n], ss[:, 0:n], inv, eps, ALU.mult, ALU.add, r=[kss], w=[kss])
        act(ss[:, 0:n], ss[:, 0:n], AF.Ln, r=[kss], w=[kss])
        act(ss[:, 0:n], ss[:, 0:n], AF.Exp, r=[kss], w=[kss], scale=-0.5)

    def layernorm(rr, gname, bname, dst_fn):
        p1, k1 = PS.get()
        p2, k2 = PS.get()
        for n in range(8):
            rb, krb = B.get()
            rq, krq = B.get()
            cp('act', rb[:, :], rr[n][0][:, :], r=[rr[n][1]], w=[krb])
            act(rq[:, :], rr[n][0][:, :], AF.Square, r=[rr[n][1]], w=[krq])
            pe_mm(p1[:, 0:TB], onesb[:, :], rb[:, :], n == 0, n == 7, r=['onesb', krb], w=[k1])
            pe_mm(p2[:, 0:TB], onesb[:, :], rq[:, :], n == 0, n == 7, r=['onesb', krq], w=[k2])
            B.put((rb, krb), (rq, krq))
        mean, kme = A.get()
        msq, kms = A.get()
        rstd, krs_ = A.get()
        ts('dve', mean[:, :], p1[:, 0:TB], 1.0 / 1024, None, ALU.mult, None, r=[k1], w=[kme])
        tt('pool', msq[:, :], mean[:, :], mean[:, :], ALU.mult, r=[kme], w=[kms])
        stt('dve', rstd[:, :], p2[:, 0:TB], 1.0 / 1024, msq[:, :], ALU.mult, ALU.subtract, r=[k2, kms], w=[krs_])
        ts('dve', rstd[:, :], rstd[:, :], LN_EPS, None, ALU.add, None, r=[krs_], w=[krs_])
        act(rstd[:, :], rstd[:, :], AF.Ln, r=[krs_], w=[krs_])
        act(rstd[:, :], rstd[:, :], AF.Exp, r=[krs_], w=[krs_], scale=-0.5)
        PS.put((p1, k1), (p2, k2))
        for n in range(8):
            t1, kt1 = A.get()
            tt('pool', t1[:, :], rr[n][0][:, :], mean[:, :], ALU.subtract, r=[rr[n][1], kme], w=[kt1])
            tt('dve', t1[:, :], t1[:, :], rstd[:, :], ALU.mult, r=[kt1, krs_], w=[kt1])
            dst, kd, post = dst_fn(n)
            act(dst, t1[:, :], AF.Identity, r=[kt1, 'pv'], w=kd, bias=pvc(bname, n), scale=pvc(gname, n))
            A.put((t1, kt1))
            if post is not None:
                post()
        A.put((mean, kme), (msq, kms), (rstd, krs_))

    def mixer_block(l, b):
        c0 = b * TB
        yT = [B.get() for _ in range(8)]
        wmap = {}

        def wp_(pi):
            if pi not in wmap:
                wmap[pi] = get_piece(l, pi)
            return wmap[pi]
        qs, tth = [], []
        for pc in range(3):
            pi, off = FMMAP[('q', pc)]
            ps, kp = fm_proj(b, wp_(pi)[0], wp_(pi)[1], off)
            a, ka = A.get()
            act(a[:, :], ps[:, 0:TB], AF.Silu, r=[kp], w=[ka])
            PS.put((ps, kp))
            qs.append((a, ka))
        for pc in range(3):
            pi, off = FMMAP[('f', pc)]
            ps, kp = fm_proj(b, wp_(pi)[0], wp_(pi)[1], off)
            a, ka = A.get()
            act(a[:, :], ps[:, 0:TB], AF.Tanh, r=[kp], w=[ka], scale=0.5)
            PS.put((ps, kp))
            tth.append((a, ka))
        if STAGE <= 2:
            raise _Stop()
        qT, kT = [], []
        for pc in range(3):
            t_, kt = tth[pc]
            lf, klf = A.get()
            act(lf[:, :], t_[:, :], AF.Ln, r=[kt, 'lba'], w=[klf], bias=lba[:, pc, l, 0:1], scale=lba[:, pc, l, 1:2])
            kk, kkk = A.get()
            ts('dve', kk[:, :], t_[:, :], lba[:, pc, l, 2:3], lba[:, pc, l, 1:2], ALU.mult, ALU.add,
               r=[kt, 'lba'], w=[kkk])
            A.put(tth[pc])
            G, kG = A.get()
            P.add('dve', lambda e, G=G, lf=lf: e.tensor_tensor_scan(G[:, :], rstm, lf[:, :], 0.0, ALU.mult, ALU.add),
                  r=['cf', klf], w=[kG])
            A.put((lf, klf))
            eG, keG = A.get()
            enG, kenG = A.get()
            act(eG[:, :], G[:, :], AF.Exp, r=[kG], w=[keG])
            act(enG[:, :], G[:, :], AF.Exp, r=[kG], w=[kenG], scale=-1.0)
            if b == 0 and pc == 0 and l == 0:
                dump('G', G[:, :], kG)
                dump('qs', qs[pc][0][:, :], qs[pc][1])
                dump('kk', kk[:, :], kkk)
                dump('eG', eG[:, :], keG)
                dump('enG', enG[:, :], kenG)
            A.put((G, kG))
            cp('pool', egl[:, pc, 0:6], v3(eG[:, :], 6)[:, :, 63], r=[keG], w=['egl'])
            q_, kq = B.get()
            q2_, kq2 = B.get()
            k_, kk_ = B.get()
            stt('dve', q_[:, :], qs[pc][0][:, :], mcol[:, 0:1], eG[:, :], ALU.mult, ALU.mult, r=[qs[pc][1], keG, 'mcol'], w=[kq])
            stt('dve', q2_[:, :], qs[pc][0][:, :], mcol[:, 1:2], eG[:, :], ALU.mult, ALU.mult, r=[qs[pc][1], keG, 'mcol'], w=[kq2])
            tt('pool', k_[:, :], kk[:, :], enG[:, :], ALU.mult, r=[kkk, kenG], w=[kk_])
            A.put(qs[pc], (kk, kkk), (eG, keG), (enG, kenG))
            qT.append(((q_, kq), (q2_, kq2)))
            kT.append((k_, kk_))
        if STAGE <= 1:
            raise _Stop()
        wi = get_piece(l, 2)
        vb = []
        for t in range(NT):
            ps, kp = tm_proj(b, t, wi[0], wi[1], 384)
            v, kv = B.get()
            cp('act', v[:, :], ps[:, 0:384], r=[kp], w=[kv])
            PS.put((ps, kp))
            vb.append((v, kv))
        wg = get_piece(l, 3)
        sg = []
        for t in range(NT):
            ps, kp = tm_proj(b, t, wg[0], wg[1], 384)
            a, ka = A.get()
            act(a[:, :], ps[:, 0:384], AF.Silu, r=[kp], w=[ka])
            PS.put((ps, kp))
            sg.append((a, ka))
        if STAGE <= 3:
            raise _Stop()
        def hg_tile(t):
            if True:
                tc0 = t * 128
                pst, kpt = PS.get()
                pstb = pst.bitcast(BF16)
                for pc in range(3):
                    pe_tr(pstb[:, pc * 128:(pc + 1) * 128], kT[pc][0][:, tc0:tc0 + 128], r=[kT[pc][1]], w=[kpt])
                ktm, kktm = B.get()
                cp('dve', ktm[:, :], pstb[:, 0:384], r=[kpt], w=[kktm])
                PS.put((pst, kpt))
                yield
                pss = [PS.get(), PS.get()]
                for h in range(6):
                    pc, po = h // 2, 64 * (h % 2)
                    ps, kp = pss[h // 3]
                    qm, kqm = qT[pc][h % 2]
                    pe_mm(ps[:, (h % 3) * 128:(h % 3 + 1) * 128], kT[pc][0][:, tc0:tc0 + 128],
                          qm[:, tc0:tc0 + 128], True, True, r=[kT[pc][1], kqm], w=[kp])
                yield
                AT = [B.get(), B.get()]
                for g in range(2):
                    tt('dve', v3(AT[g][0][:, :], 3), v3(pss[g][0][:, 0:384], 3),
                       mask2.unsqueeze(1).broadcast_to([128, 3, 128]), ALU.mult, r=[pss[g][1], 'cb'], w=[AT[g][1]])
                    PS.put(pss[g])
                yield
                gci0 = (b * NT + t) * 2
                for c in range(2):
                    cur, nw = (gci0 + c) % 3, (gci0 + c + 1) % 3
                    ps, kp = PS.get()
                    r0 = 64 * c
                    for h in range(6):
                        pc, po = h // 2, 64 * (h % 2)
                        pe_mm(ps[po:po + 64, pc * 64:(pc + 1) * 64], ktm[r0:r0 + 64, h * 64:(h + 1) * 64],
                              vb[t][0][r0:r0 + 64, h * 64:(h + 1) * 64], True, True, r=[kktm, vb[t][1]], w=[kp])
                    tmp, ktmp = A.get()
                    tt('dve', v3(tmp[:, 0:192], 3), v3(ps[:, 0:192], 3), hS[cur][:, :, :], ALU.add,
                       r=[kp, ('hS', cur)], w=[ktmp])
                    PS.put((ps, kp))
                    tt('dve', hS[nw][:, :, :], v3(tmp[:, 0:192], 3),
                       egl[:, :, 2 * t + c:2 * t + c + 1].broadcast_to([128, 3, 64]), ALU.mult,
                       r=[ktmp, 'egl'], w=[('hS', nw)])
                    tt('dve', hSb[nw][:, :, :], v3(tmp[:, 0:192], 3),
                       egl[:, :, 2 * t + c:2 * t + c + 1].broadcast_to([128, 3, 64]), ALU.mult,
                       r=[ktmp, 'egl'], w=[('hSb', nw)])
                    A.put((tmp, ktmp))
                yield
                pso, kpo = PS.get()
                for h in range(6):
                    pc, po = h // 2, 64 * (h % 2)
                    s0, s1 = gci0 % 3, (gci0 + 1) % 3
                    pe_mm(pso[:, h * 64:(h + 1) * 64], AT[h // 3][0][:, (h % 3) * 128:(h % 3 + 1) * 128],
                          vb[t][0][:, h * 64:(h + 1) * 64], True, False, r=[AT[h // 3][1], vb[t][1]], w=[kpo])
                    qm, kqm = qT[pc][h % 2]
                    pe_mm(pso[0:64, h * 64:(h + 1) * 64], qm[:, tc0:tc0 + 64],
                          hSb[s0][:, pc, :], False, True, r=[kqm, ('hSb', s0)], w=[kpo])
                    pe_mm(pso[64:128, h * 64:(h + 1) * 64], qm[:, tc0 + 64:tc0 + 128],
                          hSb[s1][:, pc, :], False, True, r=[kqm, ('hSb', s1)], w=[kpo])
                B.put(*AT)
                B.put((ktm, kktm))
                yield
                sq, ksq = A.get()
                act(sq[:, :], pso[:, 0:384], AF.Square, r=[kpo], w=[ksq])
                ss, kss = small()
                P.add('dve', lambda e, ss=ss, sq=sq: e.tensor_reduce(ss[:, 0:6], v3(sq[:, :], 6), AX.X, ALU.add),
                      r=[ksq], w=[kss])
                rstd_small(ss, kss, 6, 1.0 / 64, RMS_EPS)
                tt('dve', v3(sq[:, :], 6), v3(pso[:, 0:384], 6), bc(ss[:, 0:6], 6, 64), ALU.mult, r=[kpo, kss], w=[ksq])
                PS.put((pso, kpo))
                yield
                yb_, kyb = B.get()
                tt('pool', yb_[:, :], sq[:, :], sg[t][0][:, :], ALU.mult, r=[ksq, sg[t][1]], w=[kyb])
                A.put((sq, ksq))
                pst, kpt = PS.get()
                pstb = pst.bitcast(BF16)
                for pc in range(3):
                    pe_tr(pstb[:, pc * 128:(pc + 1) * 128], yb_[:, pc * 128:(pc + 1) * 128], r=[kyb], w=[kpt])
                for pc in range(3):
                    ts('dve', yT[pc][0][:, tc0:tc0 + 128], pstb[:, pc * 128:(pc + 1) * 128], pvc('hnw', pc), None,
                       ALU.mult, None, r=[kpt, 'pv'], w=[yT[pc][1]])
                PS.put((pst, kpt))
                yield
                B.put((yb_, kyb))

        def hg_gen():
            tiles = [hg_tile(t) for t in range(NT)]
            started, live, step = 0, [], 0
            while started < NT or live:
                if started < NT and step % HG_LAG == 0:
                    live.append(tiles[started])
                    started += 1
                for g in list(live):
                    try:
                        next(g)
                    except StopIteration:
                        live.remove(g)
                step += 1
                yield
            for x in kT + vb:
                B.put(x)
            for x in qT:
                B.put(x[0], x[1])
            for x in sg:
                A.put(x)
            yield

        xc = []
        sz = []
        def prep_gen():
            allXB = [('XB', c) for c in range(7)]
            if b > 0:
                cp('dve', XB[:, :, 0:3], XB[:, :, 384:387], r=allXB, w=allXB)
            for c in range(7):
                pi, off = FMMAP[('x', c)]
                ps, kp = fm_proj(b, wp_(pi)[0], wp_(pi)[1], off)
                cp('act', XB[:, c, 3:387], ps[:, 0:TB], r=[kp], w=[('XB', c)])
                PS.put((ps, kp))
                yield
                ps2, kp2 = PS.get()
                for j in range(4):
                    pe_mm(ps2[:, 0:TB], dgw[:, c * 4 + j, :], XB[:, c, j:j + TB], j == 0, j == 3,
                          r=['dgw', ('XB', c)], w=[kp2])
                x_, kx = B.get()
                act(x_[:, :], ps2[:, 0:TB], AF.Silu, r=[kp2, 'pv'], w=[kx], bias=pvc('convb', c))
                PS.put((ps2, kp2))
                xc.append((x_, kx))
                yield
            w6 = get_piece(l, 6)
            for t in range(NT):
                ps, kp = tm_proj(b, t, w6[0], w6[1], 390)
                a, ka = A.get()
                act(a[:, :], ps[:, 0:384], AF.Silu, r=[kp], w=[ka])
                cp('act', dtt[:, t, 0:6], ps[:, 384:390], r=[kp], w=[('dtt', t)])
                tt('dve', dtt[:, t, 0:6], dtt[:, t, 0:6], pvc('dtb', 0, 6), ALU.add, r=[('dtt', t), 'pv'], w=[('dtt', t)])
                yield
                PS.put((ps, kp))
                sz.append((a, ka))
            for t in range(NT):
                act(dtt[:, t, 0:6], dtt[:, t, 0:6], AF.Exp, r=[('dtt', t)], w=[('dtt', t)])
            for t in range(NT):
                act(dtt[:, t, 0:6], dtt[:, t, 0:6], AF.Ln, r=[('dtt', t)], w=[('dtt', t)], bias=1.0)
                tt('dve', dtt[:, t, 8:14], dtt[:, t, 0:6], abc[:, 0:6], ALU.mult, r=[('dtt', t), 'abc'], w=[('dtt', t)])
            if STAGE == 45:
                raise _Stop()
            yield

        gens0 = [(hg_gen(), 1), (prep_gen(), 2)]
        alive0 = list(gens0)
        while alive0:
            for g in list(alive0):
                for _ in range(g[1]):
                    try:
                        next(g[0])
                    except StopIteration:
                        alive0.remove(g)
                        break

        if STAGE <= 4:
            raise _Stop()
        def ssd_gen():
            for t in range(NT):
                tc0 = t * 128
                kdt = ('dtt', t)
                dtv = dtt[:, t, 0:6]
                dA = dtt[:, t, 8:14]
                pst, kpt = PS.get()
                pstb = pst.bitcast(BF16)
                for j in range(5):
                    pe_tr(pstb[:, j * 128:(j + 1) * 128], xc[j][0][:, tc0:tc0 + 128], r=[xc[j][1]], w=[kpt])
                if STAGE == 461 and t == TCUT:
                    raise _Stop()
                xtm, kxtm = B.get()
                if VAR == 1:
                    _sp = B.get()
                btm, kbtm = B.get()
                cp('act', xtm[:, :], pstb[:, 0:384], r=[kpt], w=[kxtm])
                if STAGE == 462 and t == TCUT:
                    raise _Stop()
                cp('act', btm[:, 0:256], pstb[:, 384:640], r=[kpt], w=[kbtm])
                PS.put((pst, kpt))
                if STAGE == 460 and t == TCUT:
                    raise _Stop()
                yield
                pa, kpa = PS.get()
                pe_mm(pa[:, 0:6], tri, dA, True, True, r=['cf', kdt], w=[kpa])
                pe_mm(pa[:, 8:14], onesf[:, :], dA, True, True, r=['onesf', kdt], w=[kpa])
                ac = acs[:, t, :]
                kac = ('acs', t)
                cp('dve', ac[:, 0:14], pa[:, 0:14], r=[kpa], w=[kac])
                PS.put((pa, kpa))
                if STAGE == 46 and t == TCUT:
                    raise _Stop()
                act(ac[:, 16:30], ac[:, 0:14], AF.Exp, r=[kac], w=[kac])
                ts('dve', ac[:, 32:38], ac[:, 0:6], -1.0, None, ALU.mult, None, r=[kac], w=[kac])
                tt('dve', ac[:, 40:46], ac[:, 8:14], ac[:, 0:6], ALU.subtract, r=[kac], w=[kac])
                act(ac[:, 40:46], ac[:, 40:46], AF.Exp, r=[kac], w=[kac])
                eac = ac[:, 16:22]
                etot = ac[:, 24:30]
                dec = ac[:, 40:46]
                yield
                dAb = [A.get(), A.get()]
                for g in range(2):
                    cp('pool', v3(dAb[g][0][:, :], 3), bc(dA[:, 3 * g:3 * g + 3], 3, 128), r=[kdt], w=[dAb[g][1]])
                pL = [PS.get(), PS.get()]
                for h in range(6):
                    ps, kp = pL[h // 3]
                    o = (h % 3) * 128
                    pe_mm(ps[:, o:o + 128], dAb[h // 3][0][:, o:o + 128], tri, True, False, r=[dAb[h // 3][1], 'cf'], w=[kp])
                    pe_mm(ps[:, o:o + 128], identf, negmf[:, :], False, True, r=['cf', 'negmf'], w=[kp])
                LTs = [A.get(), A.get()]
                for h in range(6):
                    o = (h % 3) * 128
                    act(LTs[h // 3][0][:, o:o + 128], pL[h // 3][0][:, o:o + 128], AF.Exp, r=[pL[h // 3][1], kac],
                        w=[LTs[h // 3][1]], bias=ac[:, 32 + h:33 + h])
                PS.put(*pL)
                A.put(*dAb)
                if STAGE == 47 and t == TCUT:
                    raise _Stop()
                yield
                psg, kpsg = PS.get()
                for g in range(2):
                    pe_mm(psg[:, g * 128:(g + 1) * 128], xc[3 + g][0][:, tc0:tc0 + 128], xc[5 + g][0][:, tc0:tc0 + 128],
                          True, True, r=[xc[3 + g][1], xc[5 + g][1]], w=[kpsg])
                AT = [B.get(), B.get()]
                for g in range(2):
                    tt('dve', v3(AT[g][0][:, :], 3), v3(LTs[g][0][:, :], 3),
                       psg[:, g * 128:(g + 1) * 128].unsqueeze(1).broadcast_to([128, 3, 128]), ALU.mult,
                       r=[LTs[g][1], kpsg], w=[AT[g][1]])
                PS.put((psg, kpsg))
                A.put(*LTs)
                if STAGE == 48 and t == TCUT:
                    raise _Stop()
                yield
                xdt, kxdt = B.get()
                xw, kxw = B.get()
                tt('pool', v3(xdt[:, :], 6), v3(xtm[:, :], 6), bc(dtv, 6, 64), ALU.mult, r=[kxtm, kdt], w=[kxdt])
                tt('pool', v3(xw[:, :], 6), v3(xdt[:, :], 6), bc(dec, 6, 64), ALU.mult, r=[kxdt, kac], w=[kxw])
                psy, kpsy = PS.get()
                psyo, kpsyo = PS.get()
                psS, kpsS = PS.get()
                for h in range(6):
                    g = h // 3
                    hs = slice(h * 64, (h + 1) * 64)
                    pe_mm(psy[:, hs], xc[h // 2][0][:, tc0:tc0 + 128], dgd[:, h // 2, (h % 2) * 64:(h % 2 + 1) * 64],
                          True, False, r=[xc[h // 2][1], 'dgd'], w=[kpsy])
                    pe_mm(psy[:, hs], AT[g][0][:, (h % 3) * 128:(h % 3 + 1) * 128], xdt[:, hs], False, True,
                          r=[AT[g][1], kxdt], w=[kpsy])
                for h in range(6):
                    g = h // 3
                    hs = slice(h * 64, (h + 1) * 64)
                    pe_mm(psyo[:, hs], xc[5 + g][0][:, tc0:tc0 + 128], Smb[:, h, :], True, True,
                          r=[xc[5 + g][1], 'Smb'], w=[kpsyo])
                for h in range(6):
                    g = h // 3
                    hs = slice(h * 64, (h + 1) * 64)
                    pe_mm(psS[:, hs], btm[:, g * 128:(g + 1) * 128], xw[:, hs], True, True, r=[kbtm, kxw], w=[kpsS])
                B.put(*AT)
                if STAGE == 49 and t == TCUT:
                    raise _Stop()
                yield
                t1, kt1 = A.get()
                t2, kt2 = A.get()
                tt('dve', v3(t1[:, :], 6), v3(psyo[:, 0:384], 6), bc(eac, 6, 64), ALU.mult, r=[kpsyo, kac], w=[kt1])
                PS.put((psyo, kpsyo))
                tt('dve', t1[:, :], psy[:, 0:384], t1[:, :], ALU.add, r=[kpsy, kt1], w=[kt1])
                PS.put((psy, kpsy))
                tt('pool', t1[:, :], t1[:, :], sz[t][0][:, :], ALU.mult, r=[kt1, sz[t][1]], w=[kt1])
                act(t2[:, :], t1[:, :], AF.Square, r=[kt1], w=[kt2])
                ss, kss = small()
                P.add('dve', lambda e, ss=ss, t2=t2: e.tensor_reduce(ss[:, 0:2], v3(t2[:, :], 2), AX.X, ALU.add),
                      r=[kt2], w=[kss])
                rstd_small(ss, kss, 2, 1.0 / 192, RMS_EPS)
                yb_, kyb = B.get()
                tt('pool', v3(yb_[:, :], 2), v3(t1[:, :], 2), bc(ss[:, 0:2], 2, 192), ALU.mult, r=[kt1, kss], w=[kyb])
                if STAGE == 50 and t == TCUT:
                    raise _Stop()
                yield
                tt('dve', v3(t2[:, :], 6), Sm[:, :, :], bc(etot, 6, 64), ALU.mult, r=['Sm', kac], w=[kt2])
                tt('dve', Sm[:, :, :], v3(t2[:, :], 6), v3(psS[:, 0:384], 6), ALU.add, r=[kt2, kpsS], w=['Sm'])
                PS.put((psS, kpsS))
                cp('act', Smb[:, :, :], Sm[:, :, :], r=['Sm'], w=['Smb'])
                if STAGE == 51 and t == TCUT:
                    raise _Stop()
                yield
                A.put((t1, kt1), (t2, kt2))
                pst, kpt = PS.get()
                pstb = pst.bitcast(BF16)
                for pc in range(3):
                    pe_tr(pstb[:, pc * 128:(pc + 1) * 128], yb_[:, pc * 128:(pc + 1) * 128], r=[kyb], w=[kpt])
                for pc in range(3):
                    act(yT[3 + pc][0][:, tc0:tc0 + 128], pstb[:, pc * 128:(pc + 1) * 128], AF.Copy,
                        r=[kpt, 'pv'], w=[yT[3 + pc][1]], scale=pvc('mnw', pc))
                PS.put((pst, kpt))
                yield
                B.put((yb_, kyb), (xtm, kxtm), (btm, kbtm), (xdt, kxdt), (xw, kxw))
                if STAGE == 52 + t:
                    raise _Stop()
            for x in xc:
                B.put(x)
            for x in sz:
                A.put(x)


        if STAGE <= 5:
            raise _Stop()
        def s5_gen():
            w7 = get_piece(l, 7)
            uf, ub = [], []
            for m in range(2):
                pi, off = FMMAP[('u', m)]
                ps, kp = fm_proj(b, w7[0], w7[1], off)
                a, ka = A.get()
                u_, ku = B.get()
                cp('act', a[:, :], ps[:, 0:TB], r=[kp], w=[ka])
                cp('act', u_[:, :], ps[:, 0:TB], r=[kp], w=[ku])
                PS.put((ps, kp))
                uf.append((a, ka))
                ub.append((u_, ku))
            if STAGE == 60:
                raise _Stop()
            gl = []
            gelb = []
            for m in range(2):
                py, kpy = PS.get()

                def chunk_gen(jj, m=m, py=py, kpy=kpy):
                    j = 4 * m + jj
                    kF, kN = ('Fr', j), ('nF', j)
                    pA, kpA = PS.get()
                    pB, kpB = PS.get()
                    pe_mm(pA[:, 0:TB], btbf[:, 0, j * 128:(j + 1) * 128], ub[m][0][:, :], True, True,
                          r=['btbf', ub[m][1]], w=[kpA])
                    pe_mm(pB[:, 0:TB], btbf[:, 1, j * 128:(j + 1) * 128], ub[m][0][:, :], True, True,
                          r=['btbf', ub[m][1]], w=[kpB])
                    yield
                    a_bf, kab = B.get()
                    b_bf, kbb = B.get()
                    cp('act', a_bf[:, :], pA[:, 0:TB], r=[kpA], w=[kab])
                    cp('act', b_bf[:, :], pB[:, 0:TB], r=[kpB], w=[kbb])
                    PS.put((pA, kpA), (pB, kpB))
                    qq = [B.get() for _ in range(4)]
                    tt('dve', qq[0][0][:, :], Fre[:, j, :], a_bf[:, :], ALU.mult, r=[kF, kab], w=[qq[0][1]])
                    tt('pool', qq[1][0][:, :], nFim[:, j, :], b_bf[:, :], ALU.mult, r=[kN, kbb], w=[qq[1][1]])
                    tt('dve', qq[2][0][:, :], Fre[:, j, :], b_bf[:, :], ALU.mult, r=[kF, kbb], w=[qq[2][1]])
                    tt('pool', qq[3][0][:, :], nFim[:, j, :], a_bf[:, :], ALU.mult, r=[kN, kab], w=[qq[3][1]])
                    B.put((a_bf, kab), (b_bf, kbb))
                    yield
                    pre, kpre = PS.get()
                    pim, kpim = PS.get()
                    pe_mm(pre[:, 0:TB], ident, qq[0][0][:, :], True, False, r=['cb', qq[0][1]], w=[kpre])
                    pe_mm(pre[:, 0:TB], nident[:, :], qq[1][0][:, :], False, True, r=['nident', qq[1][1]], w=[kpre])
                    pe_mm(pim[:, 0:TB], ident, qq[2][0][:, :], True, False, r=['cb', qq[2][1]], w=[kpim])
                    pe_mm(pim[:, 0:TB], ident, qq[3][0][:, :], False, True, r=['cb', qq[3][1]], w=[kpim])
                    B.put(*qq)
                    if STAGE == 61 and jj == 0 and m == 0:
                        raise _Stop()
                    yield
                    t2, k2 = A.get()
                    t4, k4 = A.get()
                    rj = s5s[:, 3, j:j + 1].broadcast_to([128, TB])
                    P.add('dve', lambda e, o=t2, rj=rj, p=pre, i=s5c[:, 0, j:j + 1]:
                          e.tensor_tensor_scan(o[:, :], rj, p[:, 0:TB], i, ALU.mult, ALU.add),
                          r=['s5s', kpre, ('s5c', j)], w=[k2])
                    P.add('dve', lambda e, o=t4, rj=rj, p=pim, i=s5c[:, 1, j:j + 1]:
                          e.tensor_tensor_scan(o[:, :], rj, p[:, 0:TB], i, ALU.mult, ALU.add),
                          r=['s5s', kpim, ('s5c', j)], w=[k4])
                    PS.put((pre, kpre), (pim, kpim))
                    if STAGE == 62 and jj == 0 and m == 0:
                        raise _Stop()
                    yield
                    bb = [B.get() for _ in range(4)]
                    tt('pool', bb[0][0][:, :], Fre[:, j, :], t2[:, :], ALU.mult, r=[kF, k2], w=[bb[0][1]])
                    tt('dve', bb[1][0][:, :], nFim[:, j, :], t4[:, :], ALU.mult, r=[kN, k4], w=[bb[1][1]])
                    tt('pool', bb[2][0][:, :], nFim[:, j, :], t2[:, :], ALU.mult, r=[kN, k2], w=[bb[2][1]])
                    tt('pool', bb[3][0][:, :], Fre[:, j, :], t4[:, :], ALU.mult, r=[kF, k4], w=[bb[3][1]])
                    cc, kcc = small()
                    e = TB - 1
                    tt('dve', cc[:, 0:1], Fre[:, j, e:e + 1], t2[:, e:e + 1], ALU.mult, r=[kF, k2], w=[kcc])
                    tt('dve', cc[:, 1:2], nFim[:, j, e:e + 1], t2[:, e:e + 1], ALU.mult, r=[kN, k2], w=[kcc])
                    tt('dve', cc[:, 2:3], nFim[:, j, e:e + 1], t4[:, e:e + 1], ALU.mult, r=[kN, k4], w=[kcc])
                    tt('dve', cc[:, 3:4], Fre[:, j, e:e + 1], t4[:, e:e + 1], ALU.mult, r=[kF, k4], w=[kcc])
                    tt('dve', s5c[:, 0, j:j + 1], cc[:, 0:1], cc[:, 2:3], ALU.add, r=[kcc], w=[('s5c', j)])
                    tt('dve', s5c[:, 1, j:j + 1], cc[:, 3:4], cc[:, 1:2], ALU.subtract, r=[kcc], w=[('s5c', j)])
                    if STAGE == 63 and jj == 0 and m == 0:
                        raise _Stop()
                    yield
                    A.put((t2, k2), (t4, k4))
                    for q_, ci_ in enumerate((0, 0, 1, 2)):
                        pe_mm(py[:, 0:TB], ctbf[:, ci_, j * 128:(j + 1) * 128], bb[q_][0][:, :],
                              jj == 0 and q_ == 0, jj == 3 and q_ == 3, r=['ctbf', bb[q_][1]], w=[kpy])
                    B.put(*bb)

                for pair in ((0, 1), (2, 3)):
                    live = [chunk_gen(jj) for jj in pair]
                    while live:
                        for g in list(live):
                            try:
                                next(g)
                            except StopIteration:
                                live.remove(g)
                        yield
                yv, kyv = A.get()
                stt('dve', yv[:, :], uf[m][0][:, :], pvc('s5d', m), py[:, 0:TB], ALU.mult, ALU.add,
                    r=[uf[m][1], 'pv', kpy], w=[kyv])
                PS.put((py, kpy))
                A.put(uf[m])
                yield
                gb, kgb = B.get()
                if GELU_ACT:
                    act(gb[:, :], yv[:, :], AF.Gelu_apprx_tanh, r=[kyv], w=[kgb])
                    A.put((yv, kyv))
                    gl.append(None)
                else:
                    y2, ky2 = A.get()
                    tt('pool', y2[:, :], yv[:, :], yv[:, :], ALU.mult, r=[kyv], w=[ky2])
                    ts('dve', y2[:, :], y2[:, :], 0.044715, 1.0, ALU.mult, ALU.add, r=[ky2], w=[ky2])
                    tt('pool', y2[:, :], y2[:, :], yv[:, :], ALU.mult, r=[ky2, kyv], w=[ky2])
                    act(y2[:, :], y2[:, :], AF.Tanh, r=[ky2], w=[ky2], scale=0.7978845608028654)
                    stt('dve', yv[:, :], y2[:, :], 1.0, yv[:, :], ALU.add, ALU.mult, r=[ky2, kyv], w=[kyv])
                    ts('dve', gb[:, :], yv[:, :], 0.5, None, ALU.mult, None, r=[kyv], w=[kgb])
                    A.put((y2, ky2))
                    gl.append((yv, kyv))
                gelb.append((gb, kgb))
            yield
            for m2 in range(2):
                ps, kp = PS.get()
                for m in range(2):
                    pe_mm(ps[:, 0:TB], gwbf[:, m * 256 + m2 * 128: m * 256 + (m2 + 1) * 128], gelb[m][0][:, :],
                          m == 0, m == 1, r=['gwbf', gelb[m][1]], w=[kp])
                th, kth = A.get()
                act(th[:, :], ps[:, 0:TB], AF.Tanh, r=[kp, 'hgb'], w=[kth], bias=hgb[:, m2:m2 + 1], scale=0.5)
                PS.put((ps, kp))
                if GELU_ACT:
                    ts('dve', th[:, :], th[:, :], 0.5, 0.5, ALU.mult, ALU.add, r=[kth], w=[kth])
                    tt('dve', yT[6 + m2][0][:, :], gelb[m2][0][:, :], th[:, :], ALU.mult, r=[gelb[m2][1], kth],
                       w=[yT[6 + m2][1]])
                else:
                    ts('dve', th[:, :], th[:, :], 0.25, 0.25, ALU.mult, ALU.add, r=[kth], w=[kth])
                    tt('dve', yT[6 + m2][0][:, :], gl[m2][0][:, :], th[:, :], ALU.mult, r=[gl[m2][1], kth],
                       w=[yT[6 + m2][1]])
                A.put((th, kth))
            for m in range(2):
                if gl[m] is not None:
                    A.put(gl[m])
                B.put(gelb[m], ub[m])

            yield

        if mid_hook[0] is not None:
            mid_hook[0]()
        gens = [(ssd_gen(), 1), (s5_gen(), 1)]
        alive = list(gens)
        while alive:
            for g in list(alive):
                for _ in range(g[1]):
                    try:
                        next(g[0])
                    except StopIteration:
                        alive.remove(g)
                        break
            conv_pump()
        conv_drain()
        if STAGE <= 6:
            raise _Stop()
        if DEBUG and l == 0:
            for k in range(8):
                a, ka = A.get()
                cp('dve', a[:, :], yT[k][0][:, :], r=[yT[k][1]], w=[ka])
                dma(dbgd[b * 16 + k], a[:, :], r=[ka], w=[('dbg', b, k)])
                A.put((a, ka))
        rr = []
        for n in range(8):
            slot, kslot = wp_(8 if n < 4 else 9)
            off = (n % 4) * 1024
            ps, kp = PS.get()
            for k in range(8):
                pe_mm(ps[:, 0:TB], slot[:, off + k * 128: off + (k + 1) * 128], yT[k][0][:, :], k == 0, k == 7,
                      r=[kslot, yT[k][1]], w=[kp])
            r_, kr = A.get()
            stt('dve', r_[:, :], hbf[:, n, c0:c0 + TB], ALPHA, ps[:, 0:TB], ALU.mult, ALU.add, r=[hk(b, n), kp], w=[kr])
            PS.put((ps, kp))
            rr.append((r_, kr))
        for x in yT:
            B.put(x)
        layernorm(rr, 'ln1g', 'ln1b', lambda n: (hbf[:, n, c0:c0 + TB], [hk(b, n)], None))
        for x in rr:
            A.put(x)
        if DEBUG and l == 0:
            for k in range(8):
                a, ka = A.get()
                cp('dve', a[:, :], hbf[:, k, c0:c0 + TB], r=[hk(b, k)], w=[ka])
                dma(dbgd[b * 16 + 8 + k], a[:, :], r=[ka], w=[('dbg', b, 8 + k)])
                A.put((a, ka))

    def ffn_block(l, b, last):
        if STAGE <= 7:
            raise _Stop()
        c0 = b * TB
        hid = []
        for hg in range(8):
            wp = get_piece(l, 10 + hg)
            for hc in range(4):
                ps, kp = fm_proj(b, wp[0], wp[1], hc * 1024)
                rl, krl = B.get()
                if (hg * 4 + hc) % 2 == 0:
                    act(rl[:, :], ps[:, 0:TB], AF.Relu, r=[kp], w=[krl])
                else:
                    ts('dve', rl[:, :], ps[:, 0:TB], 0.0, None, ALU.max, None, r=[kp], w=[krl])
                PS.put((ps, kp))
                hd, khd = B.get()
                tt('pool', hd[:, :], rl[:, :], rl[:, :], ALU.mult, r=[krl], w=[khd])
                B.put((rl, krl))
                hid.append((hd, khd))
            yield
        rr = []
        for n in range(8):
            wp = get_piece(l, 18 + n)
            ps, kp = PS.get()
            for j in range(32):
                pe_mm(ps[:, 0:TB], wp[0][:, j * 128:(j + 1) * 128], hid[j][0][:, :], j == 0, j == 31,
                      r=[wp[1], hid[j][1]], w=[kp])
            r_, kr = A.get()
            stt('dve', r_[:, :], hbf[:, n, c0:c0 + TB], ALPHA, ps[:, 0:TB], ALU.mult, ALU.add, r=[hk(b, n), kp], w=[kr])
            PS.put((ps, kp))
            rr.append((r_, kr))
            conv_pump()
        for x in hid:
            B.put(x)
        def ln_fn():
            if not last:
                layernorm(rr, 'ln2g', 'ln2b', lambda n: (hbf[:, n, c0:c0 + TB], [hk(b, n)], None))
            else:
                def dst(n):
                    o, ko = A.get()

                    def post():
                        dma(outd[:, n, c0:c0 + TB], o[:, :], r=[ko], w=[('out', b, n)])
                        A.put((o, ko))
                    return o[:, :], [ko], post
                layernorm(rr, 'ln2g', 'ln2b', dst)
            for x in rr:
                A.put(x)
        ffn_pend.append(ln_fn)

    ffn_pend = []
    mid_hook = [None]

    def whole():
        for l in range(DEPTH):
            flush_convert()
            layer_setup(l)
            if STAGE <= 0:
                raise _Stop()
            nconv = NPIECE * 4
            steps = 2 * NBLK
            done = 0
            dn = [done]

            def conv_hook(l=l, dn=dn):
                b = conv_hook.b
                if l == 0:
                    lo = 40 + (64 * b) // NBLK
                    hi = 40 + (64 * (b + 1)) // NBLK
                    for q in range(lo, hi):
                        conv_queue.append((0, q))
                if l + 1 < DEPTH:
                    tgt = (nconv * (b + 1)) // steps
                    for q in range(dn[0], tgt):
                        conv_queue.append((l + 1, q))
                    dn[0] = tgt
            for b in range(NBLK):
                conv_hook.b = b
                mid_hook[0] = conv_hook
                mixer_block(l, b)
                mid_hook[0] = None
                if l == 0 and b == NBLK - 1:
                    flush_convert()
            done = dn[0]
            for b in range(NBLK):
                if l + 1 < DEPTH:
                    tgt = (nconv * (NBLK + b + 1)) // steps
                    for q in range(done, tgt):
                        conv_queue.append((l + 1, q))
                    done = tgt
                g_ffn = ffn_block(l, b, l == DEPTH - 1)
                for _ in range(FFN_DEFER):
                    next(g_ffn)
                if len(ffn_pend) > 0:
                    ffn_pend.pop(0)()
                for _ in g_ffn:
                    pass
                conv_drain()
            while ffn_pend:
                ffn_pend.pop(0)()
    try:
        whole()
    except _Stop:
        A.free = [(t, ('A', i)) for i, t in enumerate(Apool)]
        for b in range(NBLK):
            for k in range(8):
                a, ka = A.get()
                cp('dve', a[:, :], hbf[:, k, b * TB:(b + 1) * TB], r=[hk(b, k)], w=[ka])
                dma(outd[:, k, b * TB:(b + 1) * TB], a[:, :], r=[ka], w=[('out', b, k)])
                A.put((a, ka))

    ops = P.ops
    for o in ops:
        for j in o['deps']:
            ops[j]['sig'] = True
    engs = ['pe', 'act', 'dve', 'pool', 'sp']
    cnt = {e: 0 for e in engs}
    ndma = 0
    for o in ops:
        if o['dma']:
            o['dsem'] = ndma % ND
            o['dval'] = 16 * (ndma // ND + 1)
            ndma += 1
        elif o['sig']:
            cnt[o['eng']] += 1
            o['cnt'] = cnt[o['eng']]
    print("ops", len(ops), "sig counts", cnt, "dmas", ndma, "minfree A/B/PS", A.minfree, B.minfree, PS.minfree)
    esem = {e: es.enter_context(nc.semaphore("sem_" + e)) for e in engs if e != 'sp'}
    dsem = [es.enter_context(nc.semaphore(f"dsem{i}")) for i in range(ND)]
    out_dmas = [o for o in ops if o['dma']]

    def emit(ename, eng):
        waited = {}
        for o in ops:
            if o['eng'] != ename:
                continue
            need = {}
            for j in o['deps']:
                oj = ops[j]
                if oj['dma']:
                    key = ('d', oj['dsem'])
                    val = oj['dval']
                else:
                    key = ('e', oj['eng'])
                    val = oj['cnt']
                if val > need.get(key, 0):
                    need[key] = val
            if o['dma'] and o['dval'] > 16:
                key = ('d', o['dsem'])
                need[key] = max(need.get(key, 0), o['dval'] - 16)
            for key, val in need.items():
                if val > waited.get(key, 0):
                    sem = dsem[key[1]] if key[0] == 'd' else esem[key[1]]
                    eng.wait_ge(sem, val)
                    waited[key] = val
            ins = o['fn'](eng)
            if o['dma']:
                ins.then_inc(dsem[o['dsem']], 16)
            elif o['sig']:
                ins.then_inc(esem[ename], 1)
        if ename == 'sp':
            last = {}
            for o in out_dmas:
                last[o['dsem']] = max(last.get(o['dsem'], 0), o['dval'])
            for s, v in last.items():
                eng.wait_ge(dsem[s], v)

    with nc.Block() as block:
        @block.tensor
        def _(e):
            emit('pe', e)

        @block.scalar
        def _(e):
            emit('act', e)

        @block.vector
        def _(e):
            emit('dve', e)

        @block.gpsimd
        def _(e):
            emit('pool', e)

        @block.sync
        def _(e):
            emit('sp', e)
    es.close()
    return nc


SPL = np.cumsum([0, 384, 384, 384, 384, 384, 896, 6, 256])


def _fmchunk(W, c0):
    return W[:, c0:c0 + 128].reshape(8, 128, 128).transpose(1, 0, 2).reshape(128, 1024)


def _tmgroup(W, cols):
    n = len(cols)
    return W[:, cols].reshape(8, 128, n).transpose(1, 0, 2).reshape(128, 8 * n)


def _pad(a):
    o = np.zeros((128, 4096), np.float32)
    o[:, :a.shape[1]] = a
    return o


def prep_shared(inp, DEPTH):
    f = np.float32
    ws, pvs, pms = [], [], []
    q0, f0, i0, g0, z0, x0, d0, u0 = [int(v) for v in SPL[:8]]
    for l in range(DEPTH):
        W = np.asarray(inp['w_in'][l], f)
        pcs = []
        pcs.append(np.concatenate([_fmchunk(W, q0), _fmchunk(W, q0 + 128), _fmchunk(W, q0 + 256), _fmchunk(W, f0)], 1))
        pcs.append(np.concatenate([_fmchunk(W, f0 + 128), _fmchunk(W, f0 + 256)], 1))
        pcs.append(_tmgroup(W, np.arange(i0, i0 + 384)))
        pcs.append(_tmgroup(W, np.arange(g0, g0 + 384)))
        pcs.append(np.concatenate([_fmchunk(W, x0 + 128 * c) for c in range(4)], 1))
        pcs.append(np.concatenate([_fmchunk(W, x0 + 128 * c) for c in range(4, 7)], 1))
        pcs.append(_tmgroup(W, np.concatenate([np.arange(z0, z0 + 384), np.arange(d0, d0 + 6)])))
        pcs.append(np.concatenate([_fmchunk(W, u0), _fmchunk(W, u0 + 128)], 1))
        Wo = np.asarray(inp['w_out'][l], f)
        pcs.append(np.concatenate([_fmchunk(Wo, 128 * n) for n in range(4)], 1))
        pcs.append(np.concatenate([_fmchunk(Wo, 128 * n) for n in range(4, 8)], 1))
        W1 = np.asarray(inp['w_mlp_in'][l], f)
        for hg in range(8):
            pcs.append(np.concatenate([_fmchunk(W1, 128 * (4 * hg + hc)) for hc in range(4)], 1))
        W2 = np.asarray(inp['w_mlp_out'][l], f)
        for n in range(8):
            pcs.append(W2[:, n * 128:(n + 1) * 128].reshape(32, 128, 128).transpose(1, 0, 2).reshape(128, 4096))
        assert len(pcs) == NPIECE
        ws.append(np.stack([_pad(p) for p in pcs]))
        pvl = np.zeros((128, NPV), f)
        cw = np.asarray(inp['m2_conv_w'][l], f)
        pvl[:, 0:28] = cw.reshape(4, 7, 128).transpose(2, 1, 0).reshape(128, 28)
        pvl[:, 28:35] = np.asarray(inp['m2_conv_b'][l], f).reshape(7, 128).T
        for nm, o in (('ln1_g', 35), ('ln1_b', 43), ('ln2_g', 51), ('ln2_b', 59)):
            pvl[:, o:o + 8] = np.asarray(inp[nm][l], f).reshape(8, 128).T
        pvl[:, 67:69] = np.asarray(inp['s5_glu_b'][l], f).reshape(2, 128).T
        pvl[:, 69:71] = np.asarray(inp['s5_d'][l], f).reshape(2, 128).T
        pvl[:, 71:79] = np.asarray(inp['s5_a_re'][l], f).reshape(1024).reshape(8, 128).T
        pvl[:, 79:87] = np.asarray(inp['s5_a_im'][l], f).reshape(1024).reshape(8, 128).T
        pvl[:, 87:95] = np.repeat(np.asarray(inp['s5_log_dt'][l], f), 64).reshape(8, 128).T
        pvl[:, 95:101] = np.asarray(inp['m2_dt_bias'][l], f)[None, :]
        pvl[:, 101:107] = np.asarray(inp['m2_a_log'][l], f)[None, :]
        pvl[:, 107:113] = np.asarray(inp['m2_d'][l], f)[None, :]
        pvl[:, 113:116] = np.tile(np.asarray(inp['hgrn_norm_w'][l], f), 6).reshape(3, 128).T
        pvl[:, 116:119] = np.asarray(inp['m2_norm_w'][l], f).reshape(3, 128).T
        pvl[:, 119:122] = np.repeat(np.asarray(inp['m2_d'][l], f), 64).reshape(3, 128).T
        pvs.append(pvl)
        pm = np.zeros((5, 128, 1024), f)
        for ri, nm in enumerate(('s5_b_re', 's5_b_im')):
            bb = np.asarray(inp[nm][l], f)
            for g in range(16):
                pm[ri, (g % 8) * 16:(g % 8) * 16 + 16, g * 64:(g + 1) * 64] = bb[g].T
        for ri, nm in enumerate(('s5_c_re', 's5_c_im')):
            cc = np.asarray(inp[nm][l], f)
            ct = pm[2 + ri].reshape(128, 8, 128)
            for g in range(16):
                j, ph = g // 2, (g % 2) * 64
                ct[ph:ph + 64, j, (g % 8) * 16:(g % 8) * 16 + 16] = cc[g].T
        gw = np.asarray(inp['s5_glu_w'][l], f)
        pm[4, :, 0:512] = gw.reshape(2, 128, 256).transpose(1, 0, 2).reshape(128, 512)
        pms.append(pm)
    wsrc = np.concatenate(ws, 0).reshape(DEPTH * NPIECE, 128, 4, 1024).transpose(0, 2, 1, 3)
    wsrc = np.ascontiguousarray(wsrc).reshape(DEPTH * NPIECE * 4, 128, 1024)
    lbl = np.asarray(inp['hgrn_lb_logits'], f).reshape(4, 3, 128).transpose(2, 1, 0).reshape(128, 12)
    s = np.arange(128)
    tri = (s[:, None] <= s[None, :]).astype(f)
    tau = np.arange(384)
    rst = np.tile(((tau % 64) != 0).astype(f)[None, :], (128, 1))
    ramp = np.tile((tau + 1).astype(f)[None, :], (128, 1))
    cfv = np.concatenate([tri, rst, ramp, np.eye(128, dtype=f)], 1)
    negm = np.where(s[:, None] <= s[None, :], 0.0, -30000.0).astype(f)
    mask2 = ((s[:, None] <= s[None, :]) & ((s[:, None] // 64) == (s[None, :] // 64))).astype(f)
    cbv = np.concatenate([np.eye(128, dtype=f), negm, mask2], 1)
    return dict(wsrc=wsrc, pv=np.stack(pvs), pm=np.concatenate(pms, 0), lbl=np.ascontiguousarray(lbl),
                cf=np.ascontiguousarray(cfv), cb=np.ascontiguousarray(cbv))


def run(inp, NBLK, DEPTH, ncores):
    x = np.asarray(inp['x'], np.float32)
    meta = np.asarray(inp['meta_tokens'], np.float32)
    LT = NBLK * TB
    shared = prep_shared(inp, DEPTH)
    in_maps = []
    for b in range(ncores):
        hf = np.zeros((LT, 1024), np.float32)
        hf[0:16] = meta
        hf[16:16 + x.shape[1]] = x[b]
        h0 = np.ascontiguousarray(hf.T.reshape(8, 128, LT).transpose(1, 0, 2))
        m = dict(shared)
        m['h0'] = h0
        in_maps.append(m)
    nc = build(NBLK, DEPTH)
    res = run_bass_kernel_spmd(nc, in_maps, core_ids=list(range(ncores)))
    outs = []
    for b in range(ncores):
        o = np.asarray(res.results[b]['out'], np.float32)
        hf = o.transpose(1, 0, 2).reshape(1024, LT).T
        outs.append(hf[16:16 + x.shape[1]])
    return np.stack(outs).astype(np.float32)


def kernel(**inputs):
    return run(inputs, 11, 4, 4)
```
